# Optimizing a Trainium2 kernel written in Bass

```python
import jax, jax.numpy as jnp
from jax import lax
import numpy as np

D_MODEL = 2048
BATCH = 1
SEQ = 16384
DEPTH = 1

MIX_WIDTH = D_MODEL
CONV_WIDTH = D_MODEL // 2
ATTN_WIDTH = MIX_WIDTH - CONV_WIDTH
HEAD_DIM = 64
N_HEADS = ATTN_WIDTH // HEAD_DIM
N_KV_HEADS = 2
GQA_GROUP = N_HEADS // N_KV_HEADS
KV_WIDTH = N_KV_HEADS * HEAD_DIM
CONV_KERNEL = 31
WINDOW = 128
BLOCK = 128
ROPE_THETA = 500000.0
ROTARY_DIM = HEAD_DIM // 4
RMS_EPS = 1e-6
LN_EPS = 1e-5
NEG_INF = -1e30

SPLIT_SIZES = (ATTN_WIDTH, KV_WIDTH, KV_WIDTH, ATTN_WIDTH,
               CONV_WIDTH, CONV_WIDTH, CONV_WIDTH)
IN_COLS = sum(SPLIT_SIZES)
SPLIT_POINTS = tuple(int(v) for v in np.cumsum(SPLIT_SIZES)[:-1])

kernel_name = "hymba_conformer_swa_sink_hybrid"


def rms_norm(x, g):
    xf = x.astype(jnp.float32)
    y = xf * lax.rsqrt(jnp.mean(xf * xf, axis=-1, keepdims=True) + RMS_EPS)
    return (y * g.astype(jnp.float32)).astype(x.dtype)


def layer_norm(x, g, b):
    xf = x.astype(jnp.float32)
    mu = jnp.mean(xf, axis=-1, keepdims=True)
    var = jnp.mean(jnp.square(xf - mu), axis=-1, keepdims=True)
    y = (xf - mu) * lax.rsqrt(var + LN_EPS)
    return (y * g.astype(jnp.float32) + b.astype(jnp.float32)).astype(x.dtype)


def rope_cos_sin(positions):
    inv_freq = ROPE_THETA ** (-jnp.arange(0, ROTARY_DIM, 2, dtype=jnp.float32) / ROTARY_DIM)
    ang = positions.astype(jnp.float32)[..., None] * inv_freq
    return jnp.cos(ang)[:, :, None, :], jnp.sin(ang)[:, :, None, :]


def partial_rope(x, cos, sin):
    half = ROTARY_DIM // 2
    cos = cos.astype(x.dtype)
    sin = sin.astype(x.dtype)
    x1 = x[..., :half]
    x2 = x[..., half:ROTARY_DIM]
    return jnp.concatenate([x1 * cos - x2 * sin, x2 * cos + x1 * sin, x[..., ROTARY_DIM:]], axis=-1)


def sliding_window_attention(q, k, v, sinks):
    B, S = q.shape[0], q.shape[1]
    nb = S // BLOCK
    qb = q.reshape(B, nb, BLOCK, N_KV_HEADS, GQA_GROUP, HEAD_DIM)
    kb = k.reshape(B, nb, BLOCK, N_KV_HEADS, HEAD_DIM)
    vb = v.reshape(B, nb, BLOCK, N_KV_HEADS, HEAD_DIM)
    pad = ((0, 0), (1, 0), (0, 0), (0, 0), (0, 0))
    kw = jnp.concatenate([jnp.pad(kb, pad)[:, :-1], kb], axis=2)
    vw = jnp.concatenate([jnp.pad(vb, pad)[:, :-1], vb], axis=2)
    scale = HEAD_DIM ** -0.5
    scores = jnp.einsum('bnqkgd,bnskd->bnkgqs', qb, kw).astype(jnp.float32) * scale
    q_pos = jnp.arange(BLOCK)[:, None] + BLOCK
    k_pos = jnp.arange(2 * BLOCK)[None, :]
    rel = q_pos - k_pos
    band = (rel >= 0) & (rel < WINDOW)
    has_prev = (jnp.arange(nb)[:, None, None] > 0) | (k_pos[None] >= BLOCK)
    mask = band[None] & has_prev
    scores = jnp.where(mask[None, :, None, None], scores, NEG_INF)
    sink = sinks.astype(jnp.float32).reshape(1, 1, N_KV_HEADS, GQA_GROUP, 1, 1)
    m = jnp.maximum(jnp.max(scores, axis=-1, keepdims=True), sink)
    e = jnp.exp(scores - m)
    denom = jnp.sum(e, axis=-1, keepdims=True) + jnp.exp(sink - m)
    probs = (e / denom).astype(v.dtype)
    out = jnp.einsum('bnkgqs,bnskd->bnqkgd', probs, vw)
    return out.reshape(B, S, N_HEADS * HEAD_DIM)


def conformer_conv(val, gate, w_dw, b_dw, ln_g, ln_b, w_pw, b_pw):
    h = val * jax.nn.sigmoid(gate)
    h = lax.conv_general_dilated(
        h, w_dw[:, None, :].astype(h.dtype), window_strides=(1,),
        padding=[(CONV_KERNEL - 1, 0)], dimension_numbers=('NWC', 'WIO', 'NWC'),
        feature_group_count=CONV_WIDTH) + b_dw
    h = jax.nn.silu(layer_norm(h, ln_g, ln_b))
    return h @ w_pw + b_pw


def setup_inputs(seed: int = 0) -> dict:
    key = jax.random.key(seed)
    ks = jax.random.split(key, 16)
    f32 = jnp.float32
    x = jax.random.normal(ks[0], (BATCH, SEQ, D_MODEL), f32)
    positions = jnp.broadcast_to(jnp.arange(SEQ, dtype=jnp.int32), (BATCH, SEQ))
    pre_norm_g = 1.0 + 0.05 * jax.random.normal(ks[1], (DEPTH, D_MODEL), f32)
    w_in = jax.random.normal(ks[2], (DEPTH, D_MODEL, IN_COLS), f32) * D_MODEL ** -0.5
    b_in = 0.02 * jax.random.normal(ks[3], (DEPTH, IN_COLS), f32)
    sinks = jax.random.normal(ks[4], (DEPTH, N_HEADS), f32)
    w_dw = jax.random.normal(ks[5], (DEPTH, CONV_KERNEL, CONV_WIDTH), f32) * CONV_KERNEL ** -0.5
    b_dw = 0.02 * jax.random.normal(ks[6], (DEPTH, CONV_WIDTH), f32)
    conv_ln_g = 1.0 + 0.05 * jax.random.normal(ks[7], (DEPTH, CONV_WIDTH), f32)
    conv_ln_b = 0.02 * jax.random.normal(ks[8], (DEPTH, CONV_WIDTH), f32)
    w_pw = jax.random.normal(ks[9], (DEPTH, CONV_WIDTH, CONV_WIDTH), f32) * CONV_WIDTH ** -0.5
    b_pw = 0.02 * jax.random.normal(ks[10], (DEPTH, CONV_WIDTH), f32)
    w_out = jax.random.normal(ks[11], (DEPTH, MIX_WIDTH, D_MODEL), f32) * MIX_WIDTH ** -0.5
    b_out = 0.02 * jax.random.normal(ks[12], (DEPTH, D_MODEL), f32)
    post_norm_g = 1.0 + 0.05 * jax.random.normal(ks[13], (DEPTH, D_MODEL), f32)
    return {"x": x, "positions": positions, "pre_norm_g": pre_norm_g, "w_in": w_in, "b_in": b_in,
            "sinks": sinks, "w_dw": w_dw, "b_dw": b_dw, "conv_ln_g": conv_ln_g, "conv_ln_b": conv_ln_b,
            "w_pw": w_pw, "b_pw": b_pw, "w_out": w_out, "b_out": b_out, "post_norm_g": post_norm_g}


def reference(x, positions, pre_norm_g, w_in, b_in, sinks, w_dw, b_dw, conv_ln_g, conv_ln_b,
              w_pw, b_pw, w_out, b_out, post_norm_g):
    B, S = x.shape[0], x.shape[1]
    cos, sin = rope_cos_sin(positions)
    for l in range(DEPTH):
        h = rms_norm(x, pre_norm_g[l])
        p = h @ w_in[l] + b_in[l]
        q, k, v, g_attn, glu_val, glu_gate, g_conv = jnp.split(p, SPLIT_POINTS, axis=-1)
        q = partial_rope(q.reshape(B, S, N_HEADS, HEAD_DIM), cos, sin)
        k = partial_rope(k.reshape(B, S, N_KV_HEADS, HEAD_DIM), cos, sin)
        v = v.reshape(B, S, N_KV_HEADS, HEAD_DIM)
        attn = sliding_window_attention(q, k, v, sinks[l]) * jax.nn.silu(g_attn)
        conv = conformer_conv(glu_val, glu_gate, w_dw[l], b_dw[l], conv_ln_g[l], conv_ln_b[l],
                              w_pw[l], b_pw[l]) * jax.nn.silu(g_conv)
        y = jnp.concatenate([attn, conv], axis=-1) @ w_out[l] + b_out[l]
        x = x + rms_norm(y, post_norm_g[l])
    return x
```

```python
import contextlib
import numpy as np
import concourse.bass as bass
import concourse.mybir as mybir
from concourse.bass_utils import run_bass_kernel_spmd

F32 = mybir.dt.float32
BF16 = mybir.dt.bfloat16
I32 = mybir.dt.int32
U8 = mybir.dt.uint8
ALU = mybir.AluOpType
AF = mybir.ActivationFunctionType

SELF_SYNC = {'pe': False, 'act': True, 'dve': True, 'pool': True, 'sp': False}


class T:
    __slots__ = ("name", "w", "r")

    def __init__(self, name):
        self.name = name
        self.w = None
        self.r = {}


class _DSem:
    def __init__(self, sem, name):
        self.sem = sem
        self.count = 0
        self.name = name


class _Op:
    __slots__ = ("eng", "fn", "seq", "waits", "marked", "dma", "rank")

    def __init__(self, eng, fn, seq):
        self.eng = eng
        self.fn = fn
        self.seq = seq
        self.waits = []
        self.marked = False
        self.dma = None
        self.rank = 0


class Ctx:
    ENGS = ('pe', 'act', 'dve', 'pool', 'sp')

    def __init__(self, nc):
        self.nc = nc
        self.ops = {e: [] for e in self.ENGS}
        self.known = {e: {} for e in self.ENGS}
        self.stack = contextlib.ExitStack()
        self.esem = {}
        self.dsems = []

    def __enter__(self):
        self.stack.__enter__()
        for e in self.ENGS:
            self.esem[e] = self.stack.enter_context(self.nc.semaphore("es_" + e))
        return self

    def __exit__(self, *a):
        return self.stack.__exit__(*a)

    def sb(self, name, shape, dtype):
        h = self.stack.enter_context(self.nc.sbuf_tensor(name, list(shape), dtype))
        return h[:, :] if len(shape) == 2 else h[:]

    def ps(self, name, shape, dtype):
        h = self.stack.enter_context(self.nc.psum_tensor(name, list(shape), dtype))
        return h[:, :] if len(shape) == 2 else h[:]

    def dsem(self, name):
        s = self.stack.enter_context(self.nc.semaphore("ds_" + name))
        d = _DSem(s, name)
        self.dsems.append(d)
        return d

    def constcol(self, v):
        return float(v)

    @staticmethod
    def _flat(L):
        out = []
        for t in L:
            if isinstance(t, T):
                out.append(t)
            else:
                out.extend(t.ts)
        return out

    def _emit(self, E, fn, R, W, dma=None, nowait=False, group=False):
        R = self._flat(R)
        W = self._flat(W)
        deps = []
        if not nowait:
            for t in R:
                if t.w is not None:
                    deps.append(t.w)
            for t in W:
                if t.w is not None:
                    deps.append(t.w)
                deps.extend(t.r.values())
        op = _Op(E, fn, len(self.ops[E]))
        kn = self.known[E]
        for d in deps:
            if d[0] == 'eng':
                _, Fe, seq = d
                if Fe == E and not SELF_SYNC[E]:
                    continue
                key = ('e', Fe)
                if kn.get(key, -1) >= seq:
                    continue
                kn[key] = seq
                op.waits.append(d)
                self.ops[Fe][seq].marked = True
            else:
                _, ds, val = d
                if group and ds is dma:
                    continue
                key = ('d', id(ds))
                if kn.get(key, -1) >= val:
                    continue
                kn[key] = val
                op.waits.append(d)
        self.ops[E].append(op)
        if dma is None:
            tok = ('eng', E, op.seq)
            rkey = E
        else:
            dma.count += 16
            tok = ('dma', dma, dma.count)
            op.dma = dma
            rkey = ('dma', id(dma))
        for t in R:
            t.r[rkey] = tok
        for t in W:
            t.w = tok
            t.r = {}
        return op

    def op(self, E, fn, R=(), W=()):
        return self._emit(E, fn, R, W)

    def dma(self, E, out, in_, ds, R=(), W=(), nowait=False, group=False, **kw):
        return self._emit(E, lambda e: e.dma_start(out=out, in_=in_, **kw), R, W, dma=ds, nowait=nowait, group=group)

    def finish(self, out_sems):
        nc = self.nc
        op = _Op('sp', None, len(self.ops['sp']))
        for ds in out_sems:
            op.waits.append(('dma', ds, ds.count))
        self.ops['sp'].append(op)
        for e in self.ENGS:
            r = 0
            for o in self.ops[e]:
                if o.marked:
                    r += 1
                    o.rank = r
        ops = self.ops
        esem = self.esem

        def replay(eng_name):
            def run(e):
                for o in ops[eng_name]:
                    for d in o.waits:
                        if d[0] == 'eng':
                            e.wait_ge(esem[d[1]], ops[d[1]][d[2]].rank)
                        else:
                            e.wait_ge(d[1].sem, d[2])
                    if o.fn is None:
                        continue
                    ins = o.fn(e)
                    if o.dma is not None:
                        ins.then_inc(o.dma.sem, 16)
                    elif o.marked:
                        ins.then_inc(esem[eng_name], 1)
            return run

        with nc.Block() as block:
            block.tensor(replay('pe'))
            block.scalar(replay('act'))
            block.vector(replay('dve'))
            block.gpsimd(replay('pool'))
            block.sync(replay('sp'))


D = 2048
SEQ = 16384
NCORES = 8
TPC = SEQ // NCORES
HALO = 128
SUP = 1024
NSUP = TPC // SUP
NTS = SUP // 128 + 1
TOKS = SUP + HALO
INC = 5376
RMS_EPS = 1e-6
LN_EPS = 1e-5
KTAP = 31
PI = float(np.pi)
BLK = 512

C_BFM = 0
C_BDW = 24
C_LNG = 32
C_LNB = 40
C_BPW = 48
C_FLAG = 56
C_WDW = 57
C_SINK = C_WDW + 8 * 32
C_INVF = C_SINK + 16
C_ID = C_INVF + 8
C_MCUR = C_ID + 128
C_MPREV = C_MCUR + 128
C_MPREVF = C_MPREV + 128
C_EYE4 = C_MPREVF + 128
NCP = C_EYE4 + 32


class Buf:
    __slots__ = ("ap", "ts", "off", "nbytes")

    def __init__(self, ap, ts, off, nbytes):
        self.ap = ap
        self.ts = ts
        self.off = off
        self.nbytes = nbytes


def _esz(dt):
    return 2 if dt == BF16 else (1 if dt == U8 else 4)


STOP = None


class _Stop(Exception):
    pass


_DUMPS = {}


def _chk(name):
    if STOP == name:
        raise _Stop()


def build_program():
    nc = bass.Bass("TRN2", target_bir_lowering=False)
    x_ext = nc.dram_tensor("x_ext", [TPC + HALO, D], F32, kind="ExternalInput").ap()
    pos_t = nc.dram_tensor("pos_t", [128, 17], I32, kind="ExternalInput").ap()
    w_in = nc.dram_tensor("w_in", [D, INC], F32, kind="ExternalInput").ap()
    w_out = nc.dram_tensor("w_out", [D, D], F32, kind="ExternalInput").ap()
    w_pw = nc.dram_tensor("w_pw", [1024, 1024], F32, kind="ExternalInput").ap()
    cpack_d = nc.dram_tensor("cpack", [128, NCP], F32, kind="ExternalInput").ap()
    gpre_d = nc.dram_tensor("gpre_bc", [128, D], F32, kind="ExternalInput").ap()
    gpost_d = nc.dram_tensor("gpost_bc", [128, D], F32, kind="ExternalInput").ap()
    bout_d = nc.dram_tensor("bout_bc", [128, D], F32, kind="ExternalInput").ap()
    bintok_d = nc.dram_tensor("bintok_bc", [128, 2304], F32, kind="ExternalInput").ap()
    out_d = nc.dram_tensor("out", [TPC, D], F32, kind="ExternalOutput").ap()

    w_in_v = w_in.rearrange("(k p) n -> p k n", p=128)
    w_out_v = w_out.rearrange("(k p) n -> p k n", p=128)
    w_pw_v = w_pw.rearrange("(k p) n -> p k n", p=128)

    cx = Ctx(nc)
    with cx:
        PERS = 12288
        O_XT = PERS
        O_YT = O_XT + 36864
        O_ZT = O_YT + 32768
        O_MIX = O_ZT + 16384
        O_WB = O_MIX + 32768
        O_DG = O_WB + 32768
        O_RS = O_DG + 4096
        O_HG = O_RS + 9216
        O_WORK = O_HG + 5120
        ARENA = O_WORK + 20480
        assert ARENA % BLK == 0 and O_WORK % BLK == 0
        arena = cx.sb("arena", [128, ARENA], U8)
        BT = [T(f"blk{i}") for i in range(ARENA // BLK)]

        def mk(off, shape, dt):
            n = int(np.prod(shape))
            nb = n * _esz(dt)
            assert off % 4 == 0 and off + nb <= ARENA, (off, nb)
            ap = arena[:, off:off + nb].bitcast(dt)
            if len(shape) == 2:
                ap = ap.rearrange("p (a b) -> p a b", b=shape[1])
            elif len(shape) == 3:
                ap = ap.rearrange("p (a b c) -> p a b c", b=shape[1], c=shape[2])
            ts = BT[off // BLK:(off + nb - 1) // BLK + 1]
            return Buf(ap, ts, off, nb)

        class Bump:
            def __init__(self, start, end):
                self.p = start
                self.end = end

            def __call__(self, shape, dt, align=BLK):
                self.p = (self.p + align - 1) // align * align
                b = mk(self.p, shape, dt)
                self.p += b.nbytes
                assert self.p <= self.end, (self.p, self.end)
                return b

        bp = Bump(0, PERS)
        cpack = bp([NCP], F32)
        cst_bf = bp([5, 128], BF16)
        ident_bf = cst_bf.ap[:, 0, :]
        mcur_bf = cst_bf.ap[:, 1, :]
        mprev_bf = cst_bf.ap[:, 2, :]
        mprevf_bf = cst_bf.ap[:, 3, :]
        ones_bf = cst_bf.ap[:, 4, :]
        expsink = bp([16], F32, align=64)
        pos_i = bp([17], I32, align=64)
        pos_f = bp([17], F32, align=64)
        cs = bp([NTS, 16], F32, align=BLK)
        ang = bp([NTS, 8], F32, align=64)
        rr_u = bp([NTS, 8], F32, align=64)
        rr_k = bp([NTS, 8], I32, align=64)
        rr_kf = bp([NTS, 8], F32, align=64)
        smb = [bp([16], F32, align=BLK) for _ in range(2)]
        mbias = bp([3, 4, 128], BF16)
        cp = cpack.ap

        bank = [cx.ps(f"bank{i}", [128, 512], F32) for i in range(8)]
        bank_bf = [b.bitcast(BF16) for b in bank]
        BK = [T(f"bk{i}") for i in range(8)]
        rot = {"list": list(range(8)), "i": 0}

        def nbank():
            L = rot["list"]
            b = L[rot["i"] % len(L)]
            rot["i"] += 1
            return b

        s_c = cx.dsem("const")
        s_c2 = cx.dsem("const2")
        s_gpre = cx.dsem("gpre")
        s_bintok = cx.dsem("bintok")
        s_gpost = cx.dsem("gpost")
        s_bout = cx.dsem("bout")
        s_wo1 = cx.dsem("wout1")
        NXB = 5
        s_x = [cx.dsem("x%d" % i) for i in range(NXB)]
        s_wb = [cx.dsem("wb0"), cx.dsem("wb1")]
        s_m = cx.dsem("misc")
        s_o = [cx.dsem("o0"), cx.dsem("o1")]
        s_x4 = [cx.dsem("x40"), cx.dsem("x41")]
        s_wo = cx.dsem("wout")
        s_rs = [cx.dsem("rs0"), cx.dsem("rs1")]

        cx.dma('sp', cp, cpack_d, s_c, W=[cpack], nowait=True)
        cx.dma('sp', pos_i.ap, pos_t, s_c2, W=[pos_i], nowait=True)
        cx.op('dve', lambda e: e.tensor_copy(out=cst_bf.ap[:, 0:4, :],
                                             in_=cp[:, C_ID:C_ID + 512].rearrange("p (a b) -> p a b", b=128)),
              R=[cpack], W=[cst_bf])
        cx.op('dve', lambda e: e.memset(ones_bf, 1.0 / 1024), W=[cst_bf])
        cx.op('dve', lambda e: e.tensor_scalar(
            out=mbias.ap, in0=cp[:, C_MCUR:C_MCUR + 384].rearrange("p (a b) -> p a b", b=128).unsqueeze(2)
            .broadcast_to([128, 3, 4, 128]), scalar1=-1.0, scalar2=30000.0, op0=ALU.add, op1=ALU.mult),
            R=[cpack], W=[mbias])
        cx.op('act', lambda e: e.activation(out=expsink.ap, in_=cp[:, C_SINK:C_SINK + 16], func=AF.Exp),
              R=[cpack], W=[expsink])
        cx.op('dve', lambda e: e.tensor_copy(out=pos_f.ap, in_=pos_i.ap), R=[pos_i], W=[pos_f])

        WBUF = [mk(O_WB + i * 16384, [16, 512], BF16) for i in range(2)]
        wb_state = {"n": 0}
        pre_issued = {"p1": False, "t2": False}

        try:
          for s in range(NSUP):
              row0 = s * SUP
              xT = mk(O_XT, [16, TOKS], BF16)
              b1 = Bump(O_YT, O_MIX)
              b1p = Bump(O_DG + 16384, ARENA)
              gpre = b1p([D], F32)
              xt2_ = mk(O_YT, [D], F32)
              xn = [mk(O_YT + 16384, [D], BF16), mk(O_YT + 20480, [D], BF16)]
              junk = mk(O_YT + 32768, [D], BF16)
              xt = [b1p([D], F32), mk(O_YT + 36864, [D], F32), xt2_, mk(O_YT + 8192, [D], F32), mk(O_YT + 24576, [D], F32)]
              if not pre_issued["p1"]:
                  cx.dma('sp', gpre.ap, gpre_d, s_gpre, W=[gpre])
              def p1_a(j):
                  b = j % 2
                  xb = xt[j % NXB]
                  if not (j <= 1 and pre_issued["p1"]) and not (j == 2 and pre_issued["t2"]):
                      cx.dma('sp', xb.ap, x_ext[row0 + j * 128: row0 + (j + 1) * 128, :], s_x[j % NXB], W=[xb])
                  sm = smb[b]
                  ssb = sm.ap[:, 0:1]
                  rsb = sm.ap[:, 1:2]
                  cx.op('act', lambda e: e.activation(out=junk.ap, in_=xb.ap, func=AF.Square, accum_out=ssb),
                        R=[xb], W=[junk, sm])
                  cx.op('act', lambda e: e.activation(out=rsb, in_=ssb, func=AF.Sqrt, scale=1.0 / D, bias=RMS_EPS),
                        R=[sm], W=[sm])
                  cx.op('dve', lambda e: e.reciprocal(out=rsb, in_=rsb), R=[sm], W=[sm])
                  cx.op('dve', lambda e: e.scalar_tensor_tensor(
                      out=xn[b].ap, in0=xb.ap, scalar=rsb, in1=gpre.ap, op0=ALU.mult, op1=ALU.mult),
                      R=[xb, sm, gpre], W=[xn[b]])

              def p1_b(j):
                  b = j % 2
                  bA, bB = nbank(), nbank()
                  for k in range(16):
                      bb = bA if k < 8 else bB
                      cx.op('pe', lambda e, k=k, bb=bb: e.transpose(
                          out=bank_bf[bb][:, (k % 8) * 128:(k % 8 + 1) * 128],
                          in_=xn[b].ap[:, k * 128:(k + 1) * 128], identity=ident_bf),
                          R=[xn[b], cst_bf], W=[BK[bb]])
                  cx.op('act', lambda e: e.copy(
                      out=xT.ap[:, 0:8, j * 128:(j + 1) * 128],
                      in_=bank_bf[bA].rearrange("p (k n) -> p k n", n=128)), R=[BK[bA]], W=[xT])
                  cx.op('dve', lambda e: e.tensor_copy(
                      out=xT.ap[:, 8:16, j * 128:(j + 1) * 128],
                      in_=bank_bf[bB].rearrange("p (k n) -> p k n", n=128)), R=[BK[bB]], W=[xT])

              p1_a(0)
              p1_a(1)
              for j in range(NTS):
                  p1_b(j)
                  if j + 2 < NTS:
                      p1_a(j + 2)

              c0 = s * 8
              cx.op('dve', lambda e, c0=c0: e.tensor_tensor(
                  out=ang.ap, in0=pos_f.ap[:, c0:c0 + NTS].unsqueeze(2).broadcast_to([128, NTS, 8]),
                  in1=cp[:, C_INVF:C_INVF + 8].unsqueeze(1).broadcast_to([128, NTS, 8]), op=ALU.mult),
                  R=[pos_f, cpack], W=[ang])
              C1 = 6.28125
              C2 = float(2 * np.pi - 6.28125)
              for which in range(2):
                  shift = 0.25 if which == 0 else 0.0
                  lo, hi = (-1.5 * PI, 0.5 * PI) if which == 0 else (-PI, PI)
                  cx.op('dve', lambda e, shift=shift: e.tensor_scalar(
                      out=rr_u.ap, in0=ang.ap, scalar1=float(1.0 / (2 * np.pi)), scalar2=shift,
                      op0=ALU.mult, op1=ALU.add), R=[ang], W=[rr_u])
                  cx.op('dve', lambda e: e.tensor_copy(out=rr_k.ap, in_=rr_u.ap), R=[rr_u], W=[rr_k])
                  cx.op('dve', lambda e: e.tensor_copy(out=rr_kf.ap, in_=rr_k.ap), R=[rr_k], W=[rr_kf])
                  cx.op('dve', lambda e: e.scalar_tensor_tensor(
                      out=rr_u.ap, in0=rr_kf.ap, scalar=-C1, in1=ang.ap, op0=ALU.mult, op1=ALU.add),
                      R=[rr_kf, ang], W=[rr_u])
                  cx.op('dve', lambda e: e.scalar_tensor_tensor(
                      out=rr_u.ap, in0=rr_kf.ap, scalar=-C2, in1=rr_u.ap, op0=ALU.mult, op1=ALU.add),
                      R=[rr_kf, rr_u], W=[rr_u])
                  cx.op('dve', lambda e, lo=lo, hi=hi: e.tensor_scalar(
                      out=rr_u.ap, in0=rr_u.ap, scalar1=float(lo), scalar2=float(hi), op0=ALU.max, op1=ALU.min),
                      R=[rr_u], W=[rr_u])
                  cx.op('act', lambda e, which=which: e.activation(
                      out=cs.ap[:, :, which * 8:(which + 1) * 8], in_=rr_u.ap, func=AF.Sin,
                      bias=(0.5 * PI if which == 0 else 0.0)), R=[rr_u], W=[cs])

              _DUMPS.clear()
              _DUMPS['xT'] = xT
              _DUMPS['cs'] = cs
              _chk('P1')
              def wload(col0, ncols, after=()):
                  n = wb_state["n"]
                  wb_state["n"] += 1
                  i = n % 2
                  cx.dma('pool', WBUF[i].ap[:, :, 0:ncols], w_in_v[:, :, col0:col0 + ncols], s_wb[i],
                         R=list(after), W=[WBUF[i]])
                  return WBUF[i]

              def wload_pw():
                  n = wb_state["n"]
                  wb_state["n"] += 1
                  i = n % 2
                  v = mk(O_WB + i * 16384, [8, 1024], BF16)
                  for h in range(2):
                      cx.dma('pool', v.ap[:, :, h * 512:(h + 1) * 512], w_pw_v[:, :, h * 512:(h + 1) * 512], s_wb[i],
                             W=[v])
                  return v

              yT = [mk(O_YT + c * 4096, [1024], F32) for c in range(8)]
              zT = [mk(O_ZT + c * 2048, [1024], BF16) for c in range(8)]
              mixT = mk(O_MIX, [16, SUP], BF16)
              mixc = [mk(O_MIX + k * 2048, [SUP], BF16) for k in range(16)]
              diag = [mk(O_DG + i * 2048, [8, 4, 32], BF16) for i in range(2)]
              rsh = [mk(O_RS, [4, TOKS], BF16) for i in range(2)]
              hglu = [mk(O_HG + i * 2560, [TOKS], BF16) for i in range(2)]
              bw = Bump(O_WORK, ARENA)
              sig = [bw([512], BF16, align=1024) for _ in range(2)]
              ybf = [bw([512], BF16, align=1024) for _ in range(2)]
              ysq = [bw([512], BF16, align=1024) for _ in range(2)]
              mean_sb = [bw([512], F32, align=2048) for _ in range(2)]
              rstd_sb = [bw([512], F32, align=2048) for _ in range(2)]
              tln = [bw([512], F32, align=2048) for _ in range(2)]
              valsb = tln
              _gofs = [O_DG, O_HG, O_DG + 2048, O_RS, O_RS + 2048, O_RS + 4096, O_RS + 6144, O_HG + 2560]
              gcs_all = [mk(_gofs[c], [1024], BF16) for c in range(8)]
              pending = []

              rot["list"] = [0, 1, 2, 3]
              mean_b = [4, 5]
              msq_b = [6, 7]
              GRP = [(96, 352), (448, 352), (800, 352)]
              cnt = {"sig": 0, "y": 0}

              def conv_inproj(c, wbuf):
                  cc = c % 2
                  hg = hglu[c % 2]
                  for (st, n) in GRP:
                      bv = nbank()
                      for k in range(16):
                          cx.op('pe', lambda e, k=k, bv=bv, st=st, n=n: e.matmul(
                              out=bank[bv][:, 0:n], lhsT=wbuf.ap[:, k, cc * 256:cc * 256 + 128],
                              rhs=xT.ap[:, k, st:st + n], start=(k == 0), stop=(k == 15)),
                              R=[wbuf, xT], W=[BK[bv]])
                      bg = nbank()
                      for k in range(16):
                          cx.op('pe', lambda e, k=k, bg=bg, st=st, n=n: e.matmul(
                              out=bank[bg][:, 0:n], lhsT=wbuf.ap[:, k, cc * 256 + 128:cc * 256 + 256],
                              rhs=xT.ap[:, k, st:st + n], start=(k == 0), stop=(k == 15)),
                              R=[wbuf, xT], W=[BK[bg]])
                      sg = sig[cnt["sig"] % 2]
                      cnt["sig"] += 1
                      cx.op('act', lambda e, bg=bg, n=n, sg=sg: e.activation(
                          out=sg.ap[:, 0:n], in_=bank[bg][:, 0:n], func=AF.Sigmoid,
                          bias=cp[:, C_BFM + 8 + c:C_BFM + 9 + c]), R=[BK[bg], cpack], W=[sg])
                      vs = valsb[(cnt["sig"] - 1) % 2]
                      cx.op('act', lambda e, bv=bv, n=n, vs=vs: e.activation(
                          out=vs.ap[:, 0:n], in_=bank[bv][:, 0:n], func=AF.Identity,
                          bias=cp[:, C_BFM + c:C_BFM + c + 1]), R=[BK[bv], cpack], W=[vs])
                      cx.op('dve', lambda e, n=n, st=st, sg=sg, vs=vs: e.tensor_tensor(
                          out=hg.ap[:, st:st + n], in0=vs.ap[:, 0:n], in1=sg.ap[:, 0:n], op=ALU.mult),
                          R=[vs, sg], W=[hg])
                  if s == 0:
                      cx.op('dve', lambda e: e.tensor_scalar(
                          out=hg.ap[:, 96:128], in0=hg.ap[:, 96:128], scalar1=cp[:, C_FLAG:C_FLAG + 1], scalar2=None,
                          op0=ALU.mult), R=[hg, cpack], W=[hg])
                  dg = diag[c % 2]
                  cx.op('dve', lambda e: e.tensor_tensor(
                      out=dg.ap, in0=cp[:, C_EYE4:C_EYE4 + 32].unsqueeze(1).unsqueeze(1).broadcast_to([128, 8, 4, 32]),
                      in1=cp[:, C_WDW + c * 32:C_WDW + (c + 1) * 32].rearrange("p (m g) -> p m g", g=4).unsqueeze(3)
                      .broadcast_to([128, 8, 4, 32]), op=ALU.mult), R=[cpack], W=[dg])
              def conv_shuffle(c):
                  hg = hglu[c % 2]
                  rs = rsh[c % 2]
                  for g in range(4):
                      for i in range(4):
                          cx.dma('sp', rs.ap[32 * i:32 * (i + 1), g, 96:TOKS - i], hg.ap[32 * g:32 * (g + 1), 96 + i:TOKS],
                                 s_rs[c % 2], R=[hg], W=[rs], group=True)

              def conv_taps(c):
                  hg = hglu[c % 2]
                  dg = diag[c % 2]
                  rs = rsh[c % 2]
                  for t2 in range(2):
                      by = nbank()
                      for m in range(8):
                          o = HALO + t2 * 512 - KTAP + 4 * m
                          for g in range(4):
                              cx.op('pe', lambda e, m=m, g=g, by=by, o=o: e.matmul(
                                  out=bank[by][32 * g:32 * (g + 1), :], lhsT=dg.ap[:, m, g, :],
                                  rhs=rs.ap[:, g, o:o + 512], start=(m == 0), stop=(m == 7),
                                  tile_position=(0, 32 * g)), R=[dg, rs], W=[BK[by]])
                      ysl = yT[c].ap[:, t2 * 512:(t2 + 1) * 512]
                      cx.op('act', lambda e, by=by, ysl=ysl: e.activation(
                          out=ysl, in_=bank[by], func=AF.Identity, bias=cp[:, C_BDW + c:C_BDW + c + 1]),
                          R=[BK[by], cpack], W=[yT[c]])
                      yb = ybf[cnt["y"] % 2]
                      yq = ysq[cnt["y"] % 2]
                      cnt["y"] += 1
                      cx.op('dve', lambda e, ysl=ysl, yb=yb: e.tensor_copy(out=yb.ap, in_=ysl), R=[yT[c]], W=[yb])
                      cx.op('act', lambda e, ysl=ysl, yq=yq: e.activation(out=yq.ap, in_=ysl, func=AF.Square),
                            R=[yT[c]], W=[yq])
                      def _stats(t2=t2, yb=yb, yq=yq, c=c):
                          cx.op('pe', lambda e: e.matmul(
                              out=bank[mean_b[t2]], lhsT=ones_bf, rhs=yb.ap, start=(c == 0), stop=(c == 7)),
                              R=[yb, cst_bf], W=[BK[mean_b[t2]]])
                          cx.op('pe', lambda e: e.matmul(
                              out=bank[msq_b[t2]], lhsT=ones_bf, rhs=yq.ap, start=(c == 0), stop=(c == 7)),
                              R=[yq, cst_bf], W=[BK[msq_b[t2]]])
                      pending.append(_stats)

              wgc = [None, None]

              def gconv(c):
                  wbuf = wgc[c // 4]
                  for t2 in range(2):
                      bgc = nbank()
                      for k in range(16):
                          cx.op('pe', lambda e, k=k, bgc=bgc, t2=t2: e.matmul(
                              out=bank[bgc], lhsT=wbuf.ap[:, k, (c % 4) * 128:(c % 4 + 1) * 128],
                              rhs=xT.ap[:, k, HALO + t2 * 512:HALO + (t2 + 1) * 512], start=(k == 0), stop=(k == 15)),
                              R=[wbuf, xT], W=[BK[bgc]])
                      cx.op('act', lambda e, bgc=bgc, t2=t2: e.activation(
                          out=gcs_all[c].ap[:, t2 * 512:(t2 + 1) * 512], in_=bank[bgc], func=AF.Silu,
                          bias=cp[:, C_BFM + 16 + c:C_BFM + 17 + c]), R=[BK[bgc], cpack], W=[gcs_all[c]])

              wq = [wload(2304, 512), wload(2304 + 512, 512, after=[xt[(NTS - 1) % NXB]])]
              for c in range(9):
                  if c < 8:
                      conv_inproj(c, wq[(c // 2) % 2])
                      if c % 2 == 1 and c // 2 + 2 < 4:
                          wq[(c // 2) % 2] = wload(2304 + (c // 2 + 2) * 512, 512)
                      if c == 5:
                          wgc[0] = wload(4352, 512)
                      if c == 7:
                          wgc[1] = wload(4352 + 512, 512)
                  for f_ in pending:
                      f_()
                  del pending[:]
                  if c == 8:
                      gconv(0)
                      gconv(1)
                  if c >= 1:
                      conv_taps(c - 1)
                  if c < 8:
                      conv_shuffle(c)
              for f_ in pending:
                  f_()
              del pending[:]

              for c in range(8):
                  _DUMPS['yT%d' % c] = yT[c]
              _DUMPS['hglu1'] = hglu[1]
              _chk('P2a')
              for t2 in range(2):
                  cx.op('act', lambda e, t2=t2: e.copy(out=mean_sb[t2].ap, in_=bank[mean_b[t2]]),
                        R=[BK[mean_b[t2]]], W=[mean_sb[t2]])
                  cx.op('dve', lambda e, t2=t2: e.tensor_tensor(out=tln[t2].ap, in0=mean_sb[t2].ap, in1=mean_sb[t2].ap,
                                                               op=ALU.mult), R=[mean_sb[t2]], W=[tln[t2]])
                  cx.op('dve', lambda e, t2=t2: e.tensor_tensor(out=rstd_sb[t2].ap, in0=bank[msq_b[t2]], in1=tln[t2].ap,
                                                               op=ALU.subtract), R=[BK[msq_b[t2]], tln[t2]], W=[rstd_sb[t2]])
              for t2 in range(2):
                  cx.op('act', lambda e, t2=t2: e.activation(out=rstd_sb[t2].ap, in_=rstd_sb[t2].ap, func=AF.Sqrt,
                                                            bias=LN_EPS), R=[rstd_sb[t2]], W=[rstd_sb[t2]])
              for t2 in range(2):
                  cx.op('dve', lambda e, t2=t2: e.reciprocal(out=rstd_sb[t2].ap, in_=rstd_sb[t2].ap),
                        R=[rstd_sb[t2]], W=[rstd_sb[t2]])
              rot["list"] = list(range(8))
              wpw = None
              wkv = None
              tln4 = [tln[0], tln[1], mk(ybf[0].off, [512], F32), mk(ybf[0].off + 2048, [512], F32)]

              def ln_chunk(c):
                  for t2 in range(2):
                      tb = tln4[(c % 2) * 2 + t2]
                      ysl = yT[c].ap[:, t2 * 512:(t2 + 1) * 512]
                      cx.op('dve', lambda e, ysl=ysl, t2=t2, tb=tb: e.tensor_tensor(
                          out=tb.ap, in0=ysl, in1=mean_sb[t2].ap, op=ALU.subtract),
                          R=[yT[c], mean_sb[t2]], W=[tb])
                      cx.op('dve', lambda e, t2=t2, tb=tb: e.tensor_tensor(
                          out=tb.ap, in0=tb.ap, in1=rstd_sb[t2].ap, op=ALU.mult),
                          R=[tb, rstd_sb[t2]], W=[tb])
                      cx.op('act', lambda e, t2=t2, tb=tb: e.activation(
                          out=zT[c].ap[:, t2 * 512:(t2 + 1) * 512], in_=tb.ap, func=AF.Silu,
                          scale=cp[:, C_LNG + c:C_LNG + c + 1], bias=cp[:, C_LNB + c:C_LNB + c + 1]),
                          R=[tb, cpack], W=[zT[c]])

              ln_chunk(0)
              ln_chunk(1)
              for c in range(8):
                  if c >= 2:
                      gconv(c)
                  if c == 7:
                      wkv = wload(1024, 256)
                  if c == 3:
                      wpw = wload_pw()
                  if c + 2 < 8:
                      ln_chunk(c + 2)

              _DUMPS.clear()
              for c in range(8):
                  _DUMPS['zT%d' % c] = zT[c]
              _chk('P2b')
              for c2 in range(8):
                  for t2 in range(2):
                      bpw = nbank()
                      for k in range(8):
                          cx.op('pe', lambda e, k=k, bpw=bpw, t2=t2, c2=c2, wpw=wpw: e.matmul(
                              out=bank[bpw], lhsT=wpw.ap[:, k, c2 * 128:(c2 + 1) * 128],
                              rhs=zT[k].ap[:, t2 * 512:(t2 + 1) * 512], start=(k == 0), stop=(k == 7)),
                              R=[wpw, zT[k]], W=[BK[bpw]])
                      cx.op('dve', lambda e, bpw=bpw, t2=t2, c2=c2: e.scalar_tensor_tensor(
                          out=mixT.ap[:, 8 + c2, t2 * 512:(t2 + 1) * 512], in0=bank[bpw],
                          scalar=cp[:, C_BPW + c2:C_BPW + c2 + 1], in1=gcs_all[c2].ap[:, t2 * 512:(t2 + 1) * 512],
                          op0=ALU.add, op1=ALU.mult),
                          R=[BK[bpw], gcs_all[c2], cpack], W=[mixc[8 + c2]])

              _DUMPS.clear()
              _DUMPS['mixT'] = mixT
              _chk('P2c')
              wga = [wload(1280, 512), None]
              b3 = Bump(O_ZT, O_MIX)
              kTd = b3([2, 2, TOKS], BF16)
              vaug = b3([NTS, 2, 65], BF16)
              b3b = Bump(O_YT + 16384, O_ZT)
              gateA = b3b([8, 1024], BF16)
              bw = Bump(O_DG, ARENA)
              qf = [bw([512], F32, align=2048) for _ in range(2)]
              kvf = [bw([256], F32, align=1024) for _ in range(3)]
              kb2 = [bw([2, 2, 128], BF16, align=1024) for _ in range(3)]
              qb = [bw([512], BF16, align=1024) for _ in range(2)]
              PT = [[bw([512], BF16, align=1024) for _ in range(4)] for _ in range(2)]
              otmp = [bw([256], F32, align=1024) for _ in range(2)]
              bintok = bw([2304], F32)
              krot = [bw([2, 64], BF16, align=512) for _ in range(3)]
              rt1 = [bw([8, 2, 8], F32, align=512) for _ in range(2)]
              rt2 = [bw([8, 2, 8], F32, align=512) for _ in range(2)]
              den = [bw([8], F32, align=256) for _ in range(2)]
              ag2 = [[bw([256], BF16, align=512) for _ in range(2)] for _ in range(2)]
              gaf = qf

              cx.dma('sp', bintok.ap, bintok_d, s_bintok, W=[bintok])
              cx.op('pool', lambda e: e.memset(vaug.ap[:, :, :, 64:65], 1.0), W=[vaug])
              for i_ in range(3):
                  cx.op('pool', lambda e, i_=i_: e.memset(kb2[i_].ap, 0.0), W=[kb2[i_]])

              def rope(src3, dst3, H, j, r1, r2, RS_, RD_):
                  u = src3[:, :, 0:16].rearrange("p h (t e) -> p h t e", e=8)
                  cosb = cs.ap[:, j, 0:8].unsqueeze(1).unsqueeze(1).broadcast_to([128, H, 2, 8])
                  sinb = cs.ap[:, j, 8:16].unsqueeze(1).unsqueeze(1).broadcast_to([128, H, 2, 8])
                  a1 = r1.ap[:, 0:H, :, :]
                  a2 = r2.ap[:, 0:H, :, :]
                  cx.op('dve', lambda e: e.tensor_tensor(out=a1, in0=u, in1=cosb, op=ALU.mult), R=[cs] + RS_, W=[r1])
                  cx.op('dve', lambda e: e.tensor_tensor(out=a2, in0=u, in1=sinb, op=ALU.mult), R=[cs] + RS_, W=[r2])
                  cx.op('dve', lambda e: e.tensor_tensor(out=dst3[:, :, 0:8], in0=a1[:, :, 0, :], in1=a2[:, :, 1, :],
                                                         op=ALU.subtract), R=[r1, r2], W=RD_)
                  cx.op('dve', lambda e: e.tensor_tensor(out=dst3[:, :, 8:16], in0=a1[:, :, 1, :], in1=a2[:, :, 0, :],
                                                         op=ALU.add), R=[r1, r2], W=RD_)
                  cx.op('act', lambda e: e.copy(out=dst3[:, :, 16:64], in_=src3[:, :, 16:64]), R=RS_, W=RD_)

              def kv_a(j, p2):
                  bkv = nbank()
                  for k in range(16):
                      cx.op('pe', lambda e, k=k, bkv=bkv, wkv=wkv: e.matmul(
                          out=bank[bkv][:, 0:256], lhsT=xT.ap[:, k, j * 128:(j + 1) * 128], rhs=wkv.ap[:, k, 0:256],
                          start=(k == 0), stop=(k == 15)), R=[wkv, xT], W=[BK[bkv]])
                  cx.op('dve', lambda e, bkv=bkv: e.tensor_tensor(out=kvf[p2].ap, in0=bank[bkv][:, 0:256],
                                                                  in1=bintok.ap[:, 1024:1280], op=ALU.add),
                        R=[BK[bkv], bintok], W=[kvf[p2]])
                  rope(kvf[p2].ap[:, 0:128].rearrange("p (h d) -> p h d", d=64), krot[p2].ap, 2, j, rt1[p2 % 2], rt2[p2 % 2],
                       [kvf[p2]], [krot[p2]])
                  for eh in range(2):
                      cx.op('pool', lambda e, eh=eh: e.tensor_copy(out=kb2[p2].ap[:, :, eh, eh * 64:(eh + 1) * 64],
                                                                   in_=krot[p2].ap), R=[krot[p2]], W=[kb2[p2]])
                  cx.op('pool', lambda e: e.tensor_copy(
                      out=vaug.ap[:, j, :, 0:64], in_=kvf[p2].ap[:, 128:256].rearrange("p (h d) -> p h d", d=64)),
                      R=[kvf[p2]], W=[vaug])

              def kv_b(j, p2):
                  bt = nbank()
                  for g in range(2):
                      for eh in range(2):
                          cx.op('pe', lambda e, g=g, eh=eh, bt=bt: e.transpose(
                              out=bank_bf[bt][:, (g * 2 + eh) * 128:(g * 2 + eh + 1) * 128],
                              in_=kb2[p2].ap[:, g, eh, :], identity=ident_bf),
                              R=[kb2[p2], cst_bf], W=[BK[bt]])
                  cx.op('act', lambda e, bt=bt: e.copy(
                      out=kTd.ap[:, :, :, j * 128:(j + 1) * 128],
                      in_=bank_bf[bt][:, 0:512].rearrange("p (g a n) -> p g a n", a=2, n=128)), R=[BK[bt]], W=[kTd])

              gstate = {"gi": 0}

              def ga_step(ci, j):
                  bga = nbank()
                  for k in range(16):
                      cx.op('pe', lambda e, k=k, bga=bga, wg_=wga[ci]: e.matmul(
                          out=bank[bga], lhsT=xT.ap[:, k, j * 128:(j + 1) * 128], rhs=wg_.ap[:, k, :],
                          start=(k == 0), stop=(k == 15)), R=[wga[ci], xT], W=[BK[bga]])
                  gf = gaf[gstate["gi"] % 2]
                  gstate["gi"] += 1
                  cx.op('dve', lambda e, bga=bga, gf=gf: e.tensor_tensor(
                      out=gf.ap, in0=bank[bga], in1=bintok.ap[:, 1280 + ci * 512:1280 + (ci + 1) * 512], op=ALU.add),
                      R=[BK[bga], bintok], W=[gf])
                  cx.op('act', lambda e, gf=gf: e.activation(
                      out=gateA.ap[:, j - 1, ci * 512:(ci + 1) * 512], in_=gf.ap, func=AF.Silu),
                      R=[gf], W=[gateA])

              kv_a(0, 0)
              kv_a(1, 1)
              for j in range(NTS):
                  if j + 2 < NTS:
                      kv_a(j + 2, (j + 2) % 3)
                  kv_b(j, j % 3)
              for j in range(1, NTS):
                  ga_step(0, j)

              _DUMPS.clear()
              _DUMPS['kTd'] = kTd
              _DUMPS['vaug'] = vaug
              _chk('P3kv')
              wga[1] = wload(1280 + 512, 512)
              wqq = [wload(0, 512), None]
              for j in range(1, NTS):
                  ga_step(1, j)

              _DUMPS.clear()
              _DUMPS['gateA'] = gateA
              _chk('P3ga')
              wqq[1] = wload(512, 512)
              qT_all = [[mk(O_YT + (qi * 8 + jj) * 1024, [4, 128], BF16) for jj in range(8)] for qi in range(2)]
              def q_a(qi, j, p2):
                  bq = nbank()
                  for k in range(16):
                      cx.op('pe', lambda e, k=k, bq=bq, wq_=wqq[qi]: e.matmul(
                          out=bank[bq], lhsT=xT.ap[:, k, j * 128:(j + 1) * 128], rhs=wq_.ap[:, k, :],
                          start=(k == 0), stop=(k == 15)), R=[wqq[qi], xT], W=[BK[bq]])
                  cx.op('dve', lambda e, bq=bq: e.tensor_tensor(
                      out=qf[p2].ap, in0=bank[bq], in1=bintok.ap[:, qi * 512:(qi + 1) * 512], op=ALU.add),
                      R=[BK[bq], bintok], W=[qf[p2]])
                  rope(qf[p2].ap.rearrange("p (h d) -> p h d", d=64),
                       qb[p2].ap.rearrange("p (h d) -> p h d", d=64), 8, j, rt1[p2], rt2[p2], [qf[p2]], [qb[p2]])

              def q_b(qi, j, p2):
                  bt = nbank()
                  for t in range(4):
                      cx.op('pe', lambda e, t=t, bt=bt: e.transpose(
                          out=bank_bf[bt][:, t * 128:(t + 1) * 128], in_=qb[p2].ap[:, t * 128:(t + 1) * 128],
                          identity=ident_bf), R=[qb[p2], cst_bf], W=[BK[bt]])
                  qTb = qT_all[qi][j - 1]
                  cx.op('act', lambda e, bt=bt, qTb=qTb: e.copy(
                      out=qTb.ap, in_=bank_bf[bt][:, 0:512].rearrange("p (t n) -> p t n", n=128)),
                      R=[BK[bt]], W=[qTb])

              qits = [(qi, j) for qi in range(2) for j in range(1, NTS)]
              q_a(qits[0][0], qits[0][1], 0)
              for n_, (qi, j) in enumerate(qits):
                  if n_ + 1 < len(qits):
                      q_a(qits[n_ + 1][0], qits[n_ + 1][1], (n_ + 1) % 2)
                  q_b(qi, j, n_ % 2)

              wo_ch = [mk(O_XT, [16, 512], BF16), mk(O_XT + 16384, [16, 512], BF16), WBUF[0], WBUF[1]]
              s_woc = [s_wo, s_wo1, s_wb[0], s_wb[1]]
              for h in range(4):
                  cx.dma('pool', wo_ch[h].ap, w_out_v[:, :, h * 512:(h + 1) * 512], s_woc[h], W=[wo_ch[h]])

              def att_s1(qi, j, p2):
                  qTb = qT_all[qi][j - 1]
                  for kb in range(2):
                      ktile = j - 1 + kb
                      if kb == 1:
                          mi = 0
                      else:
                          mi = 2 if (s == 0 and j == 1) else 1
                      for hh in range(2):
                          bs = nbank()
                          cx.op('pe', lambda e, bs=bs, mi=mi: e.matmul(
                              out=bank[bs], lhsT=ident_bf, rhs=mbias.ap[:, mi, :, :].rearrange("p h q -> p (h q)"),
                              start=True, stop=False), R=[cst_bf, mbias], W=[BK[bs]])
                          for hl in range(4):
                              head = hh * 4 + hl
                              t, eh = head // 2, head % 2
                              cx.op('pe', lambda e, bs=bs, hl=hl, t=t, eh=eh, ktile=ktile: e.matmul(
                                  out=bank[bs][:, hl * 128:(hl + 1) * 128],
                                  lhsT=kTd.ap[:, qi, eh, ktile * 128:(ktile + 1) * 128],
                                  rhs=qTb.ap[:, t, :], start=False, stop=(hl == 3)),
                                  R=[kTd, qTb], W=[BK[bs]])
                          pt = PT[p2][kb * 2 + hh]
                          cx.op('act', lambda e, bs=bs, pt=pt: e.activation(out=pt.ap, in_=bank[bs], func=AF.Exp,
                                                                            scale=0.125), R=[BK[bs]], W=[pt])

              def att_s2(qi, j, p2):
                  bos = []
                  for hh in range(2):
                      bo = nbank()
                      bos.append(bo)
                      for hl in range(4):
                          for kb in range(2):
                              pt = PT[p2][kb * 2 + hh]
                              cx.op('pe', lambda e, bo=bo, hl=hl, kb=kb, pt=pt: e.matmul(
                                  out=bank[bo][:, hl * 65:(hl + 1) * 65], lhsT=pt.ap[:, hl * 128:(hl + 1) * 128],
                                  rhs=vaug.ap[:, j - 1 + kb, qi, :], start=(kb == 0), stop=(kb == 1)),
                                  R=[pt, vaug], W=[BK[bo]])
                  for hh in range(2):
                      bo = bos[hh]
                      o3 = bank[bo][:, 0:260].rearrange("p (h d) -> p h d", d=65)
                      dn = den[hh].ap[:, 0:4]
                      h0 = qi * 8 + hh * 4
                      cx.op('dve', lambda e, o3=o3, dn=dn, h0=h0: e.tensor_tensor(
                          out=dn, in0=o3[:, :, 64], in1=expsink.ap[:, h0:h0 + 4], op=ALU.add),
                          R=[BK[bo], expsink], W=[den[hh]])
                      cx.op('dve', lambda e, dn=dn: e.reciprocal(out=dn, in_=dn), R=[den[hh]], W=[den[hh]])
                      ot3 = otmp[hh].ap.rearrange("p (h d) -> p h d", d=64)
                      cx.op('dve', lambda e, o3=o3, dn=dn, ot3=ot3: e.tensor_tensor(
                          out=ot3, in0=o3[:, :, 0:64], in1=dn.unsqueeze(2).broadcast_to([128, 4, 64]), op=ALU.mult),
                          R=[BK[bo], den[hh]], W=[otmp[hh]])
                      c0g = qi * 512 + hh * 256
                      agb = ag2[p2][hh]
                      cx.op('pool', lambda e, hh=hh, c0g=c0g, agb=agb: e.tensor_tensor(
                          out=agb.ap, in0=otmp[hh].ap, in1=gateA.ap[:, j - 1, c0g:c0g + 256], op=ALU.mult),
                          R=[otmp[hh], gateA], W=[agb])

              def att_s3(qi, j, p2):
                  for hh in range(2):
                      agb = ag2[p2][hh]
                      bt2 = nbank()
                      for t in range(2):
                          cx.op('pe', lambda e, t=t, bt2=bt2, agb=agb: e.transpose(
                              out=bank_bf[bt2][:, t * 128:(t + 1) * 128], in_=agb.ap[:, t * 128:(t + 1) * 128],
                              identity=ident_bf), R=[agb, cst_bf], W=[BK[bt2]])
                      kc0 = qi * 4 + hh * 2
                      cx.op('dve', lambda e, bt2=bt2, kc0=kc0: e.tensor_copy(
                          out=mixT.ap[:, kc0:kc0 + 2, (j - 1) * 128:j * 128],
                          in_=bank_bf[bt2][:, 0:256].rearrange("p (t n) -> p t n", n=128)),
                          R=[BK[bt2]], W=[mixc[kc0], mixc[kc0 + 1]])

              its = [(qi, j) for qi in range(2) for j in range(1, NTS)]
              NI = len(its)
              for step in range(NI + 2):
                  if step < NI:
                      att_s1(its[step][0], its[step][1], step % 2)
                  if 1 <= step <= NI:
                      att_s2(its[step - 1][0], its[step - 1][1], (step - 1) % 2)
                  if step >= 2:
                      att_s3(its[step - 2][0], its[step - 2][1], (step - 2) % 2)

              _DUMPS.clear()
              _DUMPS['mixT'] = mixT
              _chk('P3')
              b4 = Bump(O_DG, ARENA)
              gpost = b4([D], F32)
              bout = b4([D], F32)
              b4b = Bump(O_YT, O_MIX)
              xt4 = [b4b([D], F32) for _ in range(2)]
              yf = [b4b([D], F32) for _ in range(2)]
              junk4 = b4b([D], BF16)
              assert b4b.p <= O_YT + 36864
              cx.dma('sp', gpost.ap, gpost_d, s_gpost, W=[gpost])
              cx.dma('sp', bout.ap, bout_d, s_bout, W=[bout])
              for j in range(8):
                  b = j % 2
                  r_in = row0 + HALO + j * 128
                  r_out = s * SUP + j * 128
                  if j < 2:
                      cx.dma('sp', xt4[b].ap, x_ext[r_in:r_in + 128, :], s_x4[b], W=[xt4[b]])
                  if j == 4 and s + 1 < NSUP:
                      nrow0 = (s + 1) * SUP
                      cx.dma('sp', gpre.ap, gpre_d, s_gpre, W=[gpre])
                      cx.dma('sp', xt[0].ap, x_ext[nrow0: nrow0 + 128, :], s_x[0], W=[xt[0]])
                      cx.dma('sp', xt[1].ap, x_ext[nrow0 + 128: nrow0 + 256, :], s_x[1], W=[xt[1]])
                      pre_issued["p1"] = True
                  for nb in range(4):
                      by = nbank()
                      for k in range(16):
                          cx.op('pe', lambda e, k=k, by=by, j=j, woc=wo_ch[nb]: e.matmul(
                              out=bank[by], lhsT=mixT.ap[:, k, j * 128:(j + 1) * 128],
                              rhs=woc.ap[:, k, :], start=(k == 0), stop=(k == 15)),
                              R=[wo_ch[nb], mixT], W=[BK[by]])
                      cx.op('dve', lambda e, by=by, nb=nb, b=b: e.tensor_tensor(
                          out=yf[b].ap[:, nb * 512:(nb + 1) * 512], in0=bank[by],
                          in1=bout.ap[:, nb * 512:(nb + 1) * 512], op=ALU.add), R=[BK[by], bout], W=[yf[b]])
                  sm = smb[b]
                  ssb = sm.ap[:, 2:3]
                  rsb = sm.ap[:, 3:4]
                  cx.op('act', lambda e, b=b, ssb=ssb: e.activation(out=junk4.ap, in_=yf[b].ap, func=AF.Square,
                                                                    accum_out=ssb), R=[yf[b]], W=[junk4, sm])
                  cx.op('act', lambda e, ssb=ssb, rsb=rsb: e.activation(out=rsb, in_=ssb, func=AF.Sqrt,
                                                                        scale=1.0 / D, bias=RMS_EPS), R=[sm], W=[sm])
                  cx.op('dve', lambda e, rsb=rsb: e.reciprocal(out=rsb, in_=rsb), R=[sm], W=[sm])
                  cx.op('dve', lambda e, b=b, rsb=rsb: e.scalar_tensor_tensor(
                      out=yf[b].ap, in0=yf[b].ap, scalar=rsb, in1=gpost.ap, op0=ALU.mult, op1=ALU.mult),
                      R=[yf[b], sm, gpost], W=[yf[b]])
                  cx.op('dve', lambda e, b=b: e.tensor_tensor(out=xt4[b].ap, in0=yf[b].ap, in1=xt4[b].ap, op=ALU.add),
                        R=[yf[b], xt4[b]], W=[xt4[b]])
                  cx.dma('sp', out_d[r_out:r_out + 128, :], xt4[b].ap, s_o[b], R=[xt4[b]])
                  if j + 2 < 8:
                      r_n = row0 + HALO + (j + 2) * 128
                      cx.dma('sp', xt4[b].ap, x_ext[r_n:r_n + 128, :], s_x4[b], W=[xt4[b]])
                  elif j == 6 and s + 1 < NSUP:
                      nrow0 = (s + 1) * SUP
                      cx.dma('sp', xt[2].ap, x_ext[nrow0 + 256: nrow0 + 384, :], s_x[2], W=[xt[2]])
                      pre_issued["t2"] = True

        except _Stop:
            pass
        if STOP is not None:
            s_dbg = cx.dsem("dbg")
            for nm, b in _DUMPS.items():
                shp = list(b.ap.shape)
                dd = nc.dram_tensor("dbg_" + nm, shp, b.ap.dtype, kind="ExternalOutput").ap()
                cx.dma('sp', dd, b.ap, s_dbg, R=[b])
            cx.finish([s_dbg])
        else:
            cx.finish(s_o)
    return nc


_PROGRAM = None


def _host_inputs(x, positions, pre_norm_g, w_in, b_in, sinks, w_dw, b_dw, conv_ln_g, conv_ln_b,
                 w_pw, b_pw, w_out, b_out, post_norm_g):
    f32 = np.float32
    x2 = np.asarray(x, f32).reshape(SEQ, D)
    pos = np.asarray(positions).reshape(SEQ).astype(np.int32)
    w_in0 = np.asarray(w_in, f32)[0]
    b_in0 = np.asarray(b_in, f32)[0]
    cols = list(range(0, 2304))
    for c in range(8):
        cols += list(range(2304 + c * 128, 2304 + (c + 1) * 128))
        cols += list(range(3328 + c * 128, 3328 + (c + 1) * 128))
    cols += list(range(4352, 5376))
    w_in_p = np.ascontiguousarray(w_in0[:, np.array(cols)])
    w_out0 = np.ascontiguousarray(np.asarray(w_out, f32)[0])
    w_pw0 = np.ascontiguousarray(np.asarray(w_pw, f32)[0])

    def bc(v, n):
        return np.ascontiguousarray(np.broadcast_to(np.asarray(v, f32).reshape(1, n), (128, n)))

    gpre = bc(np.asarray(pre_norm_g)[0], D)
    gpost = bc(np.asarray(post_norm_g)[0], D)
    boutb = bc(np.asarray(b_out)[0], D)
    bintok = bc(b_in0[0:2304], 2304)

    def fm(v):
        v = np.asarray(v, f32)
        return np.ascontiguousarray(v.reshape(-1, 128).T)

    cpack = np.zeros((128, NCP), f32)
    cpack[:, C_BFM:C_BFM + 24] = fm(b_in0[2304:5376])
    cpack[:, C_BDW:C_BDW + 8] = fm(np.asarray(b_dw)[0])
    cpack[:, C_LNG:C_LNG + 8] = fm(np.asarray(conv_ln_g)[0])
    cpack[:, C_LNB:C_LNB + 8] = fm(np.asarray(conv_ln_b)[0])
    cpack[:, C_BPW:C_BPW + 8] = fm(np.asarray(b_pw)[0])
    wdw = np.asarray(w_dw, f32)[0]
    wpad = np.concatenate([np.zeros((1, 1024), f32), wdw], axis=0)
    wpk = wpad.reshape(8, 4, 8, 4, 32)
    cpack[:, C_WDW:C_WDW + 256] = wpk.transpose(1, 4, 2, 0, 3).reshape(128, 256)
    cpack[:, C_EYE4:C_EYE4 + 32] = np.tile(np.eye(32, dtype=f32), (4, 1))
    cpack[:, C_SINK:C_SINK + 16] = bc(np.asarray(sinks)[0], 16)
    invf = (np.float32(500000.0) ** (-(np.arange(0, 16, 2, dtype=np.float32)) / np.float32(16))).astype(f32)
    cpack[:, C_INVF:C_INVF + 8] = bc(invf, 8)
    cpack[:, C_ID:C_ID + 128] = np.eye(128, dtype=f32)
    kk = np.arange(128)[:, None]
    qq = np.arange(128)[None, :]
    cpack[:, C_MCUR:C_MCUR + 128] = (qq >= kk).astype(f32)
    cpack[:, C_MPREV:C_MPREV + 128] = (kk > qq).astype(f32)

    in_maps = []
    for c in range(NCORES):
        t0 = c * TPC
        xe = np.zeros((TPC + HALO, D), f32)
        pe = np.zeros((TPC + HALO,), np.int32)
        if c > 0:
            xe[:] = x2[t0 - HALO:t0 + TPC]
            pe[:] = pos[t0 - HALO:t0 + TPC]
        else:
            xe[HALO:] = x2[0:TPC]
            pe[HALO:] = pos[0:TPC]
        cpk = cpack.copy()
        cpk[:, C_FLAG] = 0.0 if c == 0 else 1.0
        cpk[:, C_MPREVF:C_MPREVF + 128] = 0.0 if c == 0 else cpack[:, C_MPREV:C_MPREV + 128]
        in_maps.append({
            "x_ext": xe, "pos_t": np.ascontiguousarray(pe.reshape(17, 128).T),
            "w_in": w_in_p, "w_out": w_out0, "w_pw": w_pw0, "cpack": cpk,
            "gpre_bc": gpre, "gpost_bc": gpost, "bout_bc": boutb, "bintok_bc": bintok,
        })
    return in_maps


def kernel(**inputs):
    global _PROGRAM
    in_maps = _host_inputs(**inputs)
    if _PROGRAM is None:
        _PROGRAM = build_program()
    res = run_bass_kernel_spmd(_PROGRAM, in_maps, core_ids=list(range(NCORES)))
    out = np.concatenate([np.asarray(r["out"], np.float32) for r in res.results], axis=0)
    return out.reshape(1, SEQ, D)
```

```python
import contextlib
import numpy as np
import concourse.bass as bass
import concourse.mybir as mybir
from concourse.bass_utils import run_bass_kernel_spmd

F32 = mybir.dt.float32
BF16 = mybir.dt.bfloat16
I32 = mybir.dt.int32
U8 = mybir.dt.uint8
ALU = mybir.AluOpType
AF = mybir.ActivationFunctionType

SELF_SYNC = {'pe': False, 'act': True, 'dve': True, 'pool': True, 'sp': False}


class T:
    __slots__ = ("name", "w", "r")

    def __init__(self, name):
        self.name = name
        self.w = None
        self.r = {}


class _DSem:
    def __init__(self, sem, name):
        self.sem = sem
        self.count = 0
        self.name = name


class _Op:
    __slots__ = ("eng", "fn", "seq", "waits", "marked", "dma", "rank", "kn")

    def __init__(self, eng, fn, seq):
        self.eng = eng
        self.fn = fn
        self.seq = seq
        self.waits = []
        self.marked = False
        self.dma = None
        self.rank = 0
        self.kn = None


class Ctx:
    ENGS = ('pe', 'act', 'dve', 'pool', 'sp')

    def __init__(self, nc):
        self.nc = nc
        self.ops = {e: [] for e in self.ENGS}
        self.known = {e: {} for e in self.ENGS}
        self.stack = contextlib.ExitStack()
        self.esem = {}
        self.dsems = []

    def __enter__(self):
        self.stack.__enter__()
        for e in self.ENGS:
            self.esem[e] = self.stack.enter_context(self.nc.semaphore("es_" + e))
        return self

    def __exit__(self, *a):
        return self.stack.__exit__(*a)

    def sb(self, name, shape, dtype):
        h = self.stack.enter_context(self.nc.sbuf_tensor(name, list(shape), dtype))
        return h[:, :] if len(shape) == 2 else h[:]

    def ps(self, name, shape, dtype):
        h = self.stack.enter_context(self.nc.psum_tensor(name, list(shape), dtype))
        return h[:, :] if len(shape) == 2 else h[:]

    def dsem(self, name):
        s = self.stack.enter_context(self.nc.semaphore("ds_" + name))
        d = _DSem(s, name)
        self.dsems.append(d)
        return d

    def constcol(self, v):
        return float(v)

    @staticmethod
    def _flat(L):
        out = []
        for t in L:
            if isinstance(t, T):
                out.append(t)
            else:
                out.extend(t.ts)
        return out

    def _emit(self, E, fn, R, W, dma=None, nowait=False, group=False):
        R = self._flat(R)
        W = self._flat(W)
        deps = []
        if not nowait:
            for t in R:
                if t.w is not None:
                    deps.append(t.w)
            for t in W:
                if t.w is not None:
                    deps.append(t.w)
                deps.extend(t.r.values())
        op = _Op(E, fn, len(self.ops[E]))
        kn = self.known[E]
        for d in deps:
            if d[0] == 'eng':
                _, Fe, seq = d
                if Fe == E and not SELF_SYNC[E]:
                    continue
                key = ('e', Fe)
                if kn.get(key, -1) >= seq:
                    continue
                kn[key] = seq
                op.waits.append(d)
                self.ops[Fe][seq].marked = True
                for k2, v2 in self.ops[Fe][seq].kn.items():
                    if kn.get(k2, -1) < v2:
                        kn[k2] = v2
            else:
                _, ds, val = d
                if group and ds is dma:
                    continue
                key = ('d', id(ds))
                if kn.get(key, -1) >= val:
                    continue
                kn[key] = val
                op.waits.append(d)
        op.kn = dict(kn)
        self.ops[E].append(op)
        if dma is None:
            tok = ('eng', E, op.seq)
            rkey = E
        else:
            dma.count += 16
            tok = ('dma', dma, dma.count)
            op.dma = dma
            rkey = ('dma', id(dma))
        for t in R:
            t.r[rkey] = tok
        for t in W:
            t.w = tok
            t.r = {}
        return op

    def op(self, E, fn, R=(), W=()):
        return self._emit(E, fn, R, W)

    def dma(self, E, out, in_, ds, R=(), W=(), nowait=False, group=False, **kw):
        return self._emit(E, lambda e: e.dma_start(out=out, in_=in_, **kw), R, W, dma=ds, nowait=nowait, group=group)

    def finish(self, out_sems):
        nc = self.nc
        op = _Op('sp', None, len(self.ops['sp']))
        for ds in out_sems:
            op.waits.append(('dma', ds, ds.count))
        self.ops['sp'].append(op)
        for e in self.ENGS:
            r = 0
            for o in self.ops[e]:
                if o.marked:
                    r += 1
                    o.rank = r
        ops = self.ops
        esem = self.esem

        def replay(eng_name):
            def run(e):
                for o in ops[eng_name]:
                    for d in o.waits:
                        if d[0] == 'eng':
                            e.wait_ge(esem[d[1]], ops[d[1]][d[2]].rank)
                        else:
                            e.wait_ge(d[1].sem, d[2])
                    if o.fn is None:
                        continue
                    ins = o.fn(e)
                    if o.dma is not None:
                        ins.then_inc(o.dma.sem, 16)
                    elif o.marked:
                        ins.then_inc(esem[eng_name], 1)
            return run

        with nc.Block() as block:
            block.tensor(replay('pe'))
            block.scalar(replay('act'))
            block.vector(replay('dve'))
            block.gpsimd(replay('pool'))
            block.sync(replay('sp'))


D = 2048
SEQ = 16384
NCORES = 8
TPC = SEQ // NCORES
HALO = 128
SUP = 1024
NSUP = TPC // SUP
NTS = SUP // 128 + 1
TOKS = SUP + HALO
INC = 5376
RMS_EPS = 1e-6
LN_EPS = 1e-5
KTAP = 31
PI = float(np.pi)
BLK = 512

C_BFM = 0
C_BDW = 24
C_LNG = 32
C_LNB = 40
C_BPW = 48
C_FLAG = 56
C_WDW = 57
C_SINK = C_WDW + 8 * 32
C_INVF = C_SINK + 16
C_ID = C_INVF + 8
C_MCUR = C_ID + 128
C_MPREV = C_MCUR + 128
C_MPREVF = C_MPREV + 128
C_EYE4 = C_MPREVF + 128
NCP = C_EYE4 + 32


class Buf:
    __slots__ = ("ap", "ts", "off", "nbytes")

    def __init__(self, ap, ts, off, nbytes):
        self.ap = ap
        self.ts = ts
        self.off = off
        self.nbytes = nbytes


def _esz(dt):
    return 2 if dt == BF16 else (1 if dt == U8 else 4)


STOP = None


class _Stop(Exception):
    pass


_DUMPS = {}


def _chk(name):
    if STOP == name:
        raise _Stop()


def build_program():
    nc = bass.Bass("TRN2", target_bir_lowering=False)
    x_ext = nc.dram_tensor("x_ext", [TPC + HALO, D], F32, kind="ExternalInput").ap()
    pos_t = nc.dram_tensor("pos_t", [128, 17], I32, kind="ExternalInput").ap()
    w_in = nc.dram_tensor("w_in", [D, INC], F32, kind="ExternalInput").ap()
    w_out = nc.dram_tensor("w_out", [D, D], F32, kind="ExternalInput").ap()
    w_pw = nc.dram_tensor("w_pw", [1024, 1024], F32, kind="ExternalInput").ap()
    cpack_d = nc.dram_tensor("cpack", [128, NCP], F32, kind="ExternalInput").ap()
    gpre_d = nc.dram_tensor("gpre_bc", [128, D], F32, kind="ExternalInput").ap()
    gpost_d = nc.dram_tensor("gpost_bc", [128, D], F32, kind="ExternalInput").ap()
    bout_d = nc.dram_tensor("bout_bc", [128, D], F32, kind="ExternalInput").ap()
    bintok_d = nc.dram_tensor("bintok_bc", [128, 2304], F32, kind="ExternalInput").ap()
    out_d = nc.dram_tensor("out", [TPC, D], F32, kind="ExternalOutput").ap()

    w_in_v = w_in.rearrange("(k p) n -> p k n", p=128)
    w_out_v = w_out.rearrange("(k p) n -> p k n", p=128)
    w_pw_v = w_pw.rearrange("(k p) n -> p k n", p=128)

    cx = Ctx(nc)
    with cx:
        PERS = 12288
        O_XT = PERS
        O_YT = O_XT + 36864
        O_ZT = O_YT + 32768
        O_MIX = O_ZT + 16384
        O_WB = O_MIX + 32768
        O_DG = O_WB + 32768
        O_RS = O_DG + 4096
        O_HG = O_RS + 9216
        O_WORK = O_HG + 5120
        ARENA = O_WORK + 20480
        assert ARENA % BLK == 0 and O_WORK % BLK == 0
        arena = cx.sb("arena", [128, ARENA], U8)
        BT = [T(f"blk{i}") for i in range(ARENA // BLK)]

        def mk(off, shape, dt):
            n = int(np.prod(shape))
            nb = n * _esz(dt)
            assert off % 4 == 0 and off + nb <= ARENA, (off, nb)
            ap = arena[:, off:off + nb].bitcast(dt)
            if len(shape) == 2:
                ap = ap.rearrange("p (a b) -> p a b", b=shape[1])
            elif len(shape) == 3:
                ap = ap.rearrange("p (a b c) -> p a b c", b=shape[1], c=shape[2])
            ts = BT[off // BLK:(off + nb - 1) // BLK + 1]
            return Buf(ap, ts, off, nb)

        class Bump:
            def __init__(self, start, end):
                self.p = start
                self.end = end

            def __call__(self, shape, dt, align=BLK):
                self.p = (self.p + align - 1) // align * align
                b = mk(self.p, shape, dt)
                self.p += b.nbytes
                assert self.p <= self.end, (self.p, self.end)
                return b

        bp = Bump(0, PERS)
        cpack = bp([NCP], F32)
        cst_bf = bp([5, 128], BF16)
        ident_bf = cst_bf.ap[:, 0, :]
        mcur_bf = cst_bf.ap[:, 1, :]
        mprev_bf = cst_bf.ap[:, 2, :]
        mprevf_bf = cst_bf.ap[:, 3, :]
        ones_bf = cst_bf.ap[:, 4, :]
        expsink = bp([16], F32, align=64)
        pos_i = bp([17], I32, align=64)
        pos_f = bp([17], F32, align=64)
        cs = bp([NTS, 16], F32, align=BLK)
        ang = bp([NTS, 8], F32, align=64)
        rr_u = bp([NTS, 8], F32, align=64)
        rr_k = bp([NTS, 8], I32, align=64)
        rr_kf = bp([NTS, 8], F32, align=64)
        smb = [bp([16], F32, align=BLK) for _ in range(2)]
        mbias = bp([3, 4, 128], BF16)
        cp = cpack.ap

        bank = [cx.ps(f"bank{i}", [128, 512], F32) for i in range(8)]
        bank_bf = [b.bitcast(BF16) for b in bank]
        BK = [T(f"bk{i}") for i in range(8)]
        rot = {"list": list(range(8)), "i": 0}

        def nbank():
            L = rot["list"]
            b = L[rot["i"] % len(L)]
            rot["i"] += 1
            return b

        s_c = cx.dsem("const")
        s_c2 = cx.dsem("const2")
        s_gpre = cx.dsem("gpre")
        s_bintok = cx.dsem("bintok")
        s_gpost = cx.dsem("gpost")
        s_bout = cx.dsem("bout")
        s_wo1 = cx.dsem("wout1")
        NXB = 5
        s_x = [cx.dsem("x%d" % i) for i in range(NXB)]
        s_wb = [cx.dsem("wb0"), cx.dsem("wb1")]
        s_m = cx.dsem("misc")
        s_o = [cx.dsem("o0"), cx.dsem("o1")]
        s_x4 = [cx.dsem("x40"), cx.dsem("x41")]
        s_wo = cx.dsem("wout")
        s_rs = [cx.dsem("rs0"), cx.dsem("rs1")]

        cx.dma('sp', cp, cpack_d, s_c, W=[cpack], nowait=True)
        cx.dma('sp', pos_i.ap, pos_t, s_c2, W=[pos_i], nowait=True)
        cx.op('dve', lambda e: e.tensor_copy(out=cst_bf.ap[:, 0:4, :],
                                             in_=cp[:, C_ID:C_ID + 512].rearrange("p (a b) -> p a b", b=128)),
              R=[cpack], W=[cst_bf])
        cx.op('dve', lambda e: e.memset(ones_bf, 1.0 / 1024), W=[cst_bf])
        cx.op('dve', lambda e: e.tensor_scalar(
            out=mbias.ap, in0=cp[:, C_MCUR:C_MCUR + 384].rearrange("p (a b) -> p a b", b=128).unsqueeze(2)
            .broadcast_to([128, 3, 4, 128]), scalar1=-1.0, scalar2=30000.0, op0=ALU.add, op1=ALU.mult),
            R=[cpack], W=[mbias])
        cx.op('act', lambda e: e.activation(out=expsink.ap, in_=cp[:, C_SINK:C_SINK + 16], func=AF.Exp),
              R=[cpack], W=[expsink])
        cx.op('dve', lambda e: e.tensor_copy(out=pos_f.ap, in_=pos_i.ap), R=[pos_i], W=[pos_f])

        WBUF = [mk(O_WB + i * 16384, [16, 512], BF16) for i in range(2)]
        wb_state = {"n": 0}
        pre_issued = {"p1": False}

        try:
          for s in range(NSUP):
              row0 = s * SUP
              xT = mk(O_XT, [16, TOKS], BF16)
              b1 = Bump(O_YT, O_MIX)
              b1p = Bump(O_DG + 16384, ARENA)
              gpre = b1p([D], F32)
              xt2_ = mk(O_YT, [D], F32)
              xn = [mk(O_YT + 16384, [D], BF16), mk(O_YT + 20480, [D], BF16)]
              junk = mk(O_YT + 32768, [D], BF16)
              xt = [b1p([D], F32), mk(O_YT + 36864, [D], F32), xt2_, mk(O_YT + 8192, [D], F32), mk(O_YT + 24576, [D], F32)]
              if not pre_issued["p1"]:
                  cx.dma('sp', gpre.ap, gpre_d, s_gpre, W=[gpre])
              def p1_a(j):
                  b = j % 2
                  xb = xt[j % NXB]
                  if not (j <= 1 and pre_issued["p1"]):
                      cx.dma('sp', xb.ap, x_ext[row0 + j * 128: row0 + (j + 1) * 128, :], s_x[j % NXB], W=[xb])
                  sm = smb[b]
                  ssb = sm.ap[:, 0:1]
                  rsb = sm.ap[:, 1:2]
                  cx.op('act', lambda e: e.activation(out=junk.ap, in_=xb.ap, func=AF.Square, accum_out=ssb),
                        R=[xb], W=[junk, sm])
                  cx.op('act', lambda e: e.activation(out=rsb, in_=ssb, func=AF.Sqrt, scale=1.0 / D, bias=RMS_EPS),
                        R=[sm], W=[sm])
                  cx.op('dve', lambda e: e.reciprocal(out=rsb, in_=rsb), R=[sm], W=[sm])
                  cx.op('dve', lambda e: e.scalar_tensor_tensor(
                      out=xn[b].ap, in0=xb.ap, scalar=rsb, in1=gpre.ap, op0=ALU.mult, op1=ALU.mult),
                      R=[xb, sm, gpre], W=[xn[b]])

              def p1_b(j):
                  b = j % 2
                  bA, bB = nbank(), nbank()
                  for k in range(16):
                      bb = bA if k < 8 else bB
                      cx.op('pe', lambda e, k=k, bb=bb: e.transpose(
                          out=bank_bf[bb][:, (k % 8) * 128:(k % 8 + 1) * 128],
                          in_=xn[b].ap[:, k * 128:(k + 1) * 128], identity=ident_bf),
                          R=[xn[b], cst_bf], W=[BK[bb]])
                  cx.op('act', lambda e: e.copy(
                      out=xT.ap[:, 0:8, j * 128:(j + 1) * 128],
                      in_=bank_bf[bA].rearrange("p (k n) -> p k n", n=128)), R=[BK[bA]], W=[xT])
                  cx.op('dve', lambda e: e.tensor_copy(
                      out=xT.ap[:, 8:16, j * 128:(j + 1) * 128],
                      in_=bank_bf[bB].rearrange("p (k n) -> p k n", n=128)), R=[BK[bB]], W=[xT])

              p1_a(0)
              p1_a(1)
              for j in range(NTS):
                  p1_b(j)
                  if j + 2 < NTS:
                      p1_a(j + 2)

              c0 = s * 8
              cx.op('dve', lambda e, c0=c0: e.tensor_tensor(
                  out=ang.ap, in0=pos_f.ap[:, c0:c0 + NTS].unsqueeze(2).broadcast_to([128, NTS, 8]),
                  in1=cp[:, C_INVF:C_INVF + 8].unsqueeze(1).broadcast_to([128, NTS, 8]), op=ALU.mult),
                  R=[pos_f, cpack], W=[ang])
              C1 = 6.28125
              C2 = float(2 * np.pi - 6.28125)
              for which in range(2):
                  shift = 0.25 if which == 0 else 0.0
                  lo, hi = (-1.5 * PI, 0.5 * PI) if which == 0 else (-PI, PI)
                  cx.op('dve', lambda e, shift=shift: e.tensor_scalar(
                      out=rr_u.ap, in0=ang.ap, scalar1=float(1.0 / (2 * np.pi)), scalar2=shift,
                      op0=ALU.mult, op1=ALU.add), R=[ang], W=[rr_u])
                  cx.op('dve', lambda e: e.tensor_copy(out=rr_k.ap, in_=rr_u.ap), R=[rr_u], W=[rr_k])
                  cx.op('dve', lambda e: e.tensor_copy(out=rr_kf.ap, in_=rr_k.ap), R=[rr_k], W=[rr_kf])
                  cx.op('dve', lambda e: e.scalar_tensor_tensor(
                      out=rr_u.ap, in0=rr_kf.ap, scalar=-C1, in1=ang.ap, op0=ALU.mult, op1=ALU.add),
                      R=[rr_kf, ang], W=[rr_u])
                  cx.op('dve', lambda e: e.scalar_tensor_tensor(
                      out=rr_u.ap, in0=rr_kf.ap, scalar=-C2, in1=rr_u.ap, op0=ALU.mult, op1=ALU.add),
                      R=[rr_kf, rr_u], W=[rr_u])
                  cx.op('dve', lambda e, lo=lo, hi=hi: e.tensor_scalar(
                      out=rr_u.ap, in0=rr_u.ap, scalar1=float(lo), scalar2=float(hi), op0=ALU.max, op1=ALU.min),
                      R=[rr_u], W=[rr_u])
                  cx.op('act', lambda e, which=which: e.activation(
                      out=cs.ap[:, :, which * 8:(which + 1) * 8], in_=rr_u.ap, func=AF.Sin,
                      bias=(0.5 * PI if which == 0 else 0.0)), R=[rr_u], W=[cs])

              _DUMPS.clear()
              _DUMPS['xT'] = xT
              _DUMPS['cs'] = cs
              _chk('P1')
              def wload(col0, ncols, after=()):
                  n = wb_state["n"]
                  wb_state["n"] += 1
                  i = n % 2
                  cx.dma('pool', WBUF[i].ap[:, :, 0:ncols], w_in_v[:, :, col0:col0 + ncols], s_wb[i],
                         R=list(after), W=[WBUF[i]])
                  return WBUF[i]

              def wload_pw():
                  n = wb_state["n"]
                  wb_state["n"] += 1
                  i = n % 2
                  v = mk(O_WB + i * 16384, [8, 1024], BF16)
                  for h in range(2):
                      cx.dma('pool', v.ap[:, :, h * 512:(h + 1) * 512], w_pw_v[:, :, h * 512:(h + 1) * 512], s_wb[i],
                             W=[v])
                  return v

              yT = [mk(O_YT + c * 4096, [1024], F32) for c in range(8)]
              zT = [mk(O_ZT + c * 2048, [1024], BF16) for c in range(8)]
              mixT = mk(O_MIX, [16, SUP], BF16)
              mixc = [mk(O_MIX + k * 2048, [SUP], BF16) for k in range(16)]
              diag = [mk(O_DG + i * 2048, [8, 4, 32], BF16) for i in range(2)]
              rsh = [mk(O_RS, [4, TOKS], BF16) for i in range(2)]
              hglu = [mk(O_HG + i * 2560, [TOKS], BF16) for i in range(2)]
              bw = Bump(O_WORK, ARENA)
              sig = [bw([512], BF16, align=1024) for _ in range(2)]
              ybf = [bw([512], BF16, align=1024) for _ in range(2)]
              ysq = [bw([512], BF16, align=1024) for _ in range(2)]
              mean_sb = [bw([512], F32, align=2048) for _ in range(2)]
              rstd_sb = [bw([512], F32, align=2048) for _ in range(2)]
              tln = [bw([512], F32, align=2048) for _ in range(2)]
              valsb = tln
              _gofs = [O_DG, O_HG, O_DG + 2048, O_RS, O_RS + 2048, O_RS + 4096, O_RS + 6144, O_HG + 2560]
              gcs_all = [mk(_gofs[c], [1024], BF16) for c in range(8)]
              pending = []

              rot["list"] = [0, 1, 2, 3]
              mean_b = [4, 5]
              msq_b = [6, 7]
              GRP = [(96, 352), (448, 352), (800, 352)]
              cnt = {"sig": 0, "y": 0}

              def conv_inproj(c, wbuf):
                  cc = c % 2
                  hg = hglu[c % 2]
                  for (st, n) in GRP:
                      bv = nbank()
                      for k in range(16):
                          cx.op('pe', lambda e, k=k, bv=bv, st=st, n=n: e.matmul(
                              out=bank[bv][:, 0:n], lhsT=wbuf.ap[:, k, cc * 256:cc * 256 + 128],
                              rhs=xT.ap[:, k, st:st + n], start=(k == 0), stop=(k == 15)),
                              R=[wbuf, xT], W=[BK[bv]])
                      bg = nbank()
                      for k in range(16):
                          cx.op('pe', lambda e, k=k, bg=bg, st=st, n=n: e.matmul(
                              out=bank[bg][:, 0:n], lhsT=wbuf.ap[:, k, cc * 256 + 128:cc * 256 + 256],
                              rhs=xT.ap[:, k, st:st + n], start=(k == 0), stop=(k == 15)),
                              R=[wbuf, xT], W=[BK[bg]])
                      sg = sig[cnt["sig"] % 2]
                      cnt["sig"] += 1
                      cx.op('act', lambda e, bg=bg, n=n, sg=sg: e.activation(
                          out=sg.ap[:, 0:n], in_=bank[bg][:, 0:n], func=AF.Sigmoid,
                          bias=cp[:, C_BFM + 8 + c:C_BFM + 9 + c]), R=[BK[bg], cpack], W=[sg])
                      vs = valsb[(cnt["sig"] - 1) % 2]
                      cx.op('act', lambda e, bv=bv, n=n, vs=vs: e.activation(
                          out=vs.ap[:, 0:n], in_=bank[bv][:, 0:n], func=AF.Identity,
                          bias=cp[:, C_BFM + c:C_BFM + c + 1]), R=[BK[bv], cpack], W=[vs])
                      cx.op('dve', lambda e, n=n, st=st, sg=sg, vs=vs: e.tensor_tensor(
                          out=hg.ap[:, st:st + n], in0=vs.ap[:, 0:n], in1=sg.ap[:, 0:n], op=ALU.mult),
                          R=[vs, sg], W=[hg])
                  if s == 0:
                      cx.op('dve', lambda e: e.tensor_scalar(
                          out=hg.ap[:, 96:128], in0=hg.ap[:, 96:128], scalar1=cp[:, C_FLAG:C_FLAG + 1], scalar2=None,
                          op0=ALU.mult), R=[hg, cpack], W=[hg])
                  dg = diag[c % 2]
                  cx.op('dve', lambda e: e.tensor_tensor(
                      out=dg.ap, in0=cp[:, C_EYE4:C_EYE4 + 32].unsqueeze(1).unsqueeze(1).broadcast_to([128, 8, 4, 32]),
                      in1=cp[:, C_WDW + c * 32:C_WDW + (c + 1) * 32].rearrange("p (m g) -> p m g", g=4).unsqueeze(3)
                      .broadcast_to([128, 8, 4, 32]), op=ALU.mult), R=[cpack], W=[dg])
              def conv_shuffle(c):
                  hg = hglu[c % 2]
                  rs = rsh[c % 2]
                  for g in range(4):
                      for i in range(4):
                          cx.dma('sp', rs.ap[32 * i:32 * (i + 1), g, 96:TOKS - i], hg.ap[32 * g:32 * (g + 1), 96 + i:TOKS],
                                 s_rs[c % 2], R=[hg], W=[rs], group=True)

              def conv_taps(c):
                  hg = hglu[c % 2]
                  dg = diag[c % 2]
                  rs = rsh[c % 2]
                  for t2 in range(2):
                      by = nbank()
                      for m in range(8):
                          o = HALO + t2 * 512 - KTAP + 4 * m
                          for g in range(4):
                              cx.op('pe', lambda e, m=m, g=g, by=by, o=o: e.matmul(
                                  out=bank[by][32 * g:32 * (g + 1), :], lhsT=dg.ap[:, m, g, :],
                                  rhs=rs.ap[:, g, o:o + 512], start=(m == 0), stop=(m == 7),
                                  tile_position=(0, 32 * g)), R=[dg, rs], W=[BK[by]])
                      ysl = yT[c].ap[:, t2 * 512:(t2 + 1) * 512]
                      cx.op('act', lambda e, by=by, ysl=ysl: e.activation(
                          out=ysl, in_=bank[by], func=AF.Identity, bias=cp[:, C_BDW + c:C_BDW + c + 1]),
                          R=[BK[by], cpack], W=[yT[c]])
                      yb = ybf[cnt["y"] % 2]
                      yq = ysq[cnt["y"] % 2]
                      cnt["y"] += 1
                      cx.op('dve', lambda e, ysl=ysl, yb=yb: e.tensor_copy(out=yb.ap, in_=ysl), R=[yT[c]], W=[yb])
                      cx.op('act', lambda e, ysl=ysl, yq=yq: e.activation(out=yq.ap, in_=ysl, func=AF.Square),
                            R=[yT[c]], W=[yq])
                      def _stats(t2=t2, yb=yb, yq=yq, c=c):
                          cx.op('pe', lambda e: e.matmul(
                              out=bank[mean_b[t2]], lhsT=ones_bf, rhs=yb.ap, start=(c == 0), stop=(c == 7)),
                              R=[yb, cst_bf], W=[BK[mean_b[t2]]])
                          cx.op('pe', lambda e: e.matmul(
                              out=bank[msq_b[t2]], lhsT=ones_bf, rhs=yq.ap, start=(c == 0), stop=(c == 7)),
                              R=[yq, cst_bf], W=[BK[msq_b[t2]]])
                      pending.append(_stats)

              wgc = [None, None]

              def gconv(c):
                  wbuf = wgc[c // 4]
                  for t2 in range(2):
                      bgc = nbank()
                      for k in range(16):
                          cx.op('pe', lambda e, k=k, bgc=bgc, t2=t2: e.matmul(
                              out=bank[bgc], lhsT=wbuf.ap[:, k, (c % 4) * 128:(c % 4 + 1) * 128],
                              rhs=xT.ap[:, k, HALO + t2 * 512:HALO + (t2 + 1) * 512], start=(k == 0), stop=(k == 15)),
                              R=[wbuf, xT], W=[BK[bgc]])
                      cx.op('act', lambda e, bgc=bgc, t2=t2: e.activation(
                          out=gcs_all[c].ap[:, t2 * 512:(t2 + 1) * 512], in_=bank[bgc], func=AF.Silu,
                          bias=cp[:, C_BFM + 16 + c:C_BFM + 17 + c]), R=[BK[bgc], cpack], W=[gcs_all[c]])

              wq = [wload(2304, 512), wload(2304 + 512, 512, after=[xt[(NTS - 1) % NXB]])]
              for c in range(9):
                  if c < 8:
                      conv_inproj(c, wq[(c // 2) % 2])
                      if c % 2 == 1 and c // 2 + 2 < 4:
                          wq[(c // 2) % 2] = wload(2304 + (c // 2 + 2) * 512, 512)
                      if c == 5:
                          wgc[0] = wload(4352, 512)
                      if c == 7:
                          wgc[1] = wload(4352 + 512, 512)
                  for f_ in pending:
                      f_()
                  del pending[:]
                  if c == 8:
                      gconv(0)
                      gconv(1)
                  if c >= 1:
                      conv_taps(c - 1)
                  if c < 8:
                      conv_shuffle(c)
              for f_ in pending:
                  f_()
              del pending[:]

              for c in range(8):
                  _DUMPS['yT%d' % c] = yT[c]
              _DUMPS['hglu1'] = hglu[1]
              _chk('P2a')
              for t2 in range(2):
                  cx.op('act', lambda e, t2=t2: e.copy(out=mean_sb[t2].ap, in_=bank[mean_b[t2]]),
                        R=[BK[mean_b[t2]]], W=[mean_sb[t2]])
                  cx.op('dve', lambda e, t2=t2: e.tensor_tensor(out=tln[t2].ap, in0=mean_sb[t2].ap, in1=mean_sb[t2].ap,
                                                               op=ALU.mult), R=[mean_sb[t2]], W=[tln[t2]])
                  cx.op('dve', lambda e, t2=t2: e.tensor_tensor(out=rstd_sb[t2].ap, in0=bank[msq_b[t2]], in1=tln[t2].ap,
                                                               op=ALU.subtract), R=[BK[msq_b[t2]], tln[t2]], W=[rstd_sb[t2]])
              for t2 in range(2):
                  cx.op('act', lambda e, t2=t2: e.activation(out=rstd_sb[t2].ap, in_=rstd_sb[t2].ap, func=AF.Sqrt,
                                                            bias=LN_EPS), R=[rstd_sb[t2]], W=[rstd_sb[t2]])
              for t2 in range(2):
                  cx.op('dve', lambda e, t2=t2: e.reciprocal(out=rstd_sb[t2].ap, in_=rstd_sb[t2].ap),
                        R=[rstd_sb[t2]], W=[rstd_sb[t2]])
              rot["list"] = list(range(8))
              wpw = None
              wkv = None
              tln4 = [tln[0], tln[1], mk(ybf[0].off, [512], F32), mk(ybf[0].off + 2048, [512], F32)]

              def ln_chunk(c):
                  for t2 in range(2):
                      tb = tln4[(c % 2) * 2 + t2]
                      ysl = yT[c].ap[:, t2 * 512:(t2 + 1) * 512]
                      cx.op('dve', lambda e, ysl=ysl, t2=t2, tb=tb: e.tensor_tensor(
                          out=tb.ap, in0=ysl, in1=mean_sb[t2].ap, op=ALU.subtract),
                          R=[yT[c], mean_sb[t2]], W=[tb])
                      cx.op('dve', lambda e, t2=t2, tb=tb: e.tensor_tensor(
                          out=tb.ap, in0=tb.ap, in1=rstd_sb[t2].ap, op=ALU.mult),
                          R=[tb, rstd_sb[t2]], W=[tb])
                      cx.op('act', lambda e, t2=t2, tb=tb: e.activation(
                          out=zT[c].ap[:, t2 * 512:(t2 + 1) * 512], in_=tb.ap, func=AF.Silu,
                          scale=cp[:, C_LNG + c:C_LNG + c + 1], bias=cp[:, C_LNB + c:C_LNB + c + 1]),
                          R=[tb, cpack], W=[zT[c]])

              ln_chunk(0)
              ln_chunk(1)
              for c in range(8):
                  if c >= 2:
                      gconv(c)
                  if c == 7:
                      wkv = wload(1024, 256)
                  if c == 3:
                      wpw = wload_pw()
                  if c + 2 < 8:
                      ln_chunk(c + 2)

              _DUMPS.clear()
              for c in range(8):
                  _DUMPS['zT%d' % c] = zT[c]
              _chk('P2b')
              for c2 in range(8):
                  for t2 in range(2):
                      bpw = nbank()
                      for k in range(8):
                          cx.op('pe', lambda e, k=k, bpw=bpw, t2=t2, c2=c2, wpw=wpw: e.matmul(
                              out=bank[bpw], lhsT=wpw.ap[:, k, c2 * 128:(c2 + 1) * 128],
                              rhs=zT[k].ap[:, t2 * 512:(t2 + 1) * 512], start=(k == 0), stop=(k == 7)),
                              R=[wpw, zT[k]], W=[BK[bpw]])
                      cx.op('dve', lambda e, bpw=bpw, t2=t2, c2=c2: e.scalar_tensor_tensor(
                          out=mixT.ap[:, 8 + c2, t2 * 512:(t2 + 1) * 512], in0=bank[bpw],
                          scalar=cp[:, C_BPW + c2:C_BPW + c2 + 1], in1=gcs_all[c2].ap[:, t2 * 512:(t2 + 1) * 512],
                          op0=ALU.add, op1=ALU.mult),
                          R=[BK[bpw], gcs_all[c2], cpack], W=[mixc[8 + c2]])

              _DUMPS.clear()
              _DUMPS['mixT'] = mixT
              _chk('P2c')
              wga = [wload(1280, 512), None]
              b3 = Bump(O_ZT, O_MIX)
              kTd = b3([2, 2, TOKS], BF16)
              vaug = b3([NTS, 2, 65], BF16)
              b3b = Bump(O_YT + 16384, O_ZT)
              gateA = b3b([8, 1024], BF16)
              bw = Bump(O_DG, ARENA)
              qf = [bw([512], F32, align=2048) for _ in range(2)]
              kvf = [bw([256], F32, align=1024) for _ in range(3)]
              kb2 = [bw([2, 2, 128], BF16, align=1024) for _ in range(3)]
              qb = [bw([512], BF16, align=1024) for _ in range(2)]
              PT = [[bw([512], BF16, align=1024) for _ in range(4)] for _ in range(2)]
              otmp = [bw([256], F32, align=1024) for _ in range(2)]
              bintok = bw([2304], F32)
              krot = [bw([2, 64], BF16, align=512) for _ in range(3)]
              rt1 = [bw([8, 2, 8], F32, align=512) for _ in range(2)]
              rt2 = [bw([8, 2, 8], F32, align=512) for _ in range(2)]
              den = [bw([8], F32, align=256) for _ in range(2)]
              ag2 = [[bw([256], BF16, align=512) for _ in range(2)] for _ in range(2)]
              gaf = qf

              cx.dma('sp', bintok.ap, bintok_d, s_bintok, W=[bintok])
              cx.op('pool', lambda e: e.memset(vaug.ap[:, :, :, 64:65], 1.0), W=[vaug])
              for i_ in range(3):
                  cx.op('pool', lambda e, i_=i_: e.memset(kb2[i_].ap, 0.0), W=[kb2[i_]])

              def rope(src3, dst3, H, j, r1, r2, RS_, RD_):
                  u = src3[:, :, 0:16].rearrange("p h (t e) -> p h t e", e=8)
                  cosb = cs.ap[:, j, 0:8].unsqueeze(1).unsqueeze(1).broadcast_to([128, H, 2, 8])
                  sinb = cs.ap[:, j, 8:16].unsqueeze(1).unsqueeze(1).broadcast_to([128, H, 2, 8])
                  a1 = r1.ap[:, 0:H, :, :]
                  a2 = r2.ap[:, 0:H, :, :]
                  cx.op('dve', lambda e: e.tensor_tensor(out=a1, in0=u, in1=cosb, op=ALU.mult), R=[cs] + RS_, W=[r1])
                  cx.op('dve', lambda e: e.tensor_tensor(out=a2, in0=u, in1=sinb, op=ALU.mult), R=[cs] + RS_, W=[r2])
                  cx.op('dve', lambda e: e.tensor_tensor(out=dst3[:, :, 0:8], in0=a1[:, :, 0, :], in1=a2[:, :, 1, :],
                                                         op=ALU.subtract), R=[r1, r2], W=RD_)
                  cx.op('dve', lambda e: e.tensor_tensor(out=dst3[:, :, 8:16], in0=a1[:, :, 1, :], in1=a2[:, :, 0, :],
                                                         op=ALU.add), R=[r1, r2], W=RD_)
                  cx.op('act', lambda e: e.copy(out=dst3[:, :, 16:64], in_=src3[:, :, 16:64]), R=RS_, W=RD_)

              def kv_a(j, p2):
                  bkv = nbank()
                  for k in range(16):
                      cx.op('pe', lambda e, k=k, bkv=bkv, wkv=wkv: e.matmul(
                          out=bank[bkv][:, 0:256], lhsT=xT.ap[:, k, j * 128:(j + 1) * 128], rhs=wkv.ap[:, k, 0:256],
                          start=(k == 0), stop=(k == 15)), R=[wkv, xT], W=[BK[bkv]])
                  cx.op('dve', lambda e, bkv=bkv: e.tensor_tensor(out=kvf[p2].ap, in0=bank[bkv][:, 0:256],
                                                                  in1=bintok.ap[:, 1024:1280], op=ALU.add),
                        R=[BK[bkv], bintok], W=[kvf[p2]])
                  rope(kvf[p2].ap[:, 0:128].rearrange("p (h d) -> p h d", d=64), krot[p2].ap, 2, j, rt1[p2 % 2], rt2[p2 % 2],
                       [kvf[p2]], [krot[p2]])
                  for eh in range(2):
                      cx.op('pool', lambda e, eh=eh: e.tensor_copy(out=kb2[p2].ap[:, :, eh, eh * 64:(eh + 1) * 64],
                                                                   in_=krot[p2].ap), R=[krot[p2]], W=[kb2[p2]])
                  cx.op('pool', lambda e: e.tensor_copy(
                      out=vaug.ap[:, j, :, 0:64], in_=kvf[p2].ap[:, 128:256].rearrange("p (h d) -> p h d", d=64)),
                      R=[kvf[p2]], W=[vaug])

              def kv_b(j, p2):
                  bt = nbank()
                  for g in range(2):
                      for eh in range(2):
                          cx.op('pe', lambda e, g=g, eh=eh, bt=bt: e.transpose(
                              out=bank_bf[bt][:, (g * 2 + eh) * 128:(g * 2 + eh + 1) * 128],
                              in_=kb2[p2].ap[:, g, eh, :], identity=ident_bf),
                              R=[kb2[p2], cst_bf], W=[BK[bt]])
                  cx.op('act', lambda e, bt=bt: e.copy(
                      out=kTd.ap[:, :, :, j * 128:(j + 1) * 128],
                      in_=bank_bf[bt][:, 0:512].rearrange("p (g a n) -> p g a n", a=2, n=128)), R=[BK[bt]], W=[kTd])

              gstate = {"gi": 0}

              def ga_step(ci, j):
                  bga = nbank()
                  for k in range(16):
                      cx.op('pe', lambda e, k=k, bga=bga, wg_=wga[ci]: e.matmul(
                          out=bank[bga], lhsT=xT.ap[:, k, j * 128:(j + 1) * 128], rhs=wg_.ap[:, k, :],
                          start=(k == 0), stop=(k == 15)), R=[wga[ci], xT], W=[BK[bga]])
                  gf = gaf[gstate["gi"] % 2]
                  gstate["gi"] += 1
                  cx.op('dve', lambda e, bga=bga, gf=gf: e.tensor_tensor(
                      out=gf.ap, in0=bank[bga], in1=bintok.ap[:, 1280 + ci * 512:1280 + (ci + 1) * 512], op=ALU.add),
                      R=[BK[bga], bintok], W=[gf])
                  cx.op('act', lambda e, gf=gf: e.activation(
                      out=gateA.ap[:, j - 1, ci * 512:(ci + 1) * 512], in_=gf.ap, func=AF.Silu),
                      R=[gf], W=[gateA])

              kv_a(0, 0)
              kv_a(1, 1)
              for j in range(NTS):
                  if j + 2 < NTS:
                      kv_a(j + 2, (j + 2) % 3)
                  kv_b(j, j % 3)
              for j in range(1, NTS):
                  ga_step(0, j)

              _DUMPS.clear()
              _DUMPS['kTd'] = kTd
              _DUMPS['vaug'] = vaug
              _chk('P3kv')
              wga[1] = wload(1280 + 512, 512)
              wqq = [wload(0, 512), None]
              for j in range(1, NTS):
                  ga_step(1, j)

              _DUMPS.clear()
              _DUMPS['gateA'] = gateA
              _chk('P3ga')
              wqq[1] = wload(512, 512)
              qT_all = [[mk(O_YT + (qi * 8 + jj) * 1024, [4, 128], BF16) for jj in range(8)] for qi in range(2)]
              def q_a(qi, j, p2):
                  bq = nbank()
                  for k in range(16):
                      cx.op('pe', lambda e, k=k, bq=bq, wq_=wqq[qi]: e.matmul(
                          out=bank[bq], lhsT=xT.ap[:, k, j * 128:(j + 1) * 128], rhs=wq_.ap[:, k, :],
                          start=(k == 0), stop=(k == 15)), R=[wqq[qi], xT], W=[BK[bq]])
                  cx.op('dve', lambda e, bq=bq: e.tensor_tensor(
                      out=qf[p2].ap, in0=bank[bq], in1=bintok.ap[:, qi * 512:(qi + 1) * 512], op=ALU.add),
                      R=[BK[bq], bintok], W=[qf[p2]])
                  rope(qf[p2].ap.rearrange("p (h d) -> p h d", d=64),
                       qb[p2].ap.rearrange("p (h d) -> p h d", d=64), 8, j, rt1[p2], rt2[p2], [qf[p2]], [qb[p2]])

              def q_b(qi, j, p2):
                  bt = nbank()
                  for t in range(4):
                      cx.op('pe', lambda e, t=t, bt=bt: e.transpose(
                          out=bank_bf[bt][:, t * 128:(t + 1) * 128], in_=qb[p2].ap[:, t * 128:(t + 1) * 128],
                          identity=ident_bf), R=[qb[p2], cst_bf], W=[BK[bt]])
                  qTb = qT_all[qi][j - 1]
                  cx.op('act', lambda e, bt=bt, qTb=qTb: e.copy(
                      out=qTb.ap, in_=bank_bf[bt][:, 0:512].rearrange("p (t n) -> p t n", n=128)),
                      R=[BK[bt]], W=[qTb])

              qits = [(qi, j) for qi in range(2) for j in range(1, NTS)]
              q_a(qits[0][0], qits[0][1], 0)
              for n_, (qi, j) in enumerate(qits):
                  if n_ + 1 < len(qits):
                      q_a(qits[n_ + 1][0], qits[n_ + 1][1], (n_ + 1) % 2)
                  q_b(qi, j, n_ % 2)

              wo_ch = [mk(O_XT, [16, 512], BF16), mk(O_XT + 16384, [16, 512], BF16), WBUF[0], WBUF[1]]
              s_woc = [s_wo, s_wo1, s_wb[0], s_wb[1]]
              for h in range(4):
                  cx.dma('pool', wo_ch[h].ap, w_out_v[:, :, h * 512:(h + 1) * 512], s_woc[h], W=[wo_ch[h]])

              def att_s1(qi, j, p2):
                  qTb = qT_all[qi][j - 1]
                  for kb in range(2):
                      ktile = j - 1 + kb
                      if kb == 1:
                          mi = 0
                      else:
                          mi = 2 if (s == 0 and j == 1) else 1
                      for hh in range(2):
                          bs = nbank()
                          cx.op('pe', lambda e, bs=bs, mi=mi: e.matmul(
                              out=bank[bs], lhsT=ident_bf, rhs=mbias.ap[:, mi, :, :].rearrange("p h q -> p (h q)"),
                              start=True, stop=False), R=[cst_bf, mbias], W=[BK[bs]])
                          for hl in range(4):
                              head = hh * 4 + hl
                              t, eh = head // 2, head % 2
                              cx.op('pe', lambda e, bs=bs, hl=hl, t=t, eh=eh, ktile=ktile: e.matmul(
                                  out=bank[bs][:, hl * 128:(hl + 1) * 128],
                                  lhsT=kTd.ap[:, qi, eh, ktile * 128:(ktile + 1) * 128],
                                  rhs=qTb.ap[:, t, :], start=False, stop=(hl == 3)),
                                  R=[kTd, qTb], W=[BK[bs]])
                          pt = PT[p2][kb * 2 + hh]
                          cx.op('act', lambda e, bs=bs, pt=pt: e.activation(out=pt.ap, in_=bank[bs], func=AF.Exp,
                                                                            scale=0.125), R=[BK[bs]], W=[pt])

              def att_s2(qi, j, p2):
                  bos = []
                  for hh in range(2):
                      bo = nbank()
                      bos.append(bo)
                      for hl in range(4):
                          for kb in range(2):
                              pt = PT[p2][kb * 2 + hh]
                              cx.op('pe', lambda e, bo=bo, hl=hl, kb=kb, pt=pt: e.matmul(
                                  out=bank[bo][:, hl * 65:(hl + 1) * 65], lhsT=pt.ap[:, hl * 128:(hl + 1) * 128],
                                  rhs=vaug.ap[:, j - 1 + kb, qi, :], start=(kb == 0), stop=(kb == 1)),
                                  R=[pt, vaug], W=[BK[bo]])
                  for hh in range(2):
                      bo = bos[hh]
                      o3 = bank[bo][:, 0:260].rearrange("p (h d) -> p h d", d=65)
                      dn = den[hh].ap[:, 0:4]
                      h0 = qi * 8 + hh * 4
                      cx.op('dve', lambda e, o3=o3, dn=dn, h0=h0: e.tensor_tensor(
                          out=dn, in0=o3[:, :, 64], in1=expsink.ap[:, h0:h0 + 4], op=ALU.add),
                          R=[BK[bo], expsink], W=[den[hh]])
                      cx.op('dve', lambda e, dn=dn: e.reciprocal(out=dn, in_=dn), R=[den[hh]], W=[den[hh]])
                      ot3 = otmp[hh].ap.rearrange("p (h d) -> p h d", d=64)
                      cx.op('dve', lambda e, o3=o3, dn=dn, ot3=ot3: e.tensor_tensor(
                          out=ot3, in0=o3[:, :, 0:64], in1=dn.unsqueeze(2).broadcast_to([128, 4, 64]), op=ALU.mult),
                          R=[BK[bo], den[hh]], W=[otmp[hh]])
                      c0g = qi * 512 + hh * 256
                      agb = ag2[p2][hh]
                      cx.op('pool', lambda e, hh=hh, c0g=c0g, agb=agb: e.tensor_tensor(
                          out=agb.ap, in0=otmp[hh].ap, in1=gateA.ap[:, j - 1, c0g:c0g + 256], op=ALU.mult),
                          R=[otmp[hh], gateA], W=[agb])

              def att_s3(qi, j, p2):
                  for hh in range(2):
                      agb = ag2[p2][hh]
                      bt2 = nbank()
                      for t in range(2):
                          cx.op('pe', lambda e, t=t, bt2=bt2, agb=agb: e.transpose(
                              out=bank_bf[bt2][:, t * 128:(t + 1) * 128], in_=agb.ap[:, t * 128:(t + 1) * 128],
                              identity=ident_bf), R=[agb, cst_bf], W=[BK[bt2]])
                      kc0 = qi * 4 + hh * 2
                      cx.op('dve', lambda e, bt2=bt2, kc0=kc0: e.tensor_copy(
                          out=mixT.ap[:, kc0:kc0 + 2, (j - 1) * 128:j * 128],
                          in_=bank_bf[bt2][:, 0:256].rearrange("p (t n) -> p t n", n=128)),
                          R=[BK[bt2]], W=[mixc[kc0], mixc[kc0 + 1]])

              its = [(qi, j) for qi in range(2) for j in range(1, NTS)]
              NI = len(its)
              for step in range(NI + 2):
                  if step < NI:
                      att_s1(its[step][0], its[step][1], step % 2)
                  if 1 <= step <= NI:
                      att_s2(its[step - 1][0], its[step - 1][1], (step - 1) % 2)
                  if step >= 2:
                      att_s3(its[step - 2][0], its[step - 2][1], (step - 2) % 2)

              _DUMPS.clear()
              _DUMPS['mixT'] = mixT
              _chk('P3')
              b4 = Bump(O_DG, ARENA)
              gpost = b4([D], F32)
              bout = b4([D], F32)
              b4b = Bump(O_YT, O_MIX)
              xt4 = [b4b([D], F32) for _ in range(2)]
              yf = [b4b([D], F32) for _ in range(2)]
              junk4 = b4b([D], BF16)
              assert b4b.p <= O_YT + 36864
              cx.dma('sp', gpost.ap, gpost_d, s_gpost, W=[gpost])
              cx.dma('sp', bout.ap, bout_d, s_bout, W=[bout])
              for j in range(8):
                  b = j % 2
                  r_in = row0 + HALO + j * 128
                  r_out = s * SUP + j * 128
                  if j < 2:
                      cx.dma('sp', xt4[b].ap, x_ext[r_in:r_in + 128, :], s_x4[b], W=[xt4[b]])
                  if j == 4 and s + 1 < NSUP:
                      nrow0 = (s + 1) * SUP
                      cx.dma('sp', gpre.ap, gpre_d, s_gpre, W=[gpre])
                      cx.dma('sp', xt[0].ap, x_ext[nrow0: nrow0 + 128, :], s_x[0], W=[xt[0]])
                      cx.dma('sp', xt[1].ap, x_ext[nrow0 + 128: nrow0 + 256, :], s_x[1], W=[xt[1]])
                      pre_issued["p1"] = True
                  for nb in range(4):
                      by = nbank()
                      for k in range(16):
                          cx.op('pe', lambda e, k=k, by=by, j=j, woc=wo_ch[nb]: e.matmul(
                              out=bank[by], lhsT=mixT.ap[:, k, j * 128:(j + 1) * 128],
                              rhs=woc.ap[:, k, :], start=(k == 0), stop=(k == 15)),
                              R=[wo_ch[nb], mixT], W=[BK[by]])
                      cx.op('dve', lambda e, by=by, nb=nb, b=b: e.tensor_tensor(
                          out=yf[b].ap[:, nb * 512:(nb + 1) * 512], in0=bank[by],
                          in1=bout.ap[:, nb * 512:(nb + 1) * 512], op=ALU.add), R=[BK[by], bout], W=[yf[b]])
                  sm = smb[b]
                  ssb = sm.ap[:, 2:3]
                  rsb = sm.ap[:, 3:4]
                  cx.op('act', lambda e, b=b, ssb=ssb: e.activation(out=junk4.ap, in_=yf[b].ap, func=AF.Square,
                                                                    accum_out=ssb), R=[yf[b]], W=[junk4, sm])
                  cx.op('act', lambda e, ssb=ssb, rsb=rsb: e.activation(out=rsb, in_=ssb, func=AF.Sqrt,
                                                                        scale=1.0 / D, bias=RMS_EPS), R=[sm], W=[sm])
                  cx.op('dve', lambda e, rsb=rsb: e.reciprocal(out=rsb, in_=rsb), R=[sm], W=[sm])
                  cx.op('dve', lambda e, b=b, rsb=rsb: e.scalar_tensor_tensor(
                      out=yf[b].ap, in0=yf[b].ap, scalar=rsb, in1=gpost.ap, op0=ALU.mult, op1=ALU.mult),
                      R=[yf[b], sm, gpost], W=[yf[b]])
                  cx.op('dve', lambda e, b=b: e.tensor_tensor(out=xt4[b].ap, in0=yf[b].ap, in1=xt4[b].ap, op=ALU.add),
                        R=[yf[b], xt4[b]], W=[xt4[b]])
                  cx.dma('sp', out_d[r_out:r_out + 128, :], xt4[b].ap, s_o[b], R=[xt4[b]])
                  if j + 2 < 8:
                      r_n = row0 + HALO + (j + 2) * 128
                      cx.dma('sp', xt4[b].ap, x_ext[r_n:r_n + 128, :], s_x4[b], W=[xt4[b]])

        except _Stop:
            pass
        if STOP is not None:
            s_dbg = cx.dsem("dbg")
            for nm, b in _DUMPS.items():
                shp = list(b.ap.shape)
                dd = nc.dram_tensor("dbg_" + nm, shp, b.ap.dtype, kind="ExternalOutput").ap()
                cx.dma('sp', dd, b.ap, s_dbg, R=[b])
            cx.finish([s_dbg])
        else:
            cx.finish(s_o)
    return nc


_PROGRAM = None


def _host_inputs(x, positions, pre_norm_g, w_in, b_in, sinks, w_dw, b_dw, conv_ln_g, conv_ln_b,
                 w_pw, b_pw, w_out, b_out, post_norm_g):
    f32 = np.float32
    x2 = np.asarray(x, f32).reshape(SEQ, D)
    pos = np.asarray(positions).reshape(SEQ).astype(np.int32)
    w_in0 = np.asarray(w_in, f32)[0]
    b_in0 = np.asarray(b_in, f32)[0]
    cols = list(range(0, 2304))
    for c in range(8):
        cols += list(range(2304 + c * 128, 2304 + (c + 1) * 128))
        cols += list(range(3328 + c * 128, 3328 + (c + 1) * 128))
    cols += list(range(4352, 5376))
    w_in_p = np.ascontiguousarray(w_in0[:, np.array(cols)])
    w_out0 = np.ascontiguousarray(np.asarray(w_out, f32)[0])
    w_pw0 = np.ascontiguousarray(np.asarray(w_pw, f32)[0])

    def bc(v, n):
        return np.ascontiguousarray(np.broadcast_to(np.asarray(v, f32).reshape(1, n), (128, n)))

    gpre = bc(np.asarray(pre_norm_g)[0], D)
    gpost = bc(np.asarray(post_norm_g)[0], D)
    boutb = bc(np.asarray(b_out)[0], D)
    bintok = bc(b_in0[0:2304], 2304)

    def fm(v):
        v = np.asarray(v, f32)
        return np.ascontiguousarray(v.reshape(-1, 128).T)

    cpack = np.zeros((128, NCP), f32)
    cpack[:, C_BFM:C_BFM + 24] = fm(b_in0[2304:5376])
    cpack[:, C_BDW:C_BDW + 8] = fm(np.asarray(b_dw)[0])
    cpack[:, C_LNG:C_LNG + 8] = fm(np.asarray(conv_ln_g)[0])
    cpack[:, C_LNB:C_LNB + 8] = fm(np.asarray(conv_ln_b)[0])
    cpack[:, C_BPW:C_BPW + 8] = fm(np.asarray(b_pw)[0])
    wdw = np.asarray(w_dw, f32)[0]
    wpad = np.concatenate([np.zeros((1, 1024), f32), wdw], axis=0)
    wpk = wpad.reshape(8, 4, 8, 4, 32)
    cpack[:, C_WDW:C_WDW + 256] = wpk.transpose(1, 4, 2, 0, 3).reshape(128, 256)
    cpack[:, C_EYE4:C_EYE4 + 32] = np.tile(np.eye(32, dtype=f32), (4, 1))
    cpack[:, C_SINK:C_SINK + 16] = bc(np.asarray(sinks)[0], 16)
    invf = (np.float32(500000.0) ** (-(np.arange(0, 16, 2, dtype=np.float32)) / np.float32(16))).astype(f32)
    cpack[:, C_INVF:C_INVF + 8] = bc(invf, 8)
    cpack[:, C_ID:C_ID + 128] = np.eye(128, dtype=f32)
    kk = np.arange(128)[:, None]
    qq = np.arange(128)[None, :]
    cpack[:, C_MCUR:C_MCUR + 128] = (qq >= kk).astype(f32)
    cpack[:, C_MPREV:C_MPREV + 128] = (kk > qq).astype(f32)

    in_maps = []
    for c in range(NCORES):
        t0 = c * TPC
        xe = np.zeros((TPC + HALO, D), f32)
        pe = np.zeros((TPC + HALO,), np.int32)
        if c > 0:
            xe[:] = x2[t0 - HALO:t0 + TPC]
            pe[:] = pos[t0 - HALO:t0 + TPC]
        else:
            xe[HALO:] = x2[0:TPC]
            pe[HALO:] = pos[0:TPC]
        cpk = cpack.copy()
        cpk[:, C_FLAG] = 0.0 if c == 0 else 1.0
        cpk[:, C_MPREVF:C_MPREVF + 128] = 0.0 if c == 0 else cpack[:, C_MPREV:C_MPREV + 128]
        in_maps.append({
            "x_ext": xe, "pos_t": np.ascontiguousarray(pe.reshape(17, 128).T),
            "w_in": w_in_p, "w_out": w_out0, "w_pw": w_pw0, "cpack": cpk,
            "gpre_bc": gpre, "gpost_bc": gpost, "bout_bc": boutb, "bintok_bc": bintok,
        })
    return in_maps


def kernel(**inputs):
    global _PROGRAM
    in_maps = _host_inputs(**inputs)
    if _PROGRAM is None:
        _PROGRAM = build_program()
    res = run_bass_kernel_spmd(_PROGRAM, in_maps, core_ids=list(range(NCORES)))
    out = np.concatenate([np.asarray(r["out"], np.float32) for r in res.results], axis=0)
    return out.reshape(1, SEQ, D)
```

```python
import contextlib
import numpy as np
import concourse.bass as bass
import concourse.mybir as mybir
from concourse.bass_utils import run_bass_kernel_spmd

F32 = mybir.dt.float32
BF16 = mybir.dt.bfloat16
I32 = mybir.dt.int32
U8 = mybir.dt.uint8
ALU = mybir.AluOpType
AF = mybir.ActivationFunctionType

SELF_SYNC = {'pe': False, 'act': True, 'dve': True, 'pool': True, 'sp': False}


class T:
    __slots__ = ("name", "w", "r")

    def __init__(self, name):
        self.name = name
        self.w = None
        self.r = {}


class _DSem:
    def __init__(self, sem, name):
        self.sem = sem
        self.count = 0
        self.name = name


class _Op:
    __slots__ = ("eng", "fn", "seq", "waits", "marked", "dma", "rank", "kn", "gseq")

    def __init__(self, eng, fn, seq):
        self.eng = eng
        self.fn = fn
        self.seq = seq
        self.waits = []
        self.marked = False
        self.dma = None
        self.rank = 0
        self.kn = None


class Ctx:
    ENGS = ('pe', 'act', 'dve', 'pool', 'sp')

    def __init__(self, nc):
        self.nc = nc
        self.ops = {e: [] for e in self.ENGS}
        self.known = {e: {} for e in self.ENGS}
        self.stack = contextlib.ExitStack()
        self.esem = {}
        self.dsems = []

    def __enter__(self):
        self.stack.__enter__()
        for e in self.ENGS:
            self.esem[e] = self.stack.enter_context(self.nc.semaphore("es_" + e))
        return self

    def __exit__(self, *a):
        return self.stack.__exit__(*a)

    def sb(self, name, shape, dtype):
        h = self.stack.enter_context(self.nc.sbuf_tensor(name, list(shape), dtype))
        return h[:, :] if len(shape) == 2 else h[:]

    def ps(self, name, shape, dtype):
        h = self.stack.enter_context(self.nc.psum_tensor(name, list(shape), dtype))
        return h[:, :] if len(shape) == 2 else h[:]

    def dsem(self, name):
        s = self.stack.enter_context(self.nc.semaphore("ds_" + name))
        d = _DSem(s, name)
        self.dsems.append(d)
        return d

    def constcol(self, v):
        return float(v)

    @staticmethod
    def _flat(L):
        out = []
        for t in L:
            if isinstance(t, T):
                out.append(t)
            else:
                out.extend(t.ts)
        return out

    def _emit(self, E, fn, R, W, dma=None, nowait=False, group=False):
        R = self._flat(R)
        W = self._flat(W)
        deps = []
        if not nowait:
            for t in R:
                if t.w is not None:
                    deps.append(t.w)
            for t in W:
                if t.w is not None:
                    deps.append(t.w)
                deps.extend(t.r.values())
        op = _Op(E, fn, len(self.ops[E]))
        kn = self.known[E]
        deps.sort(key=lambda d: -(self.ops[d[1]][d[2]].gseq if d[0] == 'eng' else -1))
        for d in deps:
            if d[0] == 'eng':
                _, Fe, seq = d
                if Fe == E and not SELF_SYNC[E]:
                    continue
                key = ('e', Fe)
                if kn.get(key, -1) >= seq:
                    continue
                kn[key] = seq
                op.waits.append(d)
                self.ops[Fe][seq].marked = True
                for k2, v2 in self.ops[Fe][seq].kn.items():
                    if kn.get(k2, -1) < v2:
                        kn[k2] = v2
            else:
                _, ds, val = d
                if group and ds is dma:
                    continue
                key = ('d', id(ds))
                if kn.get(key, -1) >= val:
                    continue
                kn[key] = val
                op.waits.append(d)
        op.kn = dict(kn)
        self.gcount = getattr(self, 'gcount', 0) + 1
        op.gseq = self.gcount
        self.ops[E].append(op)
        if dma is None:
            tok = ('eng', E, op.seq)
            rkey = E
        else:
            dma.count += 16
            tok = ('dma', dma, dma.count)
            op.dma = dma
            rkey = ('dma', id(dma))
        for t in R:
            t.r[rkey] = tok
        for t in W:
            t.w = tok
            t.r = {}
        return op

    def op(self, E, fn, R=(), W=()):
        return self._emit(E, fn, R, W)

    def dma(self, E, out, in_, ds, R=(), W=(), nowait=False, group=False, **kw):
        return self._emit(E, lambda e: e.dma_start(out=out, in_=in_, **kw), R, W, dma=ds, nowait=nowait, group=group)

    def finish(self, out_sems):
        nc = self.nc
        op = _Op('sp', None, len(self.ops['sp']))
        for ds in out_sems:
            op.waits.append(('dma', ds, ds.count))
        self.ops['sp'].append(op)
        for e in self.ENGS:
            r = 0
            for o in self.ops[e]:
                if o.marked:
                    r += 1
                    o.rank = r
        ops = self.ops
        esem = self.esem

        def replay(eng_name):
            def run(e):
                for o in ops[eng_name]:
                    for d in o.waits:
                        if d[0] == 'eng':
                            e.wait_ge(esem[d[1]], ops[d[1]][d[2]].rank)
                        else:
                            e.wait_ge(d[1].sem, d[2])
                    if o.fn is None:
                        continue
                    ins = o.fn(e)
                    if o.dma is not None:
                        ins.then_inc(o.dma.sem, 16)
                    elif o.marked:
                        ins.then_inc(esem[eng_name], 1)
            return run

        with nc.Block() as block:
            block.tensor(replay('pe'))
            block.scalar(replay('act'))
            block.vector(replay('dve'))
            block.gpsimd(replay('pool'))
            block.sync(replay('sp'))


D = 2048
SEQ = 16384
NCORES = 8
TPC = SEQ // NCORES
HALO = 128
SUP = 1024
NSUP = TPC // SUP
NTS = SUP // 128 + 1
TOKS = SUP + HALO
INC = 5376
RMS_EPS = 1e-6
LN_EPS = 1e-5
KTAP = 31
PI = float(np.pi)
BLK = 512

C_BFM = 0
C_BDW = 24
C_LNG = 32
C_LNB = 40
C_BPW = 48
C_FLAG = 56
C_WDW = 57
C_SINK = C_WDW + 8 * 32
C_INVF = C_SINK + 16
C_ID = C_INVF + 8
C_MCUR = C_ID + 128
C_MPREV = C_MCUR + 128
C_MPREVF = C_MPREV + 128
C_EYE4 = C_MPREVF + 128
NCP = C_EYE4 + 32


class Buf:
    __slots__ = ("ap", "ts", "off", "nbytes")

    def __init__(self, ap, ts, off, nbytes):
        self.ap = ap
        self.ts = ts
        self.off = off
        self.nbytes = nbytes


def _esz(dt):
    return 2 if dt == BF16 else (1 if dt == U8 else 4)


STOP = None


class _Stop(Exception):
    pass


_DUMPS = {}


def _chk(name):
    if STOP == name:
        raise _Stop()


def build_program():
    nc = bass.Bass("TRN2", target_bir_lowering=False)
    x_ext = nc.dram_tensor("x_ext", [TPC + HALO, D], F32, kind="ExternalInput").ap()
    pos_t = nc.dram_tensor("pos_t", [128, 17], I32, kind="ExternalInput").ap()
    w_in = nc.dram_tensor("w_in", [D, INC], F32, kind="ExternalInput").ap()
    w_out = nc.dram_tensor("w_out", [D, D], F32, kind="ExternalInput").ap()
    w_pw = nc.dram_tensor("w_pw", [1024, 1024], F32, kind="ExternalInput").ap()
    cpack_d = nc.dram_tensor("cpack", [128, NCP], F32, kind="ExternalInput").ap()
    gpre_d = nc.dram_tensor("gpre_bc", [128, D], F32, kind="ExternalInput").ap()
    gpost_d = nc.dram_tensor("gpost_bc", [128, D], F32, kind="ExternalInput").ap()
    bout_d = nc.dram_tensor("bout_bc", [128, D], F32, kind="ExternalInput").ap()
    bintok_d = nc.dram_tensor("bintok_bc", [128, 2304], F32, kind="ExternalInput").ap()
    out_d = nc.dram_tensor("out", [TPC, D], F32, kind="ExternalOutput").ap()

    w_in_v = w_in.rearrange("(k p) n -> p k n", p=128)
    w_out_v = w_out.rearrange("(k p) n -> p k n", p=128)
    w_pw_v = w_pw.rearrange("(k p) n -> p k n", p=128)

    cx = Ctx(nc)
    with cx:
        PERS = 12288
        O_XT = PERS
        O_YT = O_XT + 36864
        O_ZT = O_YT + 32768
        O_MIX = O_ZT + 16384
        O_WB = O_MIX + 32768
        O_DG = O_WB + 32768
        O_RS = O_DG + 4096
        O_HG = O_RS + 9216
        O_WORK = O_HG + 5120
        ARENA = O_WORK + 20480
        assert ARENA % BLK == 0 and O_WORK % BLK == 0
        arena = cx.sb("arena", [128, ARENA], U8)
        BT = [T(f"blk{i}") for i in range(ARENA // BLK)]

        def mk(off, shape, dt):
            n = int(np.prod(shape))
            nb = n * _esz(dt)
            assert off % 4 == 0 and off + nb <= ARENA, (off, nb)
            ap = arena[:, off:off + nb].bitcast(dt)
            if len(shape) == 2:
                ap = ap.rearrange("p (a b) -> p a b", b=shape[1])
            elif len(shape) == 3:
                ap = ap.rearrange("p (a b c) -> p a b c", b=shape[1], c=shape[2])
            ts = BT[off // BLK:(off + nb - 1) // BLK + 1]
            return Buf(ap, ts, off, nb)

        class Bump:
            def __init__(self, start, end):
                self.p = start
                self.end = end

            def __call__(self, shape, dt, align=BLK):
                self.p = (self.p + align - 1) // align * align
                b = mk(self.p, shape, dt)
                self.p += b.nbytes
                assert self.p <= self.end, (self.p, self.end)
                return b

        bp = Bump(0, PERS)
        cpack = bp([NCP], F32)
        cst_bf = bp([5, 128], BF16)
        ident_bf = cst_bf.ap[:, 0, :]
        mcur_bf = cst_bf.ap[:, 1, :]
        mprev_bf = cst_bf.ap[:, 2, :]
        mprevf_bf = cst_bf.ap[:, 3, :]
        ones_bf = cst_bf.ap[:, 4, :]
        expsink = bp([16], F32, align=64)
        pos_i = bp([17], I32, align=64)
        pos_f = bp([17], F32, align=64)
        cs = bp([NTS, 16], F32, align=BLK)
        ang = bp([NTS, 8], F32, align=64)
        rr_u = bp([NTS, 8], F32, align=64)
        rr_k = bp([NTS, 8], I32, align=64)
        rr_kf = bp([NTS, 8], F32, align=64)
        smb = [bp([16], F32, align=BLK) for _ in range(2)]
        mbias = bp([3, 4, 128], BF16)
        cp = cpack.ap

        bank = [cx.ps(f"bank{i}", [128, 512], F32) for i in range(8)]
        bank_bf = [b.bitcast(BF16) for b in bank]
        BK = [T(f"bk{i}") for i in range(8)]
        rot = {"list": list(range(8)), "i": 0}

        def nbank():
            L = rot["list"]
            b = L[rot["i"] % len(L)]
            rot["i"] += 1
            return b

        s_c = cx.dsem("const")
        s_c2 = cx.dsem("const2")
        s_gpre = cx.dsem("gpre")
        s_bintok = cx.dsem("bintok")
        s_gpost = cx.dsem("gpost")
        s_bout = cx.dsem("bout")
        s_wo1 = cx.dsem("wout1")
        NXB = 5
        s_x = [cx.dsem("x%d" % i) for i in range(NXB)]
        s_wb = [cx.dsem("wb0"), cx.dsem("wb1")]
        s_m = cx.dsem("misc")
        s_o = [cx.dsem("o0"), cx.dsem("o1")]
        s_x4 = [cx.dsem("x40"), cx.dsem("x41")]
        s_wo = cx.dsem("wout")
        s_rs = [cx.dsem("rs0"), cx.dsem("rs1")]

        cx.dma('sp', cp, cpack_d, s_c, W=[cpack], nowait=True)
        cx.dma('sp', pos_i.ap, pos_t, s_c2, W=[pos_i], nowait=True)
        cx.op('dve', lambda e: e.tensor_copy(out=cst_bf.ap[:, 0:4, :],
                                             in_=cp[:, C_ID:C_ID + 512].rearrange("p (a b) -> p a b", b=128)),
              R=[cpack], W=[cst_bf])
        cx.op('dve', lambda e: e.memset(ones_bf, 1.0 / 1024), W=[cst_bf])
        cx.op('dve', lambda e: e.tensor_scalar(
            out=mbias.ap, in0=cp[:, C_MCUR:C_MCUR + 384].rearrange("p (a b) -> p a b", b=128).unsqueeze(2)
            .broadcast_to([128, 3, 4, 128]), scalar1=-1.0, scalar2=30000.0, op0=ALU.add, op1=ALU.mult),
            R=[cpack], W=[mbias])
        cx.op('act', lambda e: e.activation(out=expsink.ap, in_=cp[:, C_SINK:C_SINK + 16], func=AF.Exp),
              R=[cpack], W=[expsink])
        cx.op('dve', lambda e: e.tensor_copy(out=pos_f.ap, in_=pos_i.ap), R=[pos_i], W=[pos_f])

        WBUF = [mk(O_WB + i * 16384, [16, 512], BF16) for i in range(2)]
        wb_state = {"n": 0}
        pre_issued = {"p1": False}

        try:
          for s in range(NSUP):
              row0 = s * SUP
              xT = mk(O_XT, [16, TOKS], BF16)
              b1 = Bump(O_YT, O_MIX)
              b1p = Bump(O_DG + 16384, ARENA)
              gpre = b1p([D], F32)
              xt2_ = mk(O_YT, [D], F32)
              xn = [mk(O_YT + 16384, [D], BF16), mk(O_YT + 20480, [D], BF16)]
              junk = mk(O_YT + 32768, [D], BF16)
              xt = [b1p([D], F32), mk(O_YT + 36864, [D], F32), xt2_, mk(O_YT + 8192, [D], F32), mk(O_YT + 24576, [D], F32)]
              if not pre_issued["p1"]:
                  cx.dma('sp', gpre.ap, gpre_d, s_gpre, W=[gpre])
              def p1_a(j):
                  b = j % 2
                  xb = xt[j % NXB]
                  if not (j <= 1 and pre_issued["p1"]):
                      cx.dma('sp', xb.ap, x_ext[row0 + j * 128: row0 + (j + 1) * 128, :], s_x[j % NXB], W=[xb])
                  sm = smb[b]
                  ssb = sm.ap[:, 0:1]
                  rsb = sm.ap[:, 1:2]
                  cx.op('act', lambda e: e.activation(out=junk.ap, in_=xb.ap, func=AF.Square, accum_out=ssb),
                        R=[xb], W=[junk, sm])
                  cx.op('act', lambda e: e.activation(out=rsb, in_=ssb, func=AF.Sqrt, scale=1.0 / D, bias=RMS_EPS),
                        R=[sm], W=[sm])
                  cx.op('dve', lambda e: e.reciprocal(out=rsb, in_=rsb), R=[sm], W=[sm])
                  cx.op('dve', lambda e: e.scalar_tensor_tensor(
                      out=xn[b].ap, in0=xb.ap, scalar=rsb, in1=gpre.ap, op0=ALU.mult, op1=ALU.mult),
                      R=[xb, sm, gpre], W=[xn[b]])

              def p1_b(j):
                  b = j % 2
                  bA, bB = nbank(), nbank()
                  for k in range(16):
                      bb = bA if k < 8 else bB
                      cx.op('pe', lambda e, k=k, bb=bb: e.transpose(
                          out=bank_bf[bb][:, (k % 8) * 128:(k % 8 + 1) * 128],
                          in_=xn[b].ap[:, k * 128:(k + 1) * 128], identity=ident_bf),
                          R=[xn[b], cst_bf], W=[BK[bb]])
                  cx.op('act', lambda e: e.copy(
                      out=xT.ap[:, 0:8, j * 128:(j + 1) * 128],
                      in_=bank_bf[bA].rearrange("p (k n) -> p k n", n=128)), R=[BK[bA]], W=[xT])
                  cx.op('dve', lambda e: e.tensor_copy(
                      out=xT.ap[:, 8:16, j * 128:(j + 1) * 128],
                      in_=bank_bf[bB].rearrange("p (k n) -> p k n", n=128)), R=[BK[bB]], W=[xT])

              p1_a(0)
              p1_a(1)
              for j in range(NTS):
                  p1_b(j)
                  if j + 2 < NTS:
                      p1_a(j + 2)

              c0 = s * 8
              cx.op('dve', lambda e, c0=c0: e.tensor_tensor(
                  out=ang.ap, in0=pos_f.ap[:, c0:c0 + NTS].unsqueeze(2).broadcast_to([128, NTS, 8]),
                  in1=cp[:, C_INVF:C_INVF + 8].unsqueeze(1).broadcast_to([128, NTS, 8]), op=ALU.mult),
                  R=[pos_f, cpack], W=[ang])
              C1 = 6.28125
              C2 = float(2 * np.pi - 6.28125)
              for which in range(2):
                  shift = 0.25 if which == 0 else 0.0
                  lo, hi = (-1.5 * PI, 0.5 * PI) if which == 0 else (-PI, PI)
                  cx.op('dve', lambda e, shift=shift: e.tensor_scalar(
                      out=rr_u.ap, in0=ang.ap, scalar1=float(1.0 / (2 * np.pi)), scalar2=shift,
                      op0=ALU.mult, op1=ALU.add), R=[ang], W=[rr_u])
                  cx.op('dve', lambda e: e.tensor_copy(out=rr_k.ap, in_=rr_u.ap), R=[rr_u], W=[rr_k])
                  cx.op('dve', lambda e: e.tensor_copy(out=rr_kf.ap, in_=rr_k.ap), R=[rr_k], W=[rr_kf])
                  cx.op('dve', lambda e: e.scalar_tensor_tensor(
                      out=rr_u.ap, in0=rr_kf.ap, scalar=-C1, in1=ang.ap, op0=ALU.mult, op1=ALU.add),
                      R=[rr_kf, ang], W=[rr_u])
                  cx.op('dve', lambda e: e.scalar_tensor_tensor(
                      out=rr_u.ap, in0=rr_kf.ap, scalar=-C2, in1=rr_u.ap, op0=ALU.mult, op1=ALU.add),
                      R=[rr_kf, rr_u], W=[rr_u])
                  cx.op('dve', lambda e, lo=lo, hi=hi: e.tensor_scalar(
                      out=rr_u.ap, in0=rr_u.ap, scalar1=float(lo), scalar2=float(hi), op0=ALU.max, op1=ALU.min),
                      R=[rr_u], W=[rr_u])
                  cx.op('act', lambda e, which=which: e.activation(
                      out=cs.ap[:, :, which * 8:(which + 1) * 8], in_=rr_u.ap, func=AF.Sin,
                      bias=(0.5 * PI if which == 0 else 0.0)), R=[rr_u], W=[cs])

              _DUMPS.clear()
              _DUMPS['xT'] = xT
              _DUMPS['cs'] = cs
              _chk('P1')
              def wload(col0, ncols, after=()):
                  n = wb_state["n"]
                  wb_state["n"] += 1
                  i = n % 2
                  cx.dma('pool', WBUF[i].ap[:, :, 0:ncols], w_in_v[:, :, col0:col0 + ncols], s_wb[i],
                         R=list(after), W=[WBUF[i]])
                  return WBUF[i]

              def wload_pw():
                  n = wb_state["n"]
                  wb_state["n"] += 1
                  i = n % 2
                  v = mk(O_WB + i * 16384, [8, 1024], BF16)
                  for h in range(2):
                      cx.dma('pool', v.ap[:, :, h * 512:(h + 1) * 512], w_pw_v[:, :, h * 512:(h + 1) * 512], s_wb[i],
                             W=[v])
                  return v

              yT = [mk(O_YT + c * 4096, [1024], F32) for c in range(8)]
              zT = [mk(O_ZT + c * 2048, [1024], BF16) for c in range(8)]
              mixT = mk(O_MIX, [16, SUP], BF16)
              mixc = [mk(O_MIX + k * 2048, [SUP], BF16) for k in range(16)]
              diag = [mk(O_DG + i * 2048, [8, 4, 32], BF16) for i in range(2)]
              rsh = [mk(O_RS, [4, TOKS], BF16) for i in range(2)]
              hglu = [mk(O_HG + i * 2560, [TOKS], BF16) for i in range(2)]
              bw = Bump(O_WORK, ARENA)
              sig = [bw([512], BF16, align=1024) for _ in range(2)]
              ybf = [bw([512], BF16, align=1024) for _ in range(2)]
              ysq = [bw([512], BF16, align=1024) for _ in range(2)]
              mean_sb = [bw([512], F32, align=2048) for _ in range(2)]
              rstd_sb = [bw([512], F32, align=2048) for _ in range(2)]
              tln = [bw([512], F32, align=2048) for _ in range(2)]
              valsb = tln
              _gofs = [O_DG, O_HG, O_DG + 2048, O_RS, O_RS + 2048, O_RS + 4096, O_RS + 6144, O_HG + 2560]
              gcs_all = [mk(_gofs[c], [1024], BF16) for c in range(8)]
              pending = []

              rot["list"] = [0, 1, 2, 3]
              mean_b = [4, 5]
              msq_b = [6, 7]
              GRP = [(96, 352), (448, 352), (800, 352)]
              cnt = {"sig": 0, "y": 0}

              def conv_inproj(c, wbuf):
                  cc = c % 2
                  hg = hglu[c % 2]
                  for (st, n) in GRP:
                      bv = nbank()
                      for k in range(16):
                          cx.op('pe', lambda e, k=k, bv=bv, st=st, n=n: e.matmul(
                              out=bank[bv][:, 0:n], lhsT=wbuf.ap[:, k, cc * 256:cc * 256 + 128],
                              rhs=xT.ap[:, k, st:st + n], start=(k == 0), stop=(k == 15)),
                              R=[wbuf, xT], W=[BK[bv]])
                      bg = nbank()
                      for k in range(16):
                          cx.op('pe', lambda e, k=k, bg=bg, st=st, n=n: e.matmul(
                              out=bank[bg][:, 0:n], lhsT=wbuf.ap[:, k, cc * 256 + 128:cc * 256 + 256],
                              rhs=xT.ap[:, k, st:st + n], start=(k == 0), stop=(k == 15)),
                              R=[wbuf, xT], W=[BK[bg]])
                      sg = sig[cnt["sig"] % 2]
                      cnt["sig"] += 1
                      cx.op('act', lambda e, bg=bg, n=n, sg=sg: e.activation(
                          out=sg.ap[:, 0:n], in_=bank[bg][:, 0:n], func=AF.Sigmoid,
                          bias=cp[:, C_BFM + 8 + c:C_BFM + 9 + c]), R=[BK[bg], cpack], W=[sg])
                      vs = valsb[(cnt["sig"] - 1) % 2]
                      cx.op('act', lambda e, bv=bv, n=n, vs=vs: e.activation(
                          out=vs.ap[:, 0:n], in_=bank[bv][:, 0:n], func=AF.Identity,
                          bias=cp[:, C_BFM + c:C_BFM + c + 1]), R=[BK[bv], cpack], W=[vs])
                      cx.op('dve', lambda e, n=n, st=st, sg=sg, vs=vs: e.tensor_tensor(
                          out=hg.ap[:, st:st + n], in0=vs.ap[:, 0:n], in1=sg.ap[:, 0:n], op=ALU.mult),
                          R=[vs, sg], W=[hg])
                  if s == 0:
                      cx.op('dve', lambda e: e.tensor_scalar(
                          out=hg.ap[:, 96:128], in0=hg.ap[:, 96:128], scalar1=cp[:, C_FLAG:C_FLAG + 1], scalar2=None,
                          op0=ALU.mult), R=[hg, cpack], W=[hg])
                  dg = diag[c % 2]
                  cx.op('dve', lambda e: e.tensor_tensor(
                      out=dg.ap, in0=cp[:, C_EYE4:C_EYE4 + 32].unsqueeze(1).unsqueeze(1).broadcast_to([128, 8, 4, 32]),
                      in1=cp[:, C_WDW + c * 32:C_WDW + (c + 1) * 32].rearrange("p (m g) -> p m g", g=4).unsqueeze(3)
                      .broadcast_to([128, 8, 4, 32]), op=ALU.mult), R=[cpack], W=[dg])
              def conv_shuffle(c):
                  hg = hglu[c % 2]
                  rs = rsh[c % 2]
                  for g in range(4):
                      for i in range(4):
                          cx.dma('sp', rs.ap[32 * i:32 * (i + 1), g, 96:TOKS - i], hg.ap[32 * g:32 * (g + 1), 96 + i:TOKS],
                                 s_rs[c % 2], R=[hg], W=[rs], group=True)

              def conv_taps(c):
                  hg = hglu[c % 2]
                  dg = diag[c % 2]
                  rs = rsh[c % 2]
                  for t2 in range(2):
                      by = nbank()
                      for m in range(8):
                          o = HALO + t2 * 512 - KTAP + 4 * m
                          for g in range(4):
                              cx.op('pe', lambda e, m=m, g=g, by=by, o=o: e.matmul(
                                  out=bank[by][32 * g:32 * (g + 1), :], lhsT=dg.ap[:, m, g, :],
                                  rhs=rs.ap[:, g, o:o + 512], start=(m == 0), stop=(m == 7),
                                  tile_position=(0, 32 * g)), R=[dg, rs], W=[BK[by]])
                      ysl = yT[c].ap[:, t2 * 512:(t2 + 1) * 512]
                      cx.op('act', lambda e, by=by, ysl=ysl: e.activation(
                          out=ysl, in_=bank[by], func=AF.Identity, bias=cp[:, C_BDW + c:C_BDW + c + 1]),
                          R=[BK[by], cpack], W=[yT[c]])
                      yb = ybf[cnt["y"] % 2]
                      yq = ysq[cnt["y"] % 2]
                      cnt["y"] += 1
                      cx.op('dve', lambda e, ysl=ysl, yb=yb: e.tensor_copy(out=yb.ap, in_=ysl), R=[yT[c]], W=[yb])
                      cx.op('act', lambda e, ysl=ysl, yq=yq: e.activation(out=yq.ap, in_=ysl, func=AF.Square),
                            R=[yT[c]], W=[yq])
                      def _stats(t2=t2, yb=yb, yq=yq, c=c):
                          cx.op('pe', lambda e: e.matmul(
                              out=bank[mean_b[t2]], lhsT=ones_bf, rhs=yb.ap, start=(c == 0), stop=(c == 7)),
                              R=[yb, cst_bf], W=[BK[mean_b[t2]]])
                          cx.op('pe', lambda e: e.matmul(
                              out=bank[msq_b[t2]], lhsT=ones_bf, rhs=yq.ap, start=(c == 0), stop=(c == 7)),
                              R=[yq, cst_bf], W=[BK[msq_b[t2]]])
                      pending.append(_stats)

              wgc = [None, None]

              def gconv(c):
                  wbuf = wgc[c // 4]
                  for t2 in range(2):
                      bgc = nbank()
                      for k in range(16):
                          cx.op('pe', lambda e, k=k, bgc=bgc, t2=t2: e.matmul(
                              out=bank[bgc], lhsT=wbuf.ap[:, k, (c % 4) * 128:(c % 4 + 1) * 128],
                              rhs=xT.ap[:, k, HALO + t2 * 512:HALO + (t2 + 1) * 512], start=(k == 0), stop=(k == 15)),
                              R=[wbuf, xT], W=[BK[bgc]])
                      cx.op('act', lambda e, bgc=bgc, t2=t2: e.activation(
                          out=gcs_all[c].ap[:, t2 * 512:(t2 + 1) * 512], in_=bank[bgc], func=AF.Silu,
                          bias=cp[:, C_BFM + 16 + c:C_BFM + 17 + c]), R=[BK[bgc], cpack], W=[gcs_all[c]])

              wq = [wload(2304, 512), wload(2304 + 512, 512, after=[xt[(NTS - 1) % NXB]])]
              for c in range(9):
                  if c < 8:
                      conv_inproj(c, wq[(c // 2) % 2])
                      if c % 2 == 1 and c // 2 + 2 < 4:
                          wq[(c // 2) % 2] = wload(2304 + (c // 2 + 2) * 512, 512)
                      if c == 5:
                          wgc[0] = wload(4352, 512)
                      if c == 7:
                          wgc[1] = wload(4352 + 512, 512)
                  for f_ in pending:
                      f_()
                  del pending[:]
                  if c == 8:
                      gconv(0)
                      gconv(1)
                  if c >= 1:
                      conv_taps(c - 1)
                  if c < 8:
                      conv_shuffle(c)
              for f_ in pending:
                  f_()
              del pending[:]

              for c in range(8):
                  _DUMPS['yT%d' % c] = yT[c]
              _DUMPS['hglu1'] = hglu[1]
              _chk('P2a')
              for t2 in range(2):
                  cx.op('act', lambda e, t2=t2: e.copy(out=mean_sb[t2].ap, in_=bank[mean_b[t2]]),
                        R=[BK[mean_b[t2]]], W=[mean_sb[t2]])
                  cx.op('dve', lambda e, t2=t2: e.tensor_tensor(out=tln[t2].ap, in0=mean_sb[t2].ap, in1=mean_sb[t2].ap,
                                                               op=ALU.mult), R=[mean_sb[t2]], W=[tln[t2]])
                  cx.op('dve', lambda e, t2=t2: e.tensor_tensor(out=rstd_sb[t2].ap, in0=bank[msq_b[t2]], in1=tln[t2].ap,
                                                               op=ALU.subtract), R=[BK[msq_b[t2]], tln[t2]], W=[rstd_sb[t2]])
              for t2 in range(2):
                  cx.op('act', lambda e, t2=t2: e.activation(out=rstd_sb[t2].ap, in_=rstd_sb[t2].ap, func=AF.Sqrt,
                                                            bias=LN_EPS), R=[rstd_sb[t2]], W=[rstd_sb[t2]])
              for t2 in range(2):
                  cx.op('dve', lambda e, t2=t2: e.reciprocal(out=rstd_sb[t2].ap, in_=rstd_sb[t2].ap),
                        R=[rstd_sb[t2]], W=[rstd_sb[t2]])
              rot["list"] = list(range(8))
              wpw = None
              wkv = None
              tln4 = [tln[0], tln[1], mk(ybf[0].off, [512], F32), mk(ybf[0].off + 2048, [512], F32)]

              def ln_chunk(c):
                  for t2 in range(2):
                      tb = tln4[(c % 2) * 2 + t2]
                      ysl = yT[c].ap[:, t2 * 512:(t2 + 1) * 512]
                      cx.op('dve', lambda e, ysl=ysl, t2=t2, tb=tb: e.tensor_tensor(
                          out=tb.ap, in0=ysl, in1=mean_sb[t2].ap, op=ALU.subtract),
                          R=[yT[c], mean_sb[t2]], W=[tb])
                      cx.op('dve', lambda e, t2=t2, tb=tb: e.tensor_tensor(
                          out=tb.ap, in0=tb.ap, in1=rstd_sb[t2].ap, op=ALU.mult),
                          R=[tb, rstd_sb[t2]], W=[tb])
                      cx.op('act', lambda e, t2=t2, tb=tb: e.activation(
                          out=zT[c].ap[:, t2 * 512:(t2 + 1) * 512], in_=tb.ap, func=AF.Silu,
                          scale=cp[:, C_LNG + c:C_LNG + c + 1], bias=cp[:, C_LNB + c:C_LNB + c + 1]),
                          R=[tb, cpack], W=[zT[c]])

              ln_chunk(0)
              ln_chunk(1)
              for c in range(8):
                  if c >= 2:
                      gconv(c)
                  if c == 7:
                      wkv = wload(1024, 256)
                  if c == 3:
                      wpw = wload_pw()
                  if c + 2 < 8:
                      ln_chunk(c + 2)

              _DUMPS.clear()
              for c in range(8):
                  _DUMPS['zT%d' % c] = zT[c]
              _chk('P2b')
              for c2 in range(8):
                  for t2 in range(2):
                      bpw = nbank()
                      for k in range(8):
                          cx.op('pe', lambda e, k=k, bpw=bpw, t2=t2, c2=c2, wpw=wpw: e.matmul(
                              out=bank[bpw], lhsT=wpw.ap[:, k, c2 * 128:(c2 + 1) * 128],
                              rhs=zT[k].ap[:, t2 * 512:(t2 + 1) * 512], start=(k == 0), stop=(k == 7)),
                              R=[wpw, zT[k]], W=[BK[bpw]])
                      cx.op('dve', lambda e, bpw=bpw, t2=t2, c2=c2: e.scalar_tensor_tensor(
                          out=mixT.ap[:, 8 + c2, t2 * 512:(t2 + 1) * 512], in0=bank[bpw],
                          scalar=cp[:, C_BPW + c2:C_BPW + c2 + 1], in1=gcs_all[c2].ap[:, t2 * 512:(t2 + 1) * 512],
                          op0=ALU.add, op1=ALU.mult),
                          R=[BK[bpw], gcs_all[c2], cpack], W=[mixc[8 + c2]])

              _DUMPS.clear()
              _DUMPS['mixT'] = mixT
              _chk('P2c')
              wga = [wload(1280, 512), None]
              b3 = Bump(O_ZT, O_MIX)
              kTd = b3([2, 2, TOKS], BF16)
              vaug = b3([NTS, 2, 65], BF16)
              b3b = Bump(O_YT + 16384, O_ZT)
              gateA = b3b([8, 1024], BF16)
              bw = Bump(O_DG, ARENA)
              qf = [bw([512], F32, align=2048) for _ in range(2)]
              kvf = [bw([256], F32, align=1024) for _ in range(3)]
              kb2 = [bw([2, 2, 128], BF16, align=1024) for _ in range(3)]
              qb = [bw([512], BF16, align=1024) for _ in range(2)]
              PT = [[bw([512], BF16, align=1024) for _ in range(4)] for _ in range(2)]
              otmp = [bw([256], F32, align=1024) for _ in range(2)]
              bintok = bw([2304], F32)
              krot = [bw([2, 64], BF16, align=512) for _ in range(3)]
              rt1 = [bw([8, 2, 8], F32, align=512) for _ in range(2)]
              rt2 = [bw([8, 2, 8], F32, align=512) for _ in range(2)]
              den = [bw([8], F32, align=256) for _ in range(2)]
              ag2 = [[bw([256], BF16, align=512) for _ in range(2)] for _ in range(2)]
              gaf = qf

              cx.dma('sp', bintok.ap, bintok_d, s_bintok, W=[bintok])
              cx.op('pool', lambda e: e.memset(vaug.ap[:, :, :, 64:65], 1.0), W=[vaug])
              for i_ in range(3):
                  cx.op('pool', lambda e, i_=i_: e.memset(kb2[i_].ap, 0.0), W=[kb2[i_]])

              def rope(src3, dst3, H, j, r1, r2, RS_, RD_):
                  u = src3[:, :, 0:16].rearrange("p h (t e) -> p h t e", e=8)
                  cosb = cs.ap[:, j, 0:8].unsqueeze(1).unsqueeze(1).broadcast_to([128, H, 2, 8])
                  sinb = cs.ap[:, j, 8:16].unsqueeze(1).unsqueeze(1).broadcast_to([128, H, 2, 8])
                  a1 = r1.ap[:, 0:H, :, :]
                  a2 = r2.ap[:, 0:H, :, :]
                  cx.op('dve', lambda e: e.tensor_tensor(out=a1, in0=u, in1=cosb, op=ALU.mult), R=[cs] + RS_, W=[r1])
                  cx.op('dve', lambda e: e.tensor_tensor(out=a2, in0=u, in1=sinb, op=ALU.mult), R=[cs] + RS_, W=[r2])
                  cx.op('dve', lambda e: e.tensor_tensor(out=dst3[:, :, 0:8], in0=a1[:, :, 0, :], in1=a2[:, :, 1, :],
                                                         op=ALU.subtract), R=[r1, r2], W=RD_)
                  cx.op('dve', lambda e: e.tensor_tensor(out=dst3[:, :, 8:16], in0=a1[:, :, 1, :], in1=a2[:, :, 0, :],
                                                         op=ALU.add), R=[r1, r2], W=RD_)
                  cx.op('act', lambda e: e.copy(out=dst3[:, :, 16:64], in_=src3[:, :, 16:64]), R=RS_, W=RD_)

              def kv_a(j, p2):
                  bkv = nbank()
                  for k in range(16):
                      cx.op('pe', lambda e, k=k, bkv=bkv, wkv=wkv: e.matmul(
                          out=bank[bkv][:, 0:256], lhsT=xT.ap[:, k, j * 128:(j + 1) * 128], rhs=wkv.ap[:, k, 0:256],
                          start=(k == 0), stop=(k == 15)), R=[wkv, xT], W=[BK[bkv]])
                  cx.op('dve', lambda e, bkv=bkv: e.tensor_tensor(out=kvf[p2].ap, in0=bank[bkv][:, 0:256],
                                                                  in1=bintok.ap[:, 1024:1280], op=ALU.add),
                        R=[BK[bkv], bintok], W=[kvf[p2]])
                  rope(kvf[p2].ap[:, 0:128].rearrange("p (h d) -> p h d", d=64), krot[p2].ap, 2, j, rt1[p2 % 2], rt2[p2 % 2],
                       [kvf[p2]], [krot[p2]])
                  for eh in range(2):
                      cx.op('pool', lambda e, eh=eh: e.tensor_copy(out=kb2[p2].ap[:, :, eh, eh * 64:(eh + 1) * 64],
                                                                   in_=krot[p2].ap), R=[krot[p2]], W=[kb2[p2]])
                  cx.op('pool', lambda e: e.tensor_copy(
                      out=vaug.ap[:, j, :, 0:64], in_=kvf[p2].ap[:, 128:256].rearrange("p (h d) -> p h d", d=64)),
                      R=[kvf[p2]], W=[vaug])

              def kv_b(j, p2):
                  bt = nbank()
                  for g in range(2):
                      for eh in range(2):
                          cx.op('pe', lambda e, g=g, eh=eh, bt=bt: e.transpose(
                              out=bank_bf[bt][:, (g * 2 + eh) * 128:(g * 2 + eh + 1) * 128],
                              in_=kb2[p2].ap[:, g, eh, :], identity=ident_bf),
                              R=[kb2[p2], cst_bf], W=[BK[bt]])
                  cx.op('act', lambda e, bt=bt: e.copy(
                      out=kTd.ap[:, :, :, j * 128:(j + 1) * 128],
                      in_=bank_bf[bt][:, 0:512].rearrange("p (g a n) -> p g a n", a=2, n=128)), R=[BK[bt]], W=[kTd])

              gstate = {"gi": 0}

              def ga_step(ci, j):
                  bga = nbank()
                  for k in range(16):
                      cx.op('pe', lambda e, k=k, bga=bga, wg_=wga[ci]: e.matmul(
                          out=bank[bga], lhsT=xT.ap[:, k, j * 128:(j + 1) * 128], rhs=wg_.ap[:, k, :],
                          start=(k == 0), stop=(k == 15)), R=[wga[ci], xT], W=[BK[bga]])
                  gf = gaf[gstate["gi"] % 2]
                  gstate["gi"] += 1
                  cx.op('dve', lambda e, bga=bga, gf=gf: e.tensor_tensor(
                      out=gf.ap, in0=bank[bga], in1=bintok.ap[:, 1280 + ci * 512:1280 + (ci + 1) * 512], op=ALU.add),
                      R=[BK[bga], bintok], W=[gf])
                  cx.op('act', lambda e, gf=gf: e.activation(
                      out=gateA.ap[:, j - 1, ci * 512:(ci + 1) * 512], in_=gf.ap, func=AF.Silu),
                      R=[gf], W=[gateA])

              kv_a(0, 0)
              kv_a(1, 1)
              for j in range(NTS):
                  if j + 2 < NTS:
                      kv_a(j + 2, (j + 2) % 3)
                  kv_b(j, j % 3)
              for j in range(1, NTS):
                  ga_step(0, j)

              _DUMPS.clear()
              _DUMPS['kTd'] = kTd
              _DUMPS['vaug'] = vaug
              _chk('P3kv')
              wga[1] = wload(1280 + 512, 512)
              wqq = [wload(0, 512), None]
              for j in range(1, NTS):
                  ga_step(1, j)

              _DUMPS.clear()
              _DUMPS['gateA'] = gateA
              _chk('P3ga')
              wqq[1] = wload(512, 512)
              qT_all = [[mk(O_YT + (qi * 8 + jj) * 1024, [4, 128], BF16) for jj in range(8)] for qi in range(2)]
              def q_a(qi, j, p2):
                  bq = nbank()
                  for k in range(16):
                      cx.op('pe', lambda e, k=k, bq=bq, wq_=wqq[qi]: e.matmul(
                          out=bank[bq], lhsT=xT.ap[:, k, j * 128:(j + 1) * 128], rhs=wq_.ap[:, k, :],
                          start=(k == 0), stop=(k == 15)), R=[wqq[qi], xT], W=[BK[bq]])
                  cx.op('dve', lambda e, bq=bq: e.tensor_tensor(
                      out=qf[p2].ap, in0=bank[bq], in1=bintok.ap[:, qi * 512:(qi + 1) * 512], op=ALU.add),
                      R=[BK[bq], bintok], W=[qf[p2]])
                  rope(qf[p2].ap.rearrange("p (h d) -> p h d", d=64),
                       qb[p2].ap.rearrange("p (h d) -> p h d", d=64), 8, j, rt1[p2], rt2[p2], [qf[p2]], [qb[p2]])

              def q_b(qi, j, p2):
                  bt = nbank()
                  for t in range(4):
                      cx.op('pe', lambda e, t=t, bt=bt: e.transpose(
                          out=bank_bf[bt][:, t * 128:(t + 1) * 128], in_=qb[p2].ap[:, t * 128:(t + 1) * 128],
                          identity=ident_bf), R=[qb[p2], cst_bf], W=[BK[bt]])
                  qTb = qT_all[qi][j - 1]
                  cx.op('act', lambda e, bt=bt, qTb=qTb: e.copy(
                      out=qTb.ap, in_=bank_bf[bt][:, 0:512].rearrange("p (t n) -> p t n", n=128)),
                      R=[BK[bt]], W=[qTb])

              qits = [(qi, j) for qi in range(2) for j in range(1, NTS)]
              q_a(qits[0][0], qits[0][1], 0)
              for n_, (qi, j) in enumerate(qits):
                  if n_ + 1 < len(qits):
                      q_a(qits[n_ + 1][0], qits[n_ + 1][1], (n_ + 1) % 2)
                  q_b(qi, j, n_ % 2)

              wo_ch = [mk(O_XT, [16, 512], BF16), mk(O_XT + 16384, [16, 512], BF16), WBUF[0], WBUF[1]]
              s_woc = [s_wo, s_wo1, s_wb[0], s_wb[1]]
              for h in range(4):
                  cx.dma('pool', wo_ch[h].ap, w_out_v[:, :, h * 512:(h + 1) * 512], s_woc[h], W=[wo_ch[h]])

              def att_s1(qi, j, p2):
                  qTb = qT_all[qi][j - 1]
                  for kb in range(2):
                      ktile = j - 1 + kb
                      if kb == 1:
                          mi = 0
                      else:
                          mi = 2 if (s == 0 and j == 1) else 1
                      for hh in range(2):
                          bs = nbank()
                          cx.op('pe', lambda e, bs=bs, mi=mi: e.matmul(
                              out=bank[bs], lhsT=ident_bf, rhs=mbias.ap[:, mi, :, :].rearrange("p h q -> p (h q)"),
                              start=True, stop=False), R=[cst_bf, mbias], W=[BK[bs]])
                          for hl in range(4):
                              head = hh * 4 + hl
                              t, eh = head // 2, head % 2
                              cx.op('pe', lambda e, bs=bs, hl=hl, t=t, eh=eh, ktile=ktile: e.matmul(
                                  out=bank[bs][:, hl * 128:(hl + 1) * 128],
                                  lhsT=kTd.ap[:, qi, eh, ktile * 128:(ktile + 1) * 128],
                                  rhs=qTb.ap[:, t, :], start=False, stop=(hl == 3)),
                                  R=[kTd, qTb], W=[BK[bs]])
                          pt = PT[p2][kb * 2 + hh]
                          cx.op('act', lambda e, bs=bs, pt=pt: e.activation(out=pt.ap, in_=bank[bs], func=AF.Exp,
                                                                            scale=0.125), R=[BK[bs]], W=[pt])

              def att_s2(qi, j, p2):
                  bos = []
                  for hh in range(2):
                      bo = nbank()
                      bos.append(bo)
                      for hl in range(4):
                          for kb in range(2):
                              pt = PT[p2][kb * 2 + hh]
                              cx.op('pe', lambda e, bo=bo, hl=hl, kb=kb, pt=pt: e.matmul(
                                  out=bank[bo][:, hl * 65:(hl + 1) * 65], lhsT=pt.ap[:, hl * 128:(hl + 1) * 128],
                                  rhs=vaug.ap[:, j - 1 + kb, qi, :], start=(kb == 0), stop=(kb == 1)),
                                  R=[pt, vaug], W=[BK[bo]])
                  for hh in range(2):
                      bo = bos[hh]
                      o3 = bank[bo][:, 0:260].rearrange("p (h d) -> p h d", d=65)
                      dn = den[hh].ap[:, 0:4]
                      h0 = qi * 8 + hh * 4
                      cx.op('dve', lambda e, o3=o3, dn=dn, h0=h0: e.tensor_tensor(
                          out=dn, in0=o3[:, :, 64], in1=expsink.ap[:, h0:h0 + 4], op=ALU.add),
                          R=[BK[bo], expsink], W=[den[hh]])
                      cx.op('dve', lambda e, dn=dn: e.reciprocal(out=dn, in_=dn), R=[den[hh]], W=[den[hh]])
                      ot3 = otmp[hh].ap.rearrange("p (h d) -> p h d", d=64)
                      cx.op('dve', lambda e, o3=o3, dn=dn, ot3=ot3: e.tensor_tensor(
                          out=ot3, in0=o3[:, :, 0:64], in1=dn.unsqueeze(2).broadcast_to([128, 4, 64]), op=ALU.mult),
                          R=[BK[bo], den[hh]], W=[otmp[hh]])
                      c0g = qi * 512 + hh * 256
                      agb = ag2[p2][hh]
                      cx.op('pool', lambda e, hh=hh, c0g=c0g, agb=agb: e.tensor_tensor(
                          out=agb.ap, in0=otmp[hh].ap, in1=gateA.ap[:, j - 1, c0g:c0g + 256], op=ALU.mult),
                          R=[otmp[hh], gateA], W=[agb])

              def att_s3(qi, j, p2):
                  for hh in range(2):
                      agb = ag2[p2][hh]
                      bt2 = nbank()
                      for t in range(2):
                          cx.op('pe', lambda e, t=t, bt2=bt2, agb=agb: e.transpose(
                              out=bank_bf[bt2][:, t * 128:(t + 1) * 128], in_=agb.ap[:, t * 128:(t + 1) * 128],
                              identity=ident_bf), R=[agb, cst_bf], W=[BK[bt2]])
                      kc0 = qi * 4 + hh * 2
                      cx.op('dve', lambda e, bt2=bt2, kc0=kc0: e.tensor_copy(
                          out=mixT.ap[:, kc0:kc0 + 2, (j - 1) * 128:j * 128],
                          in_=bank_bf[bt2][:, 0:256].rearrange("p (t n) -> p t n", n=128)),
                          R=[BK[bt2]], W=[mixc[kc0], mixc[kc0 + 1]])

              its = [(qi, j) for qi in range(2) for j in range(1, NTS)]
              NI = len(its)
              for step in range(NI + 2):
                  if step < NI:
                      att_s1(its[step][0], its[step][1], step % 2)
                  if 1 <= step <= NI:
                      att_s2(its[step - 1][0], its[step - 1][1], (step - 1) % 2)
                  if step >= 2:
                      att_s3(its[step - 2][0], its[step - 2][1], (step - 2) % 2)

              _DUMPS.clear()
              _DUMPS['mixT'] = mixT
              _chk('P3')
              b4 = Bump(O_DG, ARENA)
              gpost = b4([D], F32)
              bout = b4([D], F32)
              b4b = Bump(O_YT, O_MIX)
              xt4 = [b4b([D], F32) for _ in range(2)]
              yf = [b4b([D], F32) for _ in range(2)]
              junk4 = b4b([D], BF16)
              assert b4b.p <= O_YT + 36864
              cx.dma('sp', gpost.ap, gpost_d, s_gpost, W=[gpost])
              cx.dma('sp', bout.ap, bout_d, s_bout, W=[bout])
              for j in range(8):
                  b = j % 2
                  r_in = row0 + HALO + j * 128
                  r_out = s * SUP + j * 128
                  if j < 2:
                      cx.dma('sp', xt4[b].ap, x_ext[r_in:r_in + 128, :], s_x4[b], W=[xt4[b]])
                  if j == 4 and s + 1 < NSUP:
                      nrow0 = (s + 1) * SUP
                      cx.dma('sp', gpre.ap, gpre_d, s_gpre, W=[gpre])
                      cx.dma('sp', xt[0].ap, x_ext[nrow0: nrow0 + 128, :], s_x[0], W=[xt[0]])
                      cx.dma('sp', xt[1].ap, x_ext[nrow0 + 128: nrow0 + 256, :], s_x[1], W=[xt[1]])
                      pre_issued["p1"] = True
                  for nb in range(4):
                      by = nbank()
                      for k in range(16):
                          cx.op('pe', lambda e, k=k, by=by, j=j, woc=wo_ch[nb]: e.matmul(
                              out=bank[by], lhsT=mixT.ap[:, k, j * 128:(j + 1) * 128],
                              rhs=woc.ap[:, k, :], start=(k == 0), stop=(k == 15)),
                              R=[wo_ch[nb], mixT], W=[BK[by]])
                      cx.op('dve', lambda e, by=by, nb=nb, b=b: e.tensor_tensor(
                          out=yf[b].ap[:, nb * 512:(nb + 1) * 512], in0=bank[by],
                          in1=bout.ap[:, nb * 512:(nb + 1) * 512], op=ALU.add), R=[BK[by], bout], W=[yf[b]])
                  sm = smb[b]
                  ssb = sm.ap[:, 2:3]
                  rsb = sm.ap[:, 3:4]
                  cx.op('act', lambda e, b=b, ssb=ssb: e.activation(out=junk4.ap, in_=yf[b].ap, func=AF.Square,
                                                                    accum_out=ssb), R=[yf[b]], W=[junk4, sm])
                  cx.op('act', lambda e, ssb=ssb, rsb=rsb: e.activation(out=rsb, in_=ssb, func=AF.Sqrt,
                                                                        scale=1.0 / D, bias=RMS_EPS), R=[sm], W=[sm])
                  cx.op('dve', lambda e, rsb=rsb: e.reciprocal(out=rsb, in_=rsb), R=[sm], W=[sm])
                  cx.op('dve', lambda e, b=b, rsb=rsb: e.scalar_tensor_tensor(
                      out=yf[b].ap, in0=yf[b].ap, scalar=rsb, in1=gpost.ap, op0=ALU.mult, op1=ALU.mult),
                      R=[yf[b], sm, gpost], W=[yf[b]])
                  cx.op('dve', lambda e, b=b: e.tensor_tensor(out=xt4[b].ap, in0=yf[b].ap, in1=xt4[b].ap, op=ALU.add),
                        R=[yf[b], xt4[b]], W=[xt4[b]])
                  cx.dma('sp', out_d[r_out:r_out + 128, :], xt4[b].ap, s_o[b], R=[xt4[b]])
                  if j + 2 < 8:
                      r_n = row0 + HALO + (j + 2) * 128
                      cx.dma('sp', xt4[b].ap, x_ext[r_n:r_n + 128, :], s_x4[b], W=[xt4[b]])

        except _Stop:
            pass
        if STOP is not None:
            s_dbg = cx.dsem("dbg")
            for nm, b in _DUMPS.items():
                shp = list(b.ap.shape)
                dd = nc.dram_tensor("dbg_" + nm, shp, b.ap.dtype, kind="ExternalOutput").ap()
                cx.dma('sp', dd, b.ap, s_dbg, R=[b])
            cx.finish([s_dbg])
        else:
            cx.finish(s_o)
    return nc


_PROGRAM = None


def _host_inputs(x, positions, pre_norm_g, w_in, b_in, sinks, w_dw, b_dw, conv_ln_g, conv_ln_b,
                 w_pw, b_pw, w_out, b_out, post_norm_g):
    f32 = np.float32
    x2 = np.asarray(x, f32).reshape(SEQ, D)
    pos = np.asarray(positions).reshape(SEQ).astype(np.int32)
    w_in0 = np.asarray(w_in, f32)[0]
    b_in0 = np.asarray(b_in, f32)[0]
    cols = list(range(0, 2304))
    for c in range(8):
        cols += list(range(2304 + c * 128, 2304 + (c + 1) * 128))
        cols += list(range(3328 + c * 128, 3328 + (c + 1) * 128))
    cols += list(range(4352, 5376))
    w_in_p = np.ascontiguousarray(w_in0[:, np.array(cols)])
    w_out0 = np.ascontiguousarray(np.asarray(w_out, f32)[0])
    w_pw0 = np.ascontiguousarray(np.asarray(w_pw, f32)[0])

    def bc(v, n):
        return np.ascontiguousarray(np.broadcast_to(np.asarray(v, f32).reshape(1, n), (128, n)))

    gpre = bc(np.asarray(pre_norm_g)[0], D)
    gpost = bc(np.asarray(post_norm_g)[0], D)
    boutb = bc(np.asarray(b_out)[0], D)
    bintok = bc(b_in0[0:2304], 2304)

    def fm(v):
        v = np.asarray(v, f32)
        return np.ascontiguousarray(v.reshape(-1, 128).T)

    cpack = np.zeros((128, NCP), f32)
    cpack[:, C_BFM:C_BFM + 24] = fm(b_in0[2304:5376])
    cpack[:, C_BDW:C_BDW + 8] = fm(np.asarray(b_dw)[0])
    cpack[:, C_LNG:C_LNG + 8] = fm(np.asarray(conv_ln_g)[0])
    cpack[:, C_LNB:C_LNB + 8] = fm(np.asarray(conv_ln_b)[0])
    cpack[:, C_BPW:C_BPW + 8] = fm(np.asarray(b_pw)[0])
    wdw = np.asarray(w_dw, f32)[0]
    wpad = np.concatenate([np.zeros((1, 1024), f32), wdw], axis=0)
    wpk = wpad.reshape(8, 4, 8, 4, 32)
    cpack[:, C_WDW:C_WDW + 256] = wpk.transpose(1, 4, 2, 0, 3).reshape(128, 256)
    cpack[:, C_EYE4:C_EYE4 + 32] = np.tile(np.eye(32, dtype=f32), (4, 1))
    cpack[:, C_SINK:C_SINK + 16] = bc(np.asarray(sinks)[0], 16)
    invf = (np.float32(500000.0) ** (-(np.arange(0, 16, 2, dtype=np.float32)) / np.float32(16))).astype(f32)
    cpack[:, C_INVF:C_INVF + 8] = bc(invf, 8)
    cpack[:, C_ID:C_ID + 128] = np.eye(128, dtype=f32)
    kk = np.arange(128)[:, None]
    qq = np.arange(128)[None, :]
    cpack[:, C_MCUR:C_MCUR + 128] = (qq >= kk).astype(f32)
    cpack[:, C_MPREV:C_MPREV + 128] = (kk > qq).astype(f32)

    in_maps = []
    for c in range(NCORES):
        t0 = c * TPC
        xe = np.zeros((TPC + HALO, D), f32)
        pe = np.zeros((TPC + HALO,), np.int32)
        if c > 0:
            xe[:] = x2[t0 - HALO:t0 + TPC]
            pe[:] = pos[t0 - HALO:t0 + TPC]
        else:
            xe[HALO:] = x2[0:TPC]
            pe[HALO:] = pos[0:TPC]
        cpk = cpack.copy()
        cpk[:, C_FLAG] = 0.0 if c == 0 else 1.0
        cpk[:, C_MPREVF:C_MPREVF + 128] = 0.0 if c == 0 else cpack[:, C_MPREV:C_MPREV + 128]
        in_maps.append({
            "x_ext": xe, "pos_t": np.ascontiguousarray(pe.reshape(17, 128).T),
            "w_in": w_in_p, "w_out": w_out0, "w_pw": w_pw0, "cpack": cpk,
            "gpre_bc": gpre, "gpost_bc": gpost, "bout_bc": boutb, "bintok_bc": bintok,
        })
    return in_maps


def kernel(**inputs):
    global _PROGRAM
    in_maps = _host_inputs(**inputs)
    if _PROGRAM is None:
        _PROGRAM = build_program()
    res = run_bass_kernel_spmd(_PROGRAM, in_maps, core_ids=list(range(NCORES)))
    out = np.concatenate([np.asarray(r["out"], np.float32) for r in res.results], axis=0)
    return out.reshape(1, SEQ, D)
```

```python
import contextlib
import numpy as np
import concourse.bass as bass
import concourse.mybir as mybir
from concourse.bass_utils import run_bass_kernel_spmd

F32 = mybir.dt.float32
BF16 = mybir.dt.bfloat16
I32 = mybir.dt.int32
U8 = mybir.dt.uint8
ALU = mybir.AluOpType
AF = mybir.ActivationFunctionType

SELF_SYNC = {'pe': False, 'act': True, 'dve': True, 'pool': True, 'sp': False}


class T:
    __slots__ = ("name", "w", "r")

    def __init__(self, name):
        self.name = name
        self.w = None
        self.r = {}


class _DSem:
    def __init__(self, sem, name):
        self.sem = sem
        self.count = 0
        self.name = name


class _Op:
    __slots__ = ("eng", "fn", "seq", "waits", "marked", "dma", "rank", "kn", "gseq")

    def __init__(self, eng, fn, seq):
        self.eng = eng
        self.fn = fn
        self.seq = seq
        self.waits = []
        self.marked = False
        self.dma = None
        self.rank = 0
        self.kn = None


class Ctx:
    ENGS = ('pe', 'act', 'dve', 'pool', 'sp')

    def __init__(self, nc):
        self.nc = nc
        self.ops = {e: [] for e in self.ENGS}
        self.known = {e: {} for e in self.ENGS}
        self.stack = contextlib.ExitStack()
        self.esem = {}
        self.dsems = []

    def __enter__(self):
        self.stack.__enter__()
        for e in self.ENGS:
            self.esem[e] = self.stack.enter_context(self.nc.semaphore("es_" + e))
        return self

    def __exit__(self, *a):
        return self.stack.__exit__(*a)

    def sb(self, name, shape, dtype):
        h = self.stack.enter_context(self.nc.sbuf_tensor(name, list(shape), dtype))
        return h[:, :] if len(shape) == 2 else h[:]

    def ps(self, name, shape, dtype):
        h = self.stack.enter_context(self.nc.psum_tensor(name, list(shape), dtype))
        return h[:, :] if len(shape) == 2 else h[:]

    def dsem(self, name):
        s = self.stack.enter_context(self.nc.semaphore("ds_" + name))
        d = _DSem(s, name)
        self.dsems.append(d)
        return d

    def constcol(self, v):
        return float(v)

    @staticmethod
    def _flat(L):
        out = []
        for t in L:
            if isinstance(t, T):
                out.append(t)
            else:
                out.extend(t.ts)
        return out

    def _emit(self, E, fn, R, W, dma=None, nowait=False, group=False):
        R = self._flat(R)
        W = self._flat(W)
        deps = []
        if not nowait:
            for t in R:
                if t.w is not None:
                    deps.append(t.w)
            for t in W:
                if t.w is not None:
                    deps.append(t.w)
                deps.extend(t.r.values())
        op = _Op(E, fn, len(self.ops[E]))
        kn = self.known[E]
        deps.sort(key=lambda d: -(self.ops[d[1]][d[2]].gseq if d[0] == 'eng' else -1))
        for d in deps:
            if d[0] == 'eng':
                _, Fe, seq = d
                if Fe == E and not SELF_SYNC[E]:
                    continue
                key = ('e', Fe)
                if kn.get(key, -1) >= seq:
                    continue
                kn[key] = seq
                op.waits.append(d)
                self.ops[Fe][seq].marked = True
                for k2, v2 in self.ops[Fe][seq].kn.items():
                    if kn.get(k2, -1) < v2:
                        kn[k2] = v2
            else:
                _, ds, val = d
                if group and ds is dma:
                    continue
                key = ('d', id(ds))
                if kn.get(key, -1) >= val:
                    continue
                kn[key] = val
                op.waits.append(d)
        op.kn = dict(kn)
        self.gcount = getattr(self, 'gcount', 0) + 1
        op.gseq = self.gcount
        self.ops[E].append(op)
        if dma is None:
            tok = ('eng', E, op.seq)
            rkey = E
        else:
            dma.count += 16
            tok = ('dma', dma, dma.count)
            op.dma = dma
            rkey = ('dma', id(dma))
        for t in R:
            t.r[rkey] = tok
        for t in W:
            t.w = tok
            t.r = {}
        return op

    def op(self, E, fn, R=(), W=()):
        return self._emit(E, fn, R, W)

    def dma(self, E, out, in_, ds, R=(), W=(), nowait=False, group=False, **kw):
        return self._emit(E, lambda e: e.dma_start(out=out, in_=in_, **kw), R, W, dma=ds, nowait=nowait, group=group)

    def finish(self, out_sems):
        nc = self.nc
        op = _Op('sp', None, len(self.ops['sp']))
        for ds in out_sems:
            op.waits.append(('dma', ds, ds.count))
        self.ops['sp'].append(op)
        for e in self.ENGS:
            r = 0
            for o in self.ops[e]:
                if o.marked:
                    r += 1
                    o.rank = r
        ops = self.ops
        esem = self.esem

        def replay(eng_name):
            def run(e):
                for o in ops[eng_name]:
                    waits = o.waits
                    attach = None
                    if eng_name in ('dve', 'pool') and o.fn is not None and o.dma is None and waits:
                        attach = waits[-1]
                        waits = waits[:-1]
                    for d in waits:
                        if d[0] == 'eng':
                            e.wait_ge(esem[d[1]], ops[d[1]][d[2]].rank)
                        else:
                            e.wait_ge(d[1].sem, d[2])
                    if o.fn is None:
                        continue
                    ins = o.fn(e)
                    if attach is not None:
                        if attach[0] == 'eng':
                            ins._wait_ge(esem[attach[1]], ops[attach[1]][attach[2]].rank)
                        else:
                            ins._wait_ge(attach[1].sem, attach[2])
                    if o.dma is not None:
                        ins.then_inc(o.dma.sem, 16)
                    elif o.marked:
                        ins.then_inc(esem[eng_name], 1)
            return run

        with nc.Block() as block:
            block.tensor(replay('pe'))
            block.scalar(replay('act'))
            block.vector(replay('dve'))
            block.gpsimd(replay('pool'))
            block.sync(replay('sp'))


D = 2048
SEQ = 16384
NCORES = 8
TPC = SEQ // NCORES
HALO = 128
SUP = 1024
NSUP = TPC // SUP
NTS = SUP // 128 + 1
TOKS = SUP + HALO
INC = 5376
RMS_EPS = 1e-6
LN_EPS = 1e-5
KTAP = 31
PI = float(np.pi)
BLK = 512

C_BFM = 0
C_BDW = 24
C_LNG = 32
C_LNB = 40
C_BPW = 48
C_FLAG = 56
C_WDW = 57
C_SINK = C_WDW + 8 * 32
C_INVF = C_SINK + 16
C_ID = C_INVF + 8
C_MCUR = C_ID + 128
C_MPREV = C_MCUR + 128
C_MPREVF = C_MPREV + 128
C_EYE4 = C_MPREVF + 128
NCP = C_EYE4 + 32


class Buf:
    __slots__ = ("ap", "ts", "off", "nbytes")

    def __init__(self, ap, ts, off, nbytes):
        self.ap = ap
        self.ts = ts
        self.off = off
        self.nbytes = nbytes


def _esz(dt):
    return 2 if dt == BF16 else (1 if dt == U8 else 4)


STOP = None


class _Stop(Exception):
    pass


_DUMPS = {}


def _chk(name):
    if STOP == name:
        raise _Stop()


def build_program():
    nc = bass.Bass("TRN2", target_bir_lowering=False)
    x_ext = nc.dram_tensor("x_ext", [TPC + HALO, D], F32, kind="ExternalInput").ap()
    pos_t = nc.dram_tensor("pos_t", [128, 17], I32, kind="ExternalInput").ap()
    w_in = nc.dram_tensor("w_in", [D, INC], F32, kind="ExternalInput").ap()
    w_out = nc.dram_tensor("w_out", [D, D], F32, kind="ExternalInput").ap()
    w_pw = nc.dram_tensor("w_pw", [1024, 1024], F32, kind="ExternalInput").ap()
    cpack_d = nc.dram_tensor("cpack", [128, NCP], F32, kind="ExternalInput").ap()
    gpre_d = nc.dram_tensor("gpre_bc", [128, D], F32, kind="ExternalInput").ap()
    gpost_d = nc.dram_tensor("gpost_bc", [128, D], F32, kind="ExternalInput").ap()
    bout_d = nc.dram_tensor("bout_bc", [128, D], F32, kind="ExternalInput").ap()
    bintok_d = nc.dram_tensor("bintok_bc", [128, 2304], F32, kind="ExternalInput").ap()
    out_d = nc.dram_tensor("out", [TPC, D], F32, kind="ExternalOutput").ap()

    w_in_v = w_in.rearrange("(k p) n -> p k n", p=128)
    w_out_v = w_out.rearrange("(k p) n -> p k n", p=128)
    w_pw_v = w_pw.rearrange("(k p) n -> p k n", p=128)

    cx = Ctx(nc)
    with cx:
        PERS = 12288
        O_XT = PERS
        O_YT = O_XT + 36864
        O_ZT = O_YT + 32768
        O_MIX = O_ZT + 16384
        O_WB = O_MIX + 32768
        O_DG = O_WB + 32768
        O_RS = O_DG + 4096
        O_HG = O_RS + 9216
        O_WORK = O_HG + 5120
        ARENA = O_WORK + 20480
        assert ARENA % BLK == 0 and O_WORK % BLK == 0
        arena = cx.sb("arena", [128, ARENA], U8)
        BT = [T(f"blk{i}") for i in range(ARENA // BLK)]

        def mk(off, shape, dt):
            n = int(np.prod(shape))
            nb = n * _esz(dt)
            assert off % 4 == 0 and off + nb <= ARENA, (off, nb)
            ap = arena[:, off:off + nb].bitcast(dt)
            if len(shape) == 2:
                ap = ap.rearrange("p (a b) -> p a b", b=shape[1])
            elif len(shape) == 3:
                ap = ap.rearrange("p (a b c) -> p a b c", b=shape[1], c=shape[2])
            ts = BT[off // BLK:(off + nb - 1) // BLK + 1]
            return Buf(ap, ts, off, nb)

        class Bump:
            def __init__(self, start, end):
                self.p = start
                self.end = end

            def __call__(self, shape, dt, align=BLK):
                self.p = (self.p + align - 1) // align * align
                b = mk(self.p, shape, dt)
                self.p += b.nbytes
                assert self.p <= self.end, (self.p, self.end)
                return b

        bp = Bump(0, PERS)
        cpack = bp([NCP], F32)
        cst_bf = bp([5, 128], BF16)
        ident_bf = cst_bf.ap[:, 0, :]
        mcur_bf = cst_bf.ap[:, 1, :]
        mprev_bf = cst_bf.ap[:, 2, :]
        mprevf_bf = cst_bf.ap[:, 3, :]
        ones_bf = cst_bf.ap[:, 4, :]
        expsink = bp([16], F32, align=64)
        pos_i = bp([17], I32, align=64)
        pos_f = bp([17], F32, align=64)
        cs = bp([NTS, 16], F32, align=BLK)
        ang = bp([NTS, 8], F32, align=64)
        rr_u = bp([NTS, 8], F32, align=64)
        rr_k = bp([NTS, 8], I32, align=64)
        rr_kf = bp([NTS, 8], F32, align=64)
        smb = [bp([16], F32, align=BLK) for _ in range(2)]
        mbias = bp([3, 4, 128], BF16)
        cp = cpack.ap

        bank = [cx.ps(f"bank{i}", [128, 512], F32) for i in range(8)]
        bank_bf = [b.bitcast(BF16) for b in bank]
        BK = [T(f"bk{i}") for i in range(8)]
        rot = {"list": list(range(8)), "i": 0}

        def nbank():
            L = rot["list"]
            b = L[rot["i"] % len(L)]
            rot["i"] += 1
            return b

        s_c = cx.dsem("const")
        s_c2 = cx.dsem("const2")
        s_gpre = cx.dsem("gpre")
        s_bintok = cx.dsem("bintok")
        s_gpost = cx.dsem("gpost")
        s_bout = cx.dsem("bout")
        s_wo1 = cx.dsem("wout1")
        NXB = 5
        s_x = [cx.dsem("x%d" % i) for i in range(NXB)]
        s_wb = [cx.dsem("wb0"), cx.dsem("wb1")]
        s_m = cx.dsem("misc")
        s_o = [cx.dsem("o0"), cx.dsem("o1")]
        s_x4 = [cx.dsem("x40"), cx.dsem("x41")]
        s_wo = cx.dsem("wout")
        s_rs = [cx.dsem("rs0"), cx.dsem("rs1")]

        cx.dma('sp', cp, cpack_d, s_c, W=[cpack], nowait=True)
        cx.dma('sp', pos_i.ap, pos_t, s_c2, W=[pos_i], nowait=True)
        cx.op('dve', lambda e: e.tensor_copy(out=cst_bf.ap[:, 0:4, :],
                                             in_=cp[:, C_ID:C_ID + 512].rearrange("p (a b) -> p a b", b=128)),
              R=[cpack], W=[cst_bf])
        cx.op('dve', lambda e: e.memset(ones_bf, 1.0 / 1024), W=[cst_bf])
        cx.op('dve', lambda e: e.tensor_scalar(
            out=mbias.ap, in0=cp[:, C_MCUR:C_MCUR + 384].rearrange("p (a b) -> p a b", b=128).unsqueeze(2)
            .broadcast_to([128, 3, 4, 128]), scalar1=-1.0, scalar2=30000.0, op0=ALU.add, op1=ALU.mult),
            R=[cpack], W=[mbias])
        cx.op('act', lambda e: e.activation(out=expsink.ap, in_=cp[:, C_SINK:C_SINK + 16], func=AF.Exp),
              R=[cpack], W=[expsink])
        cx.op('dve', lambda e: e.tensor_copy(out=pos_f.ap, in_=pos_i.ap), R=[pos_i], W=[pos_f])

        WBUF = [mk(O_WB + i * 16384, [16, 512], BF16) for i in range(2)]
        wb_state = {"n": 0}
        pre_issued = {"p1": False}

        try:
          for s in range(NSUP):
              row0 = s * SUP
              xT = mk(O_XT, [16, TOKS], BF16)
              b1 = Bump(O_YT, O_MIX)
              b1p = Bump(O_DG + 16384, ARENA)
              gpre = b1p([D], F32)
              xt2_ = mk(O_YT, [D], F32)
              xn = [mk(O_YT + 16384, [D], BF16), mk(O_YT + 20480, [D], BF16)]
              junk = mk(O_YT + 32768, [D], BF16)
              xt = [b1p([D], F32), mk(O_YT + 36864, [D], F32), xt2_, mk(O_YT + 8192, [D], F32), mk(O_YT + 24576, [D], F32)]
              if not pre_issued["p1"]:
                  cx.dma('sp', gpre.ap, gpre_d, s_gpre, W=[gpre])
              def p1_a(j):
                  b = j % 2
                  xb = xt[j % NXB]
                  if not (j <= 1 and pre_issued["p1"]):
                      cx.dma('sp', xb.ap, x_ext[row0 + j * 128: row0 + (j + 1) * 128, :], s_x[j % NXB], W=[xb])
                  sm = smb[b]
                  ssb = sm.ap[:, 0:1]
                  rsb = sm.ap[:, 1:2]
                  cx.op('act', lambda e: e.activation(out=junk.ap, in_=xb.ap, func=AF.Square, accum_out=ssb),
                        R=[xb], W=[junk, sm])
                  cx.op('act', lambda e: e.activation(out=rsb, in_=ssb, func=AF.Sqrt, scale=1.0 / D, bias=RMS_EPS),
                        R=[sm], W=[sm])
                  cx.op('dve', lambda e: e.reciprocal(out=rsb, in_=rsb), R=[sm], W=[sm])
                  cx.op('dve', lambda e: e.scalar_tensor_tensor(
                      out=xn[b].ap, in0=xb.ap, scalar=rsb, in1=gpre.ap, op0=ALU.mult, op1=ALU.mult),
                      R=[xb, sm, gpre], W=[xn[b]])

              def p1_b(j):
                  b = j % 2
                  bA, bB = nbank(), nbank()
                  for k in range(16):
                      bb = bA if k < 8 else bB
                      cx.op('pe', lambda e, k=k, bb=bb: e.transpose(
                          out=bank_bf[bb][:, (k % 8) * 128:(k % 8 + 1) * 128],
                          in_=xn[b].ap[:, k * 128:(k + 1) * 128], identity=ident_bf),
                          R=[xn[b], cst_bf], W=[BK[bb]])
                  cx.op('act', lambda e: e.copy(
                      out=xT.ap[:, 0:8, j * 128:(j + 1) * 128],
                      in_=bank_bf[bA].rearrange("p (k n) -> p k n", n=128)), R=[BK[bA]], W=[xT])
                  cx.op('dve', lambda e: e.tensor_copy(
                      out=xT.ap[:, 8:16, j * 128:(j + 1) * 128],
                      in_=bank_bf[bB].rearrange("p (k n) -> p k n", n=128)), R=[BK[bB]], W=[xT])

              p1_a(0)
              p1_a(1)
              for j in range(NTS):
                  p1_b(j)
                  if j + 2 < NTS:
                      p1_a(j + 2)

              c0 = s * 8
              cx.op('dve', lambda e, c0=c0: e.tensor_tensor(
                  out=ang.ap, in0=pos_f.ap[:, c0:c0 + NTS].unsqueeze(2).broadcast_to([128, NTS, 8]),
                  in1=cp[:, C_INVF:C_INVF + 8].unsqueeze(1).broadcast_to([128, NTS, 8]), op=ALU.mult),
                  R=[pos_f, cpack], W=[ang])
              C1 = 6.28125
              C2 = float(2 * np.pi - 6.28125)
              for which in range(2):
                  shift = 0.25 if which == 0 else 0.0
                  lo, hi = (-1.5 * PI, 0.5 * PI) if which == 0 else (-PI, PI)
                  cx.op('dve', lambda e, shift=shift: e.tensor_scalar(
                      out=rr_u.ap, in0=ang.ap, scalar1=float(1.0 / (2 * np.pi)), scalar2=shift,
                      op0=ALU.mult, op1=ALU.add), R=[ang], W=[rr_u])
                  cx.op('dve', lambda e: e.tensor_copy(out=rr_k.ap, in_=rr_u.ap), R=[rr_u], W=[rr_k])
                  cx.op('dve', lambda e: e.tensor_copy(out=rr_kf.ap, in_=rr_k.ap), R=[rr_k], W=[rr_kf])
                  cx.op('dve', lambda e: e.scalar_tensor_tensor(
                      out=rr_u.ap, in0=rr_kf.ap, scalar=-C1, in1=ang.ap, op0=ALU.mult, op1=ALU.add),
                      R=[rr_kf, ang], W=[rr_u])
                  cx.op('dve', lambda e: e.scalar_tensor_tensor(
                      out=rr_u.ap, in0=rr_kf.ap, scalar=-C2, in1=rr_u.ap, op0=ALU.mult, op1=ALU.add),
                      R=[rr_kf, rr_u], W=[rr_u])
                  cx.op('dve', lambda e, lo=lo, hi=hi: e.tensor_scalar(
                      out=rr_u.ap, in0=rr_u.ap, scalar1=float(lo), scalar2=float(hi), op0=ALU.max, op1=ALU.min),
                      R=[rr_u], W=[rr_u])
                  cx.op('act', lambda e, which=which: e.activation(
                      out=cs.ap[:, :, which * 8:(which + 1) * 8], in_=rr_u.ap, func=AF.Sin,
                      bias=(0.5 * PI if which == 0 else 0.0)), R=[rr_u], W=[cs])

              _DUMPS.clear()
              _DUMPS['xT'] = xT
              _DUMPS['cs'] = cs
              _chk('P1')
              def wload(col0, ncols, after=()):
                  n = wb_state["n"]
                  wb_state["n"] += 1
                  i = n % 2
                  cx.dma('pool', WBUF[i].ap[:, :, 0:ncols], w_in_v[:, :, col0:col0 + ncols], s_wb[i],
                         R=list(after), W=[WBUF[i]])
                  return WBUF[i]

              def wload_pw():
                  n = wb_state["n"]
                  wb_state["n"] += 1
                  i = n % 2
                  v = mk(O_WB + i * 16384, [8, 1024], BF16)
                  for h in range(2):
                      cx.dma('pool', v.ap[:, :, h * 512:(h + 1) * 512], w_pw_v[:, :, h * 512:(h + 1) * 512], s_wb[i],
                             W=[v])
                  return v

              yT = [mk(O_YT + c * 4096, [1024], F32) for c in range(8)]
              zT = [mk(O_ZT + c * 2048, [1024], BF16) for c in range(8)]
              mixT = mk(O_MIX, [16, SUP], BF16)
              mixc = [mk(O_MIX + k * 2048, [SUP], BF16) for k in range(16)]
              diag = [mk(O_DG + i * 2048, [8, 4, 32], BF16) for i in range(2)]
              rsh = [mk(O_RS, [4, TOKS], BF16) for i in range(2)]
              hglu = [mk(O_HG + i * 2560, [TOKS], BF16) for i in range(2)]
              bw = Bump(O_WORK, ARENA)
              sig = [bw([512], BF16, align=1024) for _ in range(2)]
              ybf = [bw([512], BF16, align=1024) for _ in range(2)]
              ysq = [bw([512], BF16, align=1024) for _ in range(2)]
              mean_sb = [bw([512], F32, align=2048) for _ in range(2)]
              rstd_sb = [bw([512], F32, align=2048) for _ in range(2)]
              tln = [bw([512], F32, align=2048) for _ in range(2)]
              valsb = tln
              _gofs = [O_DG, O_HG, O_DG + 2048, O_RS, O_RS + 2048, O_RS + 4096, O_RS + 6144, O_HG + 2560]
              gcs_all = [mk(_gofs[c], [1024], BF16) for c in range(8)]
              pending = []

              rot["list"] = [0, 1, 2, 3]
              mean_b = [4, 5]
              msq_b = [6, 7]
              GRP = [(96, 352), (448, 352), (800, 352)]
              cnt = {"sig": 0, "y": 0}

              def conv_inproj(c, wbuf):
                  cc = c % 2
                  hg = hglu[c % 2]
                  for (st, n) in GRP:
                      bv = nbank()
                      for k in range(16):
                          cx.op('pe', lambda e, k=k, bv=bv, st=st, n=n: e.matmul(
                              out=bank[bv][:, 0:n], lhsT=wbuf.ap[:, k, cc * 256:cc * 256 + 128],
                              rhs=xT.ap[:, k, st:st + n], start=(k == 0), stop=(k == 15)),
                              R=[wbuf, xT], W=[BK[bv]])
                      bg = nbank()
                      for k in range(16):
                          cx.op('pe', lambda e, k=k, bg=bg, st=st, n=n: e.matmul(
                              out=bank[bg][:, 0:n], lhsT=wbuf.ap[:, k, cc * 256 + 128:cc * 256 + 256],
                              rhs=xT.ap[:, k, st:st + n], start=(k == 0), stop=(k == 15)),
                              R=[wbuf, xT], W=[BK[bg]])
                      sg = sig[cnt["sig"] % 2]
                      cnt["sig"] += 1
                      cx.op('act', lambda e, bg=bg, n=n, sg=sg: e.activation(
                          out=sg.ap[:, 0:n], in_=bank[bg][:, 0:n], func=AF.Sigmoid,
                          bias=cp[:, C_BFM + 8 + c:C_BFM + 9 + c]), R=[BK[bg], cpack], W=[sg])
                      vs = valsb[(cnt["sig"] - 1) % 2]
                      cx.op('act', lambda e, bv=bv, n=n, vs=vs: e.activation(
                          out=vs.ap[:, 0:n], in_=bank[bv][:, 0:n], func=AF.Identity,
                          bias=cp[:, C_BFM + c:C_BFM + c + 1]), R=[BK[bv], cpack], W=[vs])
                      cx.op('dve', lambda e, n=n, st=st, sg=sg, vs=vs: e.tensor_tensor(
                          out=hg.ap[:, st:st + n], in0=vs.ap[:, 0:n], in1=sg.ap[:, 0:n], op=ALU.mult),
                          R=[vs, sg], W=[hg])
                  if s == 0:
                      cx.op('dve', lambda e: e.tensor_scalar(
                          out=hg.ap[:, 96:128], in0=hg.ap[:, 96:128], scalar1=cp[:, C_FLAG:C_FLAG + 1], scalar2=None,
                          op0=ALU.mult), R=[hg, cpack], W=[hg])
                  dg = diag[c % 2]
                  cx.op('dve', lambda e: e.tensor_tensor(
                      out=dg.ap, in0=cp[:, C_EYE4:C_EYE4 + 32].unsqueeze(1).unsqueeze(1).broadcast_to([128, 8, 4, 32]),
                      in1=cp[:, C_WDW + c * 32:C_WDW + (c + 1) * 32].rearrange("p (m g) -> p m g", g=4).unsqueeze(3)
                      .broadcast_to([128, 8, 4, 32]), op=ALU.mult), R=[cpack], W=[dg])
              def conv_shuffle(c):
                  hg = hglu[c % 2]
                  rs = rsh[c % 2]
                  for g in range(4):
                      for i in range(4):
                          cx.dma('sp', rs.ap[32 * i:32 * (i + 1), g, 96:TOKS - i], hg.ap[32 * g:32 * (g + 1), 96 + i:TOKS],
                                 s_rs[c % 2], R=[hg], W=[rs], group=True)

              def conv_taps(c):
                  hg = hglu[c % 2]
                  dg = diag[c % 2]
                  rs = rsh[c % 2]
                  for t2 in range(2):
                      by = nbank()
                      for m in range(8):
                          o = HALO + t2 * 512 - KTAP + 4 * m
                          for g in range(4):
                              cx.op('pe', lambda e, m=m, g=g, by=by, o=o: e.matmul(
                                  out=bank[by][32 * g:32 * (g + 1), :], lhsT=dg.ap[:, m, g, :],
                                  rhs=rs.ap[:, g, o:o + 512], start=(m == 0), stop=(m == 7),
                                  tile_position=(0, 32 * g)), R=[dg, rs], W=[BK[by]])
                      ysl = yT[c].ap[:, t2 * 512:(t2 + 1) * 512]
                      cx.op('act', lambda e, by=by, ysl=ysl: e.activation(
                          out=ysl, in_=bank[by], func=AF.Identity, bias=cp[:, C_BDW + c:C_BDW + c + 1]),
                          R=[BK[by], cpack], W=[yT[c]])
                      yb = ybf[cnt["y"] % 2]
                      yq = ysq[cnt["y"] % 2]
                      cnt["y"] += 1
                      cx.op('dve', lambda e, ysl=ysl, yb=yb: e.tensor_copy(out=yb.ap, in_=ysl), R=[yT[c]], W=[yb])
                      cx.op('act', lambda e, ysl=ysl, yq=yq: e.activation(out=yq.ap, in_=ysl, func=AF.Square),
                            R=[yT[c]], W=[yq])
                      def _stats(t2=t2, yb=yb, yq=yq, c=c):
                          cx.op('pe', lambda e: e.matmul(
                              out=bank[mean_b[t2]], lhsT=ones_bf, rhs=yb.ap, start=(c == 0), stop=(c == 7)),
                              R=[yb, cst_bf], W=[BK[mean_b[t2]]])
                          cx.op('pe', lambda e: e.matmul(
                              out=bank[msq_b[t2]], lhsT=ones_bf, rhs=yq.ap, start=(c == 0), stop=(c == 7)),
                              R=[yq, cst_bf], W=[BK[msq_b[t2]]])
                      pending.append(_stats)

              wgc = [None, None]

              def gconv(c):
                  wbuf = wgc[c // 4]
                  for t2 in range(2):
                      bgc = nbank()
                      for k in range(16):
                          cx.op('pe', lambda e, k=k, bgc=bgc, t2=t2: e.matmul(
                              out=bank[bgc], lhsT=wbuf.ap[:, k, (c % 4) * 128:(c % 4 + 1) * 128],
                              rhs=xT.ap[:, k, HALO + t2 * 512:HALO + (t2 + 1) * 512], start=(k == 0), stop=(k == 15)),
                              R=[wbuf, xT], W=[BK[bgc]])
                      cx.op('act', lambda e, bgc=bgc, t2=t2: e.activation(
                          out=gcs_all[c].ap[:, t2 * 512:(t2 + 1) * 512], in_=bank[bgc], func=AF.Silu,
                          bias=cp[:, C_BFM + 16 + c:C_BFM + 17 + c]), R=[BK[bgc], cpack], W=[gcs_all[c]])

              wq = [wload(2304, 512), wload(2304 + 512, 512, after=[xt[(NTS - 1) % NXB]])]
              for c in range(9):
                  if c < 8:
                      conv_inproj(c, wq[(c // 2) % 2])
                      if c % 2 == 1 and c // 2 + 2 < 4:
                          wq[(c // 2) % 2] = wload(2304 + (c // 2 + 2) * 512, 512)
                      if c == 5:
                          wgc[0] = wload(4352, 512)
                      if c == 7:
                          wgc[1] = wload(4352 + 512, 512)
                  for f_ in pending:
                      f_()
                  del pending[:]
                  if c == 8:
                      gconv(0)
                      gconv(1)
                  if c >= 1:
                      conv_taps(c - 1)
                  if c < 8:
                      conv_shuffle(c)
              for f_ in pending:
                  f_()
              del pending[:]

              for c in range(8):
                  _DUMPS['yT%d' % c] = yT[c]
              _DUMPS['hglu1'] = hglu[1]
              _chk('P2a')
              for t2 in range(2):
                  cx.op('act', lambda e, t2=t2: e.copy(out=mean_sb[t2].ap, in_=bank[mean_b[t2]]),
                        R=[BK[mean_b[t2]]], W=[mean_sb[t2]])
                  cx.op('dve', lambda e, t2=t2: e.tensor_tensor(out=tln[t2].ap, in0=mean_sb[t2].ap, in1=mean_sb[t2].ap,
                                                               op=ALU.mult), R=[mean_sb[t2]], W=[tln[t2]])
                  cx.op('dve', lambda e, t2=t2: e.tensor_tensor(out=rstd_sb[t2].ap, in0=bank[msq_b[t2]], in1=tln[t2].ap,
                                                               op=ALU.subtract), R=[BK[msq_b[t2]], tln[t2]], W=[rstd_sb[t2]])
              for t2 in range(2):
                  cx.op('act', lambda e, t2=t2: e.activation(out=rstd_sb[t2].ap, in_=rstd_sb[t2].ap, func=AF.Sqrt,
                                                            bias=LN_EPS), R=[rstd_sb[t2]], W=[rstd_sb[t2]])
              for t2 in range(2):
                  cx.op('dve', lambda e, t2=t2: e.reciprocal(out=rstd_sb[t2].ap, in_=rstd_sb[t2].ap),
                        R=[rstd_sb[t2]], W=[rstd_sb[t2]])
              rot["list"] = list(range(8))
              wpw = None
              wkv = None
              tln4 = [tln[0], tln[1], mk(ybf[0].off, [512], F32), mk(ybf[0].off + 2048, [512], F32)]

              def ln_chunk(c):
                  for t2 in range(2):
                      tb = tln4[(c % 2) * 2 + t2]
                      ysl = yT[c].ap[:, t2 * 512:(t2 + 1) * 512]
                      cx.op('dve', lambda e, ysl=ysl, t2=t2, tb=tb: e.tensor_tensor(
                          out=tb.ap, in0=ysl, in1=mean_sb[t2].ap, op=ALU.subtract),
                          R=[yT[c], mean_sb[t2]], W=[tb])
                      cx.op('dve', lambda e, t2=t2, tb=tb: e.tensor_tensor(
                          out=tb.ap, in0=tb.ap, in1=rstd_sb[t2].ap, op=ALU.mult),
                          R=[tb, rstd_sb[t2]], W=[tb])
                      cx.op('act', lambda e, t2=t2, tb=tb: e.activation(
                          out=zT[c].ap[:, t2 * 512:(t2 + 1) * 512], in_=tb.ap, func=AF.Silu,
                          scale=cp[:, C_LNG + c:C_LNG + c + 1], bias=cp[:, C_LNB + c:C_LNB + c + 1]),
                          R=[tb, cpack], W=[zT[c]])

              ln_chunk(0)
              ln_chunk(1)
              for c in range(8):
                  if c >= 2:
                      gconv(c)
                  if c == 7:
                      wkv = wload(1024, 256)
                  if c == 3:
                      wpw = wload_pw()
                  if c + 2 < 8:
                      ln_chunk(c + 2)

              _DUMPS.clear()
              for c in range(8):
                  _DUMPS['zT%d' % c] = zT[c]
              _chk('P2b')
              for c2 in range(8):
                  for t2 in range(2):
                      bpw = nbank()
                      for k in range(8):
                          cx.op('pe', lambda e, k=k, bpw=bpw, t2=t2, c2=c2, wpw=wpw: e.matmul(
                              out=bank[bpw], lhsT=wpw.ap[:, k, c2 * 128:(c2 + 1) * 128],
                              rhs=zT[k].ap[:, t2 * 512:(t2 + 1) * 512], start=(k == 0), stop=(k == 7)),
                              R=[wpw, zT[k]], W=[BK[bpw]])
                      cx.op('dve', lambda e, bpw=bpw, t2=t2, c2=c2: e.scalar_tensor_tensor(
                          out=mixT.ap[:, 8 + c2, t2 * 512:(t2 + 1) * 512], in0=bank[bpw],
                          scalar=cp[:, C_BPW + c2:C_BPW + c2 + 1], in1=gcs_all[c2].ap[:, t2 * 512:(t2 + 1) * 512],
                          op0=ALU.add, op1=ALU.mult),
                          R=[BK[bpw], gcs_all[c2], cpack], W=[mixc[8 + c2]])

              _DUMPS.clear()
              _DUMPS['mixT'] = mixT
              _chk('P2c')
              wga = [wload(1280, 512), None]
              b3 = Bump(O_ZT, O_MIX)
              kTd = b3([2, 2, TOKS], BF16)
              vaug = b3([NTS, 2, 65], BF16)
              b3b = Bump(O_YT + 16384, O_ZT)
              gateA = b3b([8, 1024], BF16)
              bw = Bump(O_DG, ARENA)
              qf = [bw([512], F32, align=2048) for _ in range(2)]
              kvf = [bw([256], F32, align=1024) for _ in range(3)]
              kb2 = [bw([2, 2, 128], BF16, align=1024) for _ in range(3)]
              qb = [bw([512], BF16, align=1024) for _ in range(2)]
              PT = [[bw([512], BF16, align=1024) for _ in range(4)] for _ in range(2)]
              otmp = [bw([256], F32, align=1024) for _ in range(2)]
              bintok = bw([2304], F32)
              krot = [bw([2, 64], BF16, align=512) for _ in range(3)]
              rt1 = [bw([8, 2, 8], F32, align=512) for _ in range(2)]
              rt2 = [bw([8, 2, 8], F32, align=512) for _ in range(2)]
              den = [bw([8], F32, align=256) for _ in range(2)]
              ag2 = [[bw([256], BF16, align=512) for _ in range(2)] for _ in range(2)]
              gaf = qf

              cx.dma('sp', bintok.ap, bintok_d, s_bintok, W=[bintok])
              cx.op('pool', lambda e: e.memset(vaug.ap[:, :, :, 64:65], 1.0), W=[vaug])
              for i_ in range(3):
                  cx.op('pool', lambda e, i_=i_: e.memset(kb2[i_].ap, 0.0), W=[kb2[i_]])

              def rope(src3, dst3, H, j, r1, r2, RS_, RD_):
                  u = src3[:, :, 0:16].rearrange("p h (t e) -> p h t e", e=8)
                  cosb = cs.ap[:, j, 0:8].unsqueeze(1).unsqueeze(1).broadcast_to([128, H, 2, 8])
                  sinb = cs.ap[:, j, 8:16].unsqueeze(1).unsqueeze(1).broadcast_to([128, H, 2, 8])
                  a1 = r1.ap[:, 0:H, :, :]
                  a2 = r2.ap[:, 0:H, :, :]
                  cx.op('dve', lambda e: e.tensor_tensor(out=a1, in0=u, in1=cosb, op=ALU.mult), R=[cs] + RS_, W=[r1])
                  cx.op('dve', lambda e: e.tensor_tensor(out=a2, in0=u, in1=sinb, op=ALU.mult), R=[cs] + RS_, W=[r2])
                  cx.op('dve', lambda e: e.tensor_tensor(out=dst3[:, :, 0:8], in0=a1[:, :, 0, :], in1=a2[:, :, 1, :],
                                                         op=ALU.subtract), R=[r1, r2], W=RD_)
                  cx.op('dve', lambda e: e.tensor_tensor(out=dst3[:, :, 8:16], in0=a1[:, :, 1, :], in1=a2[:, :, 0, :],
                                                         op=ALU.add), R=[r1, r2], W=RD_)
                  cx.op('act', lambda e: e.copy(out=dst3[:, :, 16:64], in_=src3[:, :, 16:64]), R=RS_, W=RD_)

              def kv_a(j, p2):
                  bkv = nbank()
                  for k in range(16):
                      cx.op('pe', lambda e, k=k, bkv=bkv, wkv=wkv: e.matmul(
                          out=bank[bkv][:, 0:256], lhsT=xT.ap[:, k, j * 128:(j + 1) * 128], rhs=wkv.ap[:, k, 0:256],
                          start=(k == 0), stop=(k == 15)), R=[wkv, xT], W=[BK[bkv]])
                  cx.op('dve', lambda e, bkv=bkv: e.tensor_tensor(out=kvf[p2].ap, in0=bank[bkv][:, 0:256],
                                                                  in1=bintok.ap[:, 1024:1280], op=ALU.add),
                        R=[BK[bkv], bintok], W=[kvf[p2]])
                  rope(kvf[p2].ap[:, 0:128].rearrange("p (h d) -> p h d", d=64), krot[p2].ap, 2, j, rt1[p2 % 2], rt2[p2 % 2],
                       [kvf[p2]], [krot[p2]])
                  for eh in range(2):
                      cx.op('pool', lambda e, eh=eh: e.tensor_copy(out=kb2[p2].ap[:, :, eh, eh * 64:(eh + 1) * 64],
                                                                   in_=krot[p2].ap), R=[krot[p2]], W=[kb2[p2]])
                  cx.op('pool', lambda e: e.tensor_copy(
                      out=vaug.ap[:, j, :, 0:64], in_=kvf[p2].ap[:, 128:256].rearrange("p (h d) -> p h d", d=64)),
                      R=[kvf[p2]], W=[vaug])

              def kv_b(j, p2):
                  bt = nbank()
                  for g in range(2):
                      for eh in range(2):
                          cx.op('pe', lambda e, g=g, eh=eh, bt=bt: e.transpose(
                              out=bank_bf[bt][:, (g * 2 + eh) * 128:(g * 2 + eh + 1) * 128],
                              in_=kb2[p2].ap[:, g, eh, :], identity=ident_bf),
                              R=[kb2[p2], cst_bf], W=[BK[bt]])
                  cx.op('act', lambda e, bt=bt: e.copy(
                      out=kTd.ap[:, :, :, j * 128:(j + 1) * 128],
                      in_=bank_bf[bt][:, 0:512].rearrange("p (g a n) -> p g a n", a=2, n=128)), R=[BK[bt]], W=[kTd])

              gstate = {"gi": 0}

              def ga_step(ci, j):
                  bga = nbank()
                  for k in range(16):
                      cx.op('pe', lambda e, k=k, bga=bga, wg_=wga[ci]: e.matmul(
                          out=bank[bga], lhsT=xT.ap[:, k, j * 128:(j + 1) * 128], rhs=wg_.ap[:, k, :],
                          start=(k == 0), stop=(k == 15)), R=[wga[ci], xT], W=[BK[bga]])
                  gf = gaf[gstate["gi"] % 2]
                  gstate["gi"] += 1
                  cx.op('dve', lambda e, bga=bga, gf=gf: e.tensor_tensor(
                      out=gf.ap, in0=bank[bga], in1=bintok.ap[:, 1280 + ci * 512:1280 + (ci + 1) * 512], op=ALU.add),
                      R=[BK[bga], bintok], W=[gf])
                  cx.op('act', lambda e, gf=gf: e.activation(
                      out=gateA.ap[:, j - 1, ci * 512:(ci + 1) * 512], in_=gf.ap, func=AF.Silu),
                      R=[gf], W=[gateA])

              kv_a(0, 0)
              kv_a(1, 1)
              for j in range(NTS):
                  if j + 2 < NTS:
                      kv_a(j + 2, (j + 2) % 3)
                  kv_b(j, j % 3)
              for j in range(1, NTS):
                  ga_step(0, j)

              _DUMPS.clear()
              _DUMPS['kTd'] = kTd
              _DUMPS['vaug'] = vaug
              _chk('P3kv')
              wga[1] = wload(1280 + 512, 512)
              wqq = [wload(0, 512), None]
              for j in range(1, NTS):
                  ga_step(1, j)

              _DUMPS.clear()
              _DUMPS['gateA'] = gateA
              _chk('P3ga')
              wqq[1] = wload(512, 512)
              qT_all = [[mk(O_YT + (qi * 8 + jj) * 1024, [4, 128], BF16) for jj in range(8)] for qi in range(2)]
              def q_a(qi, j, p2):
                  bq = nbank()
                  for k in range(16):
                      cx.op('pe', lambda e, k=k, bq=bq, wq_=wqq[qi]: e.matmul(
                          out=bank[bq], lhsT=xT.ap[:, k, j * 128:(j + 1) * 128], rhs=wq_.ap[:, k, :],
                          start=(k == 0), stop=(k == 15)), R=[wqq[qi], xT], W=[BK[bq]])
                  cx.op('dve', lambda e, bq=bq: e.tensor_tensor(
                      out=qf[p2].ap, in0=bank[bq], in1=bintok.ap[:, qi * 512:(qi + 1) * 512], op=ALU.add),
                      R=[BK[bq], bintok], W=[qf[p2]])
                  rope(qf[p2].ap.rearrange("p (h d) -> p h d", d=64),
                       qb[p2].ap.rearrange("p (h d) -> p h d", d=64), 8, j, rt1[p2], rt2[p2], [qf[p2]], [qb[p2]])

              def q_b(qi, j, p2):
                  bt = nbank()
                  for t in range(4):
                      cx.op('pe', lambda e, t=t, bt=bt: e.transpose(
                          out=bank_bf[bt][:, t * 128:(t + 1) * 128], in_=qb[p2].ap[:, t * 128:(t + 1) * 128],
                          identity=ident_bf), R=[qb[p2], cst_bf], W=[BK[bt]])
                  qTb = qT_all[qi][j - 1]
                  cx.op('act', lambda e, bt=bt, qTb=qTb: e.copy(
                      out=qTb.ap, in_=bank_bf[bt][:, 0:512].rearrange("p (t n) -> p t n", n=128)),
                      R=[BK[bt]], W=[qTb])

              qits = [(qi, j) for qi in range(2) for j in range(1, NTS)]
              q_a(qits[0][0], qits[0][1], 0)
              for n_, (qi, j) in enumerate(qits):
                  if n_ + 1 < len(qits):
                      q_a(qits[n_ + 1][0], qits[n_ + 1][1], (n_ + 1) % 2)
                  q_b(qi, j, n_ % 2)

              wo_ch = [mk(O_XT, [16, 512], BF16), mk(O_XT + 16384, [16, 512], BF16), WBUF[0], WBUF[1]]
              s_woc = [s_wo, s_wo1, s_wb[0], s_wb[1]]
              for h in range(4):
                  cx.dma('pool', wo_ch[h].ap, w_out_v[:, :, h * 512:(h + 1) * 512], s_woc[h], W=[wo_ch[h]])

              def att_s1(qi, j, p2):
                  qTb = qT_all[qi][j - 1]
                  for kb in range(2):
                      ktile = j - 1 + kb
                      if kb == 1:
                          mi = 0
                      else:
                          mi = 2 if (s == 0 and j == 1) else 1
                      for hh in range(2):
                          bs = nbank()
                          cx.op('pe', lambda e, bs=bs, mi=mi: e.matmul(
                              out=bank[bs], lhsT=ident_bf, rhs=mbias.ap[:, mi, :, :].rearrange("p h q -> p (h q)"),
                              start=True, stop=False), R=[cst_bf, mbias], W=[BK[bs]])
                          for hl in range(4):
                              head = hh * 4 + hl
                              t, eh = head // 2, head % 2
                              cx.op('pe', lambda e, bs=bs, hl=hl, t=t, eh=eh, ktile=ktile: e.matmul(
                                  out=bank[bs][:, hl * 128:(hl + 1) * 128],
                                  lhsT=kTd.ap[:, qi, eh, ktile * 128:(ktile + 1) * 128],
                                  rhs=qTb.ap[:, t, :], start=False, stop=(hl == 3)),
                                  R=[kTd, qTb], W=[BK[bs]])
                          pt = PT[p2][kb * 2 + hh]
                          cx.op('act', lambda e, bs=bs, pt=pt: e.activation(out=pt.ap, in_=bank[bs], func=AF.Exp,
                                                                            scale=0.125), R=[BK[bs]], W=[pt])

              def att_s2(qi, j, p2):
                  bos = []
                  for hh in range(2):
                      bo = nbank()
                      bos.append(bo)
                      for hl in range(4):
                          for kb in range(2):
                              pt = PT[p2][kb * 2 + hh]
                              cx.op('pe', lambda e, bo=bo, hl=hl, kb=kb, pt=pt: e.matmul(
                                  out=bank[bo][:, hl * 65:(hl + 1) * 65], lhsT=pt.ap[:, hl * 128:(hl + 1) * 128],
                                  rhs=vaug.ap[:, j - 1 + kb, qi, :], start=(kb == 0), stop=(kb == 1)),
                                  R=[pt, vaug], W=[BK[bo]])
                  for hh in range(2):
                      bo = bos[hh]
                      o3 = bank[bo][:, 0:260].rearrange("p (h d) -> p h d", d=65)
                      dn = den[hh].ap[:, 0:4]
                      h0 = qi * 8 + hh * 4
                      cx.op('dve', lambda e, o3=o3, dn=dn, h0=h0: e.tensor_tensor(
                          out=dn, in0=o3[:, :, 64], in1=expsink.ap[:, h0:h0 + 4], op=ALU.add),
                          R=[BK[bo], expsink], W=[den[hh]])
                      cx.op('dve', lambda e, dn=dn: e.reciprocal(out=dn, in_=dn), R=[den[hh]], W=[den[hh]])
                      ot3 = otmp[hh].ap.rearrange("p (h d) -> p h d", d=64)
                      cx.op('dve', lambda e, o3=o3, dn=dn, ot3=ot3: e.tensor_tensor(
                          out=ot3, in0=o3[:, :, 0:64], in1=dn.unsqueeze(2).broadcast_to([128, 4, 64]), op=ALU.mult),
                          R=[BK[bo], den[hh]], W=[otmp[hh]])
                      c0g = qi * 512 + hh * 256
                      agb = ag2[p2][hh]
                      cx.op('pool', lambda e, hh=hh, c0g=c0g, agb=agb: e.tensor_tensor(
                          out=agb.ap, in0=otmp[hh].ap, in1=gateA.ap[:, j - 1, c0g:c0g + 256], op=ALU.mult),
                          R=[otmp[hh], gateA], W=[agb])

              def att_s3(qi, j, p2):
                  for hh in range(2):
                      agb = ag2[p2][hh]
                      bt2 = nbank()
                      for t in range(2):
                          cx.op('pe', lambda e, t=t, bt2=bt2, agb=agb: e.transpose(
                              out=bank_bf[bt2][:, t * 128:(t + 1) * 128], in_=agb.ap[:, t * 128:(t + 1) * 128],
                              identity=ident_bf), R=[agb, cst_bf], W=[BK[bt2]])
                      kc0 = qi * 4 + hh * 2
                      cx.op('dve', lambda e, bt2=bt2, kc0=kc0: e.tensor_copy(
                          out=mixT.ap[:, kc0:kc0 + 2, (j - 1) * 128:j * 128],
                          in_=bank_bf[bt2][:, 0:256].rearrange("p (t n) -> p t n", n=128)),
                          R=[BK[bt2]], W=[mixc[kc0], mixc[kc0 + 1]])

              its = [(qi, j) for qi in range(2) for j in range(1, NTS)]
              NI = len(its)
              for step in range(NI + 2):
                  if step < NI:
                      att_s1(its[step][0], its[step][1], step % 2)
                  if 1 <= step <= NI:
                      att_s2(its[step - 1][0], its[step - 1][1], (step - 1) % 2)
                  if step >= 2:
                      att_s3(its[step - 2][0], its[step - 2][1], (step - 2) % 2)

              _DUMPS.clear()
              _DUMPS['mixT'] = mixT
              _chk('P3')
              b4 = Bump(O_DG, ARENA)
              gpost = b4([D], F32)
              bout = b4([D], F32)
              b4b = Bump(O_YT, O_MIX)
              xt4 = [b4b([D], F32) for _ in range(2)]
              yf = [b4b([D], F32) for _ in range(2)]
              junk4 = b4b([D], BF16)
              assert b4b.p <= O_YT + 36864
              cx.dma('sp', gpost.ap, gpost_d, s_gpost, W=[gpost])
              cx.dma('sp', bout.ap, bout_d, s_bout, W=[bout])
              for j in range(8):
                  b = j % 2
                  r_in = row0 + HALO + j * 128
                  r_out = s * SUP + j * 128
                  if j < 2:
                      cx.dma('sp', xt4[b].ap, x_ext[r_in:r_in + 128, :], s_x4[b], W=[xt4[b]])
                  if j == 4 and s + 1 < NSUP:
                      nrow0 = (s + 1) * SUP
                      cx.dma('sp', gpre.ap, gpre_d, s_gpre, W=[gpre])
                      cx.dma('sp', xt[0].ap, x_ext[nrow0: nrow0 + 128, :], s_x[0], W=[xt[0]])
                      cx.dma('sp', xt[1].ap, x_ext[nrow0 + 128: nrow0 + 256, :], s_x[1], W=[xt[1]])
                      pre_issued["p1"] = True
                  for nb in range(4):
                      by = nbank()
                      for k in range(16):
                          cx.op('pe', lambda e, k=k, by=by, j=j, woc=wo_ch[nb]: e.matmul(
                              out=bank[by], lhsT=mixT.ap[:, k, j * 128:(j + 1) * 128],
                              rhs=woc.ap[:, k, :], start=(k == 0), stop=(k == 15)),
                              R=[wo_ch[nb], mixT], W=[BK[by]])
                      cx.op('dve', lambda e, by=by, nb=nb, b=b: e.tensor_tensor(
                          out=yf[b].ap[:, nb * 512:(nb + 1) * 512], in0=bank[by],
                          in1=bout.ap[:, nb * 512:(nb + 1) * 512], op=ALU.add), R=[BK[by], bout], W=[yf[b]])
                  sm = smb[b]
                  ssb = sm.ap[:, 2:3]
                  rsb = sm.ap[:, 3:4]
                  cx.op('act', lambda e, b=b, ssb=ssb: e.activation(out=junk4.ap, in_=yf[b].ap, func=AF.Square,
                                                                    accum_out=ssb), R=[yf[b]], W=[junk4, sm])
                  cx.op('act', lambda e, ssb=ssb, rsb=rsb: e.activation(out=rsb, in_=ssb, func=AF.Sqrt,
                                                                        scale=1.0 / D, bias=RMS_EPS), R=[sm], W=[sm])
                  cx.op('dve', lambda e, rsb=rsb: e.reciprocal(out=rsb, in_=rsb), R=[sm], W=[sm])
                  cx.op('dve', lambda e, b=b, rsb=rsb: e.scalar_tensor_tensor(
                      out=yf[b].ap, in0=yf[b].ap, scalar=rsb, in1=gpost.ap, op0=ALU.mult, op1=ALU.mult),
                      R=[yf[b], sm, gpost], W=[yf[b]])
                  cx.op('dve', lambda e, b=b: e.tensor_tensor(out=xt4[b].ap, in0=yf[b].ap, in1=xt4[b].ap, op=ALU.add),
                        R=[yf[b], xt4[b]], W=[xt4[b]])
                  cx.dma('sp', out_d[r_out:r_out + 128, :], xt4[b].ap, s_o[b], R=[xt4[b]])
                  if j + 2 < 8:
                      r_n = row0 + HALO + (j + 2) * 128
                      cx.dma('sp', xt4[b].ap, x_ext[r_n:r_n + 128, :], s_x4[b], W=[xt4[b]])

        except _Stop:
            pass
        if STOP is not None:
            s_dbg = cx.dsem("dbg")
            for nm, b in _DUMPS.items():
                shp = list(b.ap.shape)
                dd = nc.dram_tensor("dbg_" + nm, shp, b.ap.dtype, kind="ExternalOutput").ap()
                cx.dma('sp', dd, b.ap, s_dbg, R=[b])
            cx.finish([s_dbg])
        else:
            cx.finish(s_o)
    return nc


_PROGRAM = None


def _host_inputs(x, positions, pre_norm_g, w_in, b_in, sinks, w_dw, b_dw, conv_ln_g, conv_ln_b,
                 w_pw, b_pw, w_out, b_out, post_norm_g):
    f32 = np.float32
    x2 = np.asarray(x, f32).reshape(SEQ, D)
    pos = np.asarray(positions).reshape(SEQ).astype(np.int32)
    w_in0 = np.asarray(w_in, f32)[0]
    b_in0 = np.asarray(b_in, f32)[0]
    cols = list(range(0, 2304))
    for c in range(8):
        cols += list(range(2304 + c * 128, 2304 + (c + 1) * 128))
        cols += list(range(3328 + c * 128, 3328 + (c + 1) * 128))
    cols += list(range(4352, 5376))
    w_in_p = np.ascontiguousarray(w_in0[:, np.array(cols)])
    w_out0 = np.ascontiguousarray(np.asarray(w_out, f32)[0])
    w_pw0 = np.ascontiguousarray(np.asarray(w_pw, f32)[0])

    def bc(v, n):
        return np.ascontiguousarray(np.broadcast_to(np.asarray(v, f32).reshape(1, n), (128, n)))

    gpre = bc(np.asarray(pre_norm_g)[0], D)
    gpost = bc(np.asarray(post_norm_g)[0], D)
    boutb = bc(np.asarray(b_out)[0], D)
    bintok = bc(b_in0[0:2304], 2304)

    def fm(v):
        v = np.asarray(v, f32)
        return np.ascontiguousarray(v.reshape(-1, 128).T)

    cpack = np.zeros((128, NCP), f32)
    cpack[:, C_BFM:C_BFM + 24] = fm(b_in0[2304:5376])
    cpack[:, C_BDW:C_BDW + 8] = fm(np.asarray(b_dw)[0])
    cpack[:, C_LNG:C_LNG + 8] = fm(np.asarray(conv_ln_g)[0])
    cpack[:, C_LNB:C_LNB + 8] = fm(np.asarray(conv_ln_b)[0])
    cpack[:, C_BPW:C_BPW + 8] = fm(np.asarray(b_pw)[0])
    wdw = np.asarray(w_dw, f32)[0]
    wpad = np.concatenate([np.zeros((1, 1024), f32), wdw], axis=0)
    wpk = wpad.reshape(8, 4, 8, 4, 32)
    cpack[:, C_WDW:C_WDW + 256] = wpk.transpose(1, 4, 2, 0, 3).reshape(128, 256)
    cpack[:, C_EYE4:C_EYE4 + 32] = np.tile(np.eye(32, dtype=f32), (4, 1))
    cpack[:, C_SINK:C_SINK + 16] = bc(np.asarray(sinks)[0], 16)
    invf = (np.float32(500000.0) ** (-(np.arange(0, 16, 2, dtype=np.float32)) / np.float32(16))).astype(f32)
    cpack[:, C_INVF:C_INVF + 8] = bc(invf, 8)
    cpack[:, C_ID:C_ID + 128] = np.eye(128, dtype=f32)
    kk = np.arange(128)[:, None]
    qq = np.arange(128)[None, :]
    cpack[:, C_MCUR:C_MCUR + 128] = (qq >= kk).astype(f32)
    cpack[:, C_MPREV:C_MPREV + 128] = (kk > qq).astype(f32)

    in_maps = []
    for c in range(NCORES):
        t0 = c * TPC
        xe = np.zeros((TPC + HALO, D), f32)
        pe = np.zeros((TPC + HALO,), np.int32)
        if c > 0:
            xe[:] = x2[t0 - HALO:t0 + TPC]
            pe[:] = pos[t0 - HALO:t0 + TPC]
        else:
            xe[HALO:] = x2[0:TPC]
            pe[HALO:] = pos[0:TPC]
        cpk = cpack.copy()
        cpk[:, C_FLAG] = 0.0 if c == 0 else 1.0
        cpk[:, C_MPREVF:C_MPREVF + 128] = 0.0 if c == 0 else cpack[:, C_MPREV:C_MPREV + 128]
        in_maps.append({
            "x_ext": xe, "pos_t": np.ascontiguousarray(pe.reshape(17, 128).T),
            "w_in": w_in_p, "w_out": w_out0, "w_pw": w_pw0, "cpack": cpk,
            "gpre_bc": gpre, "gpost_bc": gpost, "bout_bc": boutb, "bintok_bc": bintok,
        })
    return in_maps


def kernel(**inputs):
    global _PROGRAM
    in_maps = _host_inputs(**inputs)
    if _PROGRAM is None:
        _PROGRAM = build_program()
    res = run_bass_kernel_spmd(_PROGRAM, in_maps, core_ids=list(range(NCORES)))
    out = np.concatenate([np.asarray(r["out"], np.float32) for r in res.results], axis=0)
    return out.reshape(1, SEQ, D)
```
